# Optimizing a Trainium2 kernel written in Bass

```python
import jax, jax.numpy as jnp
from jax import lax
import numpy as np

D_MODEL = 1024
BATCH = 4
SEQ = 8192
DEPTH = 4

GRID_W = 64
CTX_LEN = 256
N_MIXERS = 4
D_FF = -(-8 * D_MODEL // (3 * 256)) * 256
DEEPNORM_ALPHA = (2 * DEPTH) ** 0.25
DEEPNORM_BETA = (8 * DEPTH) ** -0.25
LN_EPS = 1e-5
RMS_EPS = 1e-6
ROPE_BASE = 10000.0
CHUNK = 64
NEG_INF = -1e30

RET_HEADS = 4
RET_DK = D_MODEL // RET_HEADS
RET_DV = 2 * RET_DK
NA_HEADS = 16
NA_DH = D_MODEL // NA_HEADS
NA_WIN_ROWS = 8
NA_WIN_COLS = 16
NA_QBLOCK_W = 16
NA_BAND_W = NA_QBLOCK_W + NA_WIN_COLS
MLA_HEADS = 16
MLA_NOPE = 64
MLA_ROPE = 32
MLA_V = 64
MLA_Q_LORA = 512
MLA_KV_LORA = 256
MLA_QBLOCK = 128
HG_EXPAND = 128
HG_HEADS = D_MODEL // HG_EXPAND
HG_DI = D_MODEL // HG_HEADS
HG_FDIM = HG_HEADS * HG_EXPAND

kernel_name = 'hybrid_interleaved_flow_backbone'


def layer_norm(x, g, b):
    xf = x.astype(jnp.float32)
    xc = xf - jnp.mean(xf, -1, keepdims=True)
    var = jnp.mean(xc * xc, -1, keepdims=True)
    return (xc * lax.rsqrt(var + LN_EPS) * g.astype(jnp.float32) + b.astype(jnp.float32)).astype(x.dtype)


def rms_norm(x, g=None):
    xf = x.astype(jnp.float32)
    y = xf * lax.rsqrt(jnp.mean(xf * xf, -1, keepdims=True) + RMS_EPS)
    if g is not None:
        y = y * g.astype(jnp.float32)
    return y.astype(x.dtype)


def modulate(h, shift, scale):
    return h * (1 + scale) + shift


def axial_rope(n_tokens, rot_dim):
    t = jnp.arange(n_tokens)
    rows = (t // GRID_W).astype(jnp.float32)
    cols = (t % GRID_W).astype(jnp.float32)
    n_freq = rot_dim // 4
    inv = ROPE_BASE ** (-jnp.arange(n_freq, dtype=jnp.float32) / n_freq)
    ang = jnp.concatenate([rows[:, None] * inv, cols[:, None] * inv], -1)
    return jnp.cos(ang)[:, None, :], jnp.sin(ang)[:, None, :]


def apply_rope(x, cos, sin):
    x1, x2 = jnp.split(x.astype(jnp.float32), 2, axis=-1)
    return jnp.concatenate([x1 * cos - x2 * sin, x1 * sin + x2 * cos], -1).astype(x.dtype)


def chunk_gla(q, k, v, log_f, s0):
    B, H, L, dk = q.shape
    dv = v.shape[-1]
    n = L // CHUNK

    def to_chunks(a):
        return a.astype(jnp.float32).reshape(B, H, n, CHUNK, a.shape[-1]).transpose(2, 0, 1, 3, 4)

    mask = jnp.tril(jnp.ones((CHUNK, CHUNK), bool))

    def step(S, inp):
        qi, ki, vi, gi = inp
        b = jnp.cumsum(gi, axis=-2)
        b_last = b[..., -1:, :]
        q_d = qi * jnp.exp(b)
        att = jnp.where(mask, jnp.einsum('bhtk,bhsk->bhts', q_d, ki * jnp.exp(-b)), 0.0)
        o = jnp.einsum('bhts,bhsv->bhtv', att, vi) + jnp.einsum('bhtk,bhkv->bhtv', q_d, S)
        S_new = jnp.exp(b_last[..., 0, :])[..., None] * S + jnp.einsum('bhsk,bhsv->bhkv', ki * jnp.exp(b_last - b), vi)
        return S_new, o

    S_fin, oc = lax.scan(step, s0, (to_chunks(q), to_chunks(k), to_chunks(v), to_chunks(log_f)))
    return oc.transpose(1, 2, 0, 3, 4).reshape(B, H, L, dv), S_fin


def bidirectional_scan(ctx_terms, lat_terms):
    qc, kcf, kcb, vc, gcf, gcb = ctx_terms
    ql, klf, klb, vl, glf, glb = lat_terms
    B, H, _, dk = qc.shape
    s0 = jnp.zeros((B, H, dk, vc.shape[-1]), jnp.float32)
    flip = lambda t: jnp.flip(t, axis=2)
    o_cf, s_cf = chunk_gla(qc, kcf, vc, gcf, s0)
    o_cb, s_cb = chunk_gla(flip(qc), flip(kcb), flip(vc), flip(gcb), s0)
    o_lf, _ = chunk_gla(ql, klf, vl, glf, s_cf)
    o_lb, _ = chunk_gla(flip(ql), flip(klb), flip(vl), flip(glb), s_cb)
    return o_cf + flip(o_cb), o_lf + flip(o_lb)


def retention_mixer(a_ctx, a_lat, w_in, decay_param, w_out, want_ctx):
    B, L, _ = a_lat.shape
    hk, hv = RET_HEADS * RET_DK, RET_HEADS * RET_DV
    log_gamma = -jnp.exp(decay_param.astype(jnp.float32))

    def project(a, rope):
        n = a.shape[1]
        q, k, v, g = jnp.split(a @ w_in, [hk, 2 * hk, 2 * hk + hv], axis=-1)
        q = q.reshape(B, n, RET_HEADS, RET_DK)
        k = k.reshape(B, n, RET_HEADS, RET_DK)
        if rope is not None:
            q, k = apply_rope(q, *rope), apply_rope(k, *rope)
        k = k * RET_DK ** -0.5
        v = v.reshape(B, n, RET_HEADS, RET_DV)
        decay = lambda lg: jnp.broadcast_to(lg[None, :, None, None], (B, RET_HEADS, n, RET_DK))
        qh, kh, vh = (t.transpose(0, 2, 1, 3) for t in (q, k, v))
        return (qh, kh, kh, vh, decay(log_gamma[0]), decay(log_gamma[1])), g

    ctx_terms, g_ctx = project(a_ctx, None)
    lat_terms, g_lat = project(a_lat, axial_rope(L, RET_DK))
    o_ctx, o_lat = bidirectional_scan(ctx_terms, lat_terms)

    def readout(o, g):
        n = o.shape[2]
        y = rms_norm(o).transpose(0, 2, 1, 3).reshape(B, n, hv).astype(g.dtype)
        return (jax.nn.silu(g) * y) @ w_out

    return (readout(o_ctx, g_ctx) if want_ctx else None), readout(o_lat, g_lat)


def neighbourhood_mixer(a_ctx, a_lat, w_qkv, rpb, w_out, want_ctx):
    B, L, _ = a_lat.shape
    Lc = a_ctx.shape[1]
    rows = L // GRID_W
    wr = min(NA_WIN_ROWS, rows)
    scale = NA_DH ** -0.5
    q, k, v = jnp.split(a_lat @ w_qkv, 3, axis=-1)
    grid = lambda t: t.reshape(B, rows, GRID_W, NA_HEADS, NA_DH)
    q, k, v = grid(q * scale), grid(k), grid(v)
    qc, kc, vc = (t.reshape(B, Lc, NA_HEADS, NA_DH) for t in jnp.split(a_ctx @ w_qkv, 3, axis=-1))

    n_cb = GRID_W // NA_QBLOCK_W
    band0 = np.clip(np.arange(n_cb) * NA_QBLOCK_W - NA_WIN_COLS // 2, 0, GRID_W - NA_BAND_W)
    kcol = band0[:, None] + np.arange(NA_BAND_W)
    qcol = np.arange(GRID_W).reshape(n_cb, NA_QBLOCK_W)
    wstart = np.clip(qcol - NA_WIN_COLS // 2, 0, GRID_W - NA_WIN_COLS)
    kc3 = kcol[:, None, :]
    col_ok = (kc3 >= wstart[..., None]) & (kc3 < wstart[..., None] + NA_WIN_COLS)
    c_idx = np.clip(kc3 - qcol[..., None] + NA_WIN_COLS - 1, 0, 2 * NA_WIN_COLS - 2)

    def row(r):
        rs = jnp.clip(r - wr // 2, 0, rows - wr)
        qr = lax.dynamic_index_in_dim(q, r, axis=1, keepdims=False).reshape(B, n_cb, NA_QBLOCK_W, NA_HEADS, NA_DH)
        kb = lax.dynamic_slice_in_dim(k, rs, wr, axis=1)[:, :, kcol]
        vb = lax.dynamic_slice_in_dim(v, rs, wr, axis=1)[:, :, kcol]
        r_idx = rs + jnp.arange(wr) - r + NA_WIN_ROWS - 1
        bias = rpb[:, r_idx[None, None, :, None], c_idx[:, :, None, :]]
        s_lat = jnp.einsum('bnqhd,brnkhd->bhnqrk', qr, kb).astype(jnp.float32) + bias[None].astype(jnp.float32)
        s_lat = jnp.where(col_ok[:, :, None, :], s_lat, NEG_INF).reshape(B, NA_HEADS, n_cb, NA_QBLOCK_W, wr * NA_BAND_W)
        s_ctx = jnp.einsum('bnqhd,bchd->bhnqc', qr, kc).astype(jnp.float32)
        p = jax.nn.softmax(jnp.concatenate([s_lat, s_ctx], -1), -1).astype(v.dtype)
        p_lat = p[..., :wr * NA_BAND_W].reshape(B, NA_HEADS, n_cb, NA_QBLOCK_W, wr, NA_BAND_W)
        o = jnp.einsum('bhnqrk,brnkhd->bnqhd', p_lat, vb) + jnp.einsum('bhnqc,bchd->bnqhd', p[..., wr * NA_BAND_W:], vc)
        return o.reshape(B, GRID_W, NA_HEADS * NA_DH)

    o_lat = lax.map(row, jnp.arange(rows)).transpose(1, 0, 2, 3).reshape(B, L, NA_HEADS * NA_DH)
    y_ctx = None
    if want_ctx:
        s = jnp.einsum('bqhd,bkhd->bhqk', qc * scale, kc).astype(jnp.float32)
        p = jax.nn.softmax(s, -1).astype(vc.dtype)
        y_ctx = jnp.einsum('bhqk,bkhd->bqhd', p, vc).reshape(B, Lc, NA_HEADS * NA_DH) @ w_out
    return y_ctx, o_lat @ w_out


def mla_mixer(a_ctx, a_lat, w_down, q_norm, kv_norm, w_uq, w_ukv, w_out, want_ctx):
    B, L, _ = a_lat.shape
    scale = (MLA_NOPE + MLA_ROPE) ** -0.5

    def project(a, rope):
        n = a.shape[1]
        cq, ckv, kr = jnp.split(a @ w_down, [MLA_Q_LORA, MLA_Q_LORA + MLA_KV_LORA], axis=-1)
        q = (rms_norm(cq, q_norm) @ w_uq).reshape(B, n, MLA_HEADS, MLA_NOPE + MLA_ROPE)
        kv = (rms_norm(ckv, kv_norm) @ w_ukv).reshape(B, n, MLA_HEADS, MLA_NOPE + MLA_V)
        q_nope, q_rope = jnp.split(q, [MLA_NOPE], axis=-1)
        k_nope, v = jnp.split(kv, [MLA_NOPE], axis=-1)
        kr = kr[:, :, None, :]
        if rope is not None:
            q_rope, kr = apply_rope(q_rope, *rope), apply_rope(kr, *rope)
        q = jnp.concatenate([q_nope, q_rope], -1) * scale
        k = jnp.concatenate([k_nope, jnp.broadcast_to(kr, (B, n, MLA_HEADS, MLA_ROPE))], -1)
        return q, k, v

    qc, kc, vc = project(a_ctx, None)
    ql, kl, vl = project(a_lat, axial_rope(L, MLA_ROPE))
    k_all = jnp.concatenate([kc, kl], axis=1)
    v_all = jnp.concatenate([vc, vl], axis=1)

    def attend(qb, keys, vals):
        s = jnp.einsum('bqhd,bkhd->bhqk', qb, keys).astype(jnp.float32)
        p = jax.nn.softmax(s, -1).astype(vals.dtype)
        return jnp.einsum('bhqk,bkhd->bqhd', p, vals)

    nb = L // MLA_QBLOCK
    qblocks = ql.reshape(B, nb, MLA_QBLOCK, MLA_HEADS, MLA_NOPE + MLA_ROPE).transpose(1, 0, 2, 3, 4)
    o = lax.map(lambda qb: attend(qb, k_all, v_all), qblocks)
    o_lat = o.transpose(1, 0, 2, 3, 4).reshape(B, L, MLA_HEADS * MLA_V)
    y_ctx = attend(qc, kc, vc).reshape(B, -1, MLA_HEADS * MLA_V) @ w_out if want_ctx else None
    return y_ctx, o_lat @ w_out


def hgrn2_mixer(a_ctx, a_lat, w_in, lower_bounds, norm_g, w_out, layer_idx, want_ctx):
    B = a_lat.shape[0]
    lb_soft = jax.nn.softmax(lower_bounds.astype(jnp.float32), axis=0)
    lb = (jnp.cumsum(lb_soft, axis=0) - lb_soft[0])[layer_idx]

    def project(a):
        n = a.shape[1]
        q, f_f, f_b, i, g = jnp.split(a @ w_in, [HG_FDIM, 2 * HG_FDIM, 3 * HG_FDIM, 3 * HG_FDIM + HG_HEADS * HG_DI], axis=-1)
        heads = lambda t, d: t.reshape(B, n, HG_HEADS, d).transpose(0, 2, 1, 3)
        q = heads(jax.nn.silu(q), HG_EXPAND) * HG_EXPAND ** -0.5

        def gate(f):
            forget = lb + (1 - lb) * jax.nn.sigmoid(f.astype(jnp.float32))
            return heads(1 - forget, HG_EXPAND), heads(jnp.log(forget), HG_EXPAND)

        kf, gf = gate(f_f)
        kb, gb = gate(f_b)
        return (q, kf, kb, heads(i, HG_DI), gf, gb), g

    ctx_terms, g_ctx = project(a_ctx)
    lat_terms, g_lat = project(a_lat)
    o_ctx, o_lat = bidirectional_scan(ctx_terms, lat_terms)

    def readout(o, g):
        n = o.shape[2]
        y = rms_norm(o, norm_g).transpose(0, 2, 1, 3).reshape(B, n, HG_HEADS * HG_DI).astype(g.dtype)
        return (y * jax.nn.silu(g)) @ w_out

    return (readout(o_ctx, g_ctx) if want_ctx else None), readout(o_lat, g_lat)


def swiglu(a, w13, w2):
    gate, up = jnp.split(a @ w13, 2, axis=-1)
    return (jax.nn.silu(gate) * up) @ w2


def setup_inputs(seed: int = 0) -> dict:
    key = jax.random.key(seed)
    ks = iter(jax.random.split(key, 32))
    nrm = lambda shape, std: std * jax.random.normal(next(ks), shape, jnp.float32)
    D, F, beta = D_MODEL, D_FF, DEEPNORM_BETA
    ret_decay_base = jnp.log(-jnp.log1p(-(2.0 ** (-5.0 - jnp.arange(RET_HEADS, dtype=jnp.float32)))))
    return {
        'x': nrm((BATCH, SEQ, D), 1.0),
        'c': nrm((BATCH, D), 1.0),
        'ctx': nrm((BATCH, CTX_LEN, D), 1.0),
        'c_ctx': nrm((D,), 1.0),
        'ada_w': nrm((DEPTH, D, 6 * D), D ** -0.5),
        'ada_b': nrm((DEPTH, 6 * D), 0.01),
        'ln_g': 1.0 + nrm((DEPTH, 2, D), 0.01),
        'ln_b': nrm((DEPTH, 2, D), 0.01),
        'ffn_w13': nrm((DEPTH, D, 2 * F), D ** -0.5),
        'ffn_w2': nrm((DEPTH, F, D), beta * F ** -0.5),
        'ret_w_in': nrm((D, 2 * RET_HEADS * RET_DK + 2 * RET_HEADS * RET_DV), D ** -0.5),
        'ret_decay': ret_decay_base[None, :] + nrm((2, RET_HEADS), 0.01),
        'ret_w_out': nrm((RET_HEADS * RET_DV, D), beta * (RET_HEADS * RET_DV) ** -0.5),
        'na_w_qkv': nrm((D, 3 * NA_HEADS * NA_DH), D ** -0.5),
        'na_rpb': nrm((NA_HEADS, 2 * NA_WIN_ROWS - 1, 2 * NA_WIN_COLS - 1), 0.02),
        'na_w_out': nrm((NA_HEADS * NA_DH, D), beta * (NA_HEADS * NA_DH) ** -0.5),
        'mla_w_down': nrm((D, MLA_Q_LORA + MLA_KV_LORA + MLA_ROPE), D ** -0.5),
        'mla_q_norm': 1.0 + nrm((MLA_Q_LORA,), 0.01),
        'mla_kv_norm': 1.0 + nrm((MLA_KV_LORA,), 0.01),
        'mla_w_uq': nrm((MLA_Q_LORA, MLA_HEADS * (MLA_NOPE + MLA_ROPE)), MLA_Q_LORA ** -0.5),
        'mla_w_ukv': nrm((MLA_KV_LORA, MLA_HEADS * (MLA_NOPE + MLA_V)), MLA_KV_LORA ** -0.5),
        'mla_w_out': nrm((MLA_HEADS * MLA_V, D), beta * (MLA_HEADS * MLA_V) ** -0.5),
        'hg_w_in': nrm((D, 3 * HG_FDIM + 2 * HG_HEADS * HG_DI), D ** -0.5),
        'hg_lower_bounds': nrm((DEPTH, HG_FDIM), 0.1),
        'hg_norm_g': 1.0 + nrm((HG_DI,), 0.01),
        'hg_w_out': nrm((HG_HEADS * HG_DI, D), beta * (HG_HEADS * HG_DI) ** -0.5),
    }


def reference(x, c, ctx, c_ctx, ada_w, ada_b, ln_g, ln_b, ffn_w13, ffn_w2,
              ret_w_in, ret_decay, ret_w_out,
              na_w_qkv, na_rpb, na_w_out,
              mla_w_down, mla_q_norm, mla_kv_norm, mla_w_uq, mla_w_ukv, mla_w_out,
              hg_w_in, hg_lower_bounds, hg_norm_g, hg_w_out):
    h_lat, h_ctx = x, ctx
    cond_lat = jax.nn.silu(c)[:, None, :]
    cond_ctx = jax.nn.silu(c_ctx)[None, None, :]
    for i in range(DEPTH):
        want_ctx = i < DEPTH - 1
        m_lat = jnp.split(cond_lat @ ada_w[i] + ada_b[i], 6, axis=-1)
        m_ctx = jnp.split(cond_ctx @ ada_w[i] + ada_b[i], 6, axis=-1)
        a_lat = modulate(h_lat, m_lat[0], m_lat[1])
        a_ctx = modulate(h_ctx, m_ctx[0], m_ctx[1])
        kind = i % N_MIXERS
        if kind == 0:
            y_ctx, y_lat = retention_mixer(a_ctx, a_lat, ret_w_in, ret_decay, ret_w_out, want_ctx)
        elif kind == 1:
            y_ctx, y_lat = neighbourhood_mixer(a_ctx, a_lat, na_w_qkv, na_rpb, na_w_out, want_ctx)
        elif kind == 2:
            y_ctx, y_lat = mla_mixer(a_ctx, a_lat, mla_w_down, mla_q_norm, mla_kv_norm, mla_w_uq, mla_w_ukv, mla_w_out, want_ctx)
        else:
            y_ctx, y_lat = hgrn2_mixer(a_ctx, a_lat, hg_w_in, hg_lower_bounds, hg_norm_g, hg_w_out, i, want_ctx)
        h_lat = layer_norm(DEEPNORM_ALPHA * h_lat + m_lat[2] * y_lat, ln_g[i, 0], ln_b[i, 0])
        f_lat = swiglu(modulate(h_lat, m_lat[3], m_lat[4]), ffn_w13[i], ffn_w2[i])
        h_lat = layer_norm(DEEPNORM_ALPHA * h_lat + m_lat[5] * f_lat, ln_g[i, 1], ln_b[i, 1])
        if want_ctx:
            h_ctx = layer_norm(DEEPNORM_ALPHA * h_ctx + m_ctx[2] * y_ctx, ln_g[i, 0], ln_b[i, 0])
            f_ctx = swiglu(modulate(h_ctx, m_ctx[3], m_ctx[4]), ffn_w13[i], ffn_w2[i])
            h_ctx = layer_norm(DEEPNORM_ALPHA * h_ctx + m_ctx[5] * f_ctx, ln_g[i, 1], ln_b[i, 1])
    return h_lat
```

```python
import numpy as np
from contextlib import ExitStack
import concourse.bass as bass
import concourse.mybir as mybir
from concourse.bass_utils import run_bass_kernel_spmd

F32 = mybir.dt.float32
BF16 = mybir.dt.bfloat16
AF = mybir.ActivationFunctionType
ALU = mybir.AluOpType
AX = mybir.AxisListType

NCORES = 8


class Buf:
    __slots__ = ("name", "t", "last_w", "readers")

    def __init__(self, name, t):
        self.name = name
        self.t = t
        self.last_w = None
        self.readers = []

    def __getitem__(self, idx):
        return self.t[idx]


class Op:
    __slots__ = ("eng", "fn", "deps", "signal", "token", "is_dma", "prewait")

    def __init__(self, eng, fn, is_dma):
        self.eng = eng
        self.fn = fn
        self.deps = []
        self.signal = is_dma
        self.token = None
        self.is_dma = is_dma
        self.prewait = None


ENGS = ("pe", "act", "dve", "pool", "sp")
NPOOL = {"sp": 40, "pool": 24, "act": 8}


class Prog:
    def __init__(self):
        self.nc = bass.Bass("TRN2", target_bir_lowering=False)
        self.es = ExitStack()
        self.ops = []
        self.n = 0

    def din(self, name, shape, dt):
        return self.nc.dram_tensor(name, list(shape), dt, kind="ExternalInput").ap()

    def dout(self, name, shape, dt):
        return self.nc.dram_tensor(name, list(shape), dt, kind="ExternalOutput").ap()

    def dscratch(self, name, shape, dt):
        return Buf(name, self.nc.dram_tensor(name, list(shape), dt, kind="Internal").ap())

    def sb(self, name, shape, dt):
        self.n += 1
        t = self.es.enter_context(self.nc.sbuf_tensor(f"{name}_{self.n}", list(shape), dt))
        return Buf(name, t)

    def ps(self, name, shape, dt=F32):
        self.n += 1
        t = self.es.enter_context(self.nc.psum_tensor(f"{name}_{self.n}", list(shape), dt))
        return Buf(name, t)

    def op(self, eng, fn, reads=(), writes=(), is_dma=False):
        o = Op(eng, fn, is_dma)
        deps = []
        for b in reads:
            if b.last_w is not None:
                deps.append(b.last_w)
        for b in writes:
            w = b.last_w
            if w is not None and (is_dma or w.is_dma or w.eng != eng):
                deps.append(w)
            for r in b.readers:
                if is_dma or r.is_dma or r.eng != eng:
                    deps.append(r)
        for b in writes:
            b.last_w = o
            b.readers = []
        for b in reads:
            b.readers.append(o)
        seen = set()
        for d in deps:
            if id(d) not in seen and d is not o:
                seen.add(id(d))
                o.deps.append(d)
                d.signal = True
        self.ops.append(o)
        return o

    def dma(self, eng, out, in_, reads=(), writes=()):
        return self.op(eng, lambda e: e.dma_start(out=out, in_=in_), reads, writes, is_dma=True)

    def mm(self, out, lhsT, rhs, start, stop, reads, writes):
        return self.op("pe", lambda e: e.matmul(out, lhsT, rhs, start=start, stop=stop), reads, writes)

    def transpose(self, out, in_, ident, reads, writes):
        return self.op("pe", lambda e: e.transpose(out, in_, ident), reads, writes)

    def act(self, out, in_, func, reads, writes, bias=None, scale=None, accum_out=None, eng="act"):
        kw = {}
        if bias is not None:
            kw["bias"] = bias
        if scale is not None:
            kw["scale"] = scale
        if accum_out is not None:
            kw["accum_out"] = accum_out
        return self.op(eng, lambda e: e.activation(out, in_, func, **kw), reads, writes)

    def tt(self, out, in0, in1, op, reads, writes, eng="dve"):
        return self.op(eng, lambda e: e.tensor_tensor(out, in0, in1, op), reads, writes)

    def ts(self, out, in0, s1, s2, op0, op1, reads, writes, eng="dve", accum_out=None):
        if op1 is None:
            return self.op(eng, lambda e: e.tensor_scalar(out, in0, s1, None, op0), reads, writes)
        if accum_out is not None:
            return self.op(eng, lambda e: e.tensor_scalar(out, in0, s1, s2, op0, op1, accum_out=accum_out), reads, writes)
        return self.op(eng, lambda e: e.tensor_scalar(out, in0, s1, s2, op0, op1), reads, writes)

    def stt(self, out, in0, scalar, in1, op0, op1, reads, writes, eng="dve"):
        return self.op(eng, lambda e: e.scalar_tensor_tensor(out, in0, scalar, in1, op0, op1), reads, writes)

    def copy(self, out, in_, reads, writes, eng="dve"):
        return self.op(eng, lambda e: e.tensor_copy(out, in_), reads, writes)

    def memset(self, out, val, writes, eng="dve"):
        return self.op(eng, lambda e: e.memset(out, val), (), writes)

    def finish(self):
        nc = self.nc
        es = self.es
        esem = {e: es.enter_context(nc.semaphore(f"s_{e}")) for e in ENGS if e != "sp"}
        dsem = {q: [es.enter_context(nc.semaphore(f"d_{q}{i}")) for i in range(NPOOL[q])] for q in NPOOL}
        cnt = {e: 0 for e in ENGS}
        dcnt = {q: 0 for q in NPOOL}
        for o in self.ops:
            if o.is_dma:
                q = o.eng
                n = dcnt[q]
                dcnt[q] += 1
                slot = n % NPOOL[q]
                use = n // NPOOL[q]
                o.token = (dsem[q][slot], 16 * (use + 1))
                if use > 0:
                    o.prewait = (dsem[q][slot], 16 * use)
            elif o.signal:
                cnt[o.eng] += 1
                o.token = (esem[o.eng], cnt[o.eng])
        per = {e: [o for o in self.ops if o.eng == e] for e in ENGS}
        final_waits = []
        for q in NPOOL:
            for slot in range(min(dcnt[q], NPOOL[q])):
                uses = (dcnt[q] - 1 - slot) // NPOOL[q] + 1
                final_waits.append((dsem[q][slot], 16 * uses))

        def emit(eng_name, e):
            known = {}

            def wait(tok):
                s, v = tok
                if known.get(id(s), 0) >= v:
                    return
                known[id(s)] = v
                e.wait_ge(s, v)

            for o in per[eng_name]:
                if o.prewait is not None:
                    wait(o.prewait)
                for d in o.deps:
                    wait(d.token)
                inst = o.fn(e)
                if o.is_dma:
                    inst.then_inc(o.token[0], 16)
                elif o.signal:
                    inst.then_inc(o.token[0], 1)
            if eng_name == "sp":
                for tok in final_waits:
                    wait(tok)

        with nc.Block() as block:
            @block.sync
            def _(e):
                emit("sp", e)

            @block.tensor
            def _(e):
                emit("pe", e)

            @block.scalar
            def _(e):
                emit("act", e)

            @block.vector
            def _(e):
                emit("dve", e)

            @block.gpsimd
            def _(e):
                emit("pool", e)
        es.close()
        return nc


def run(prog, in_maps):
    nc = prog.finish()
    res = run_bass_kernel_spmd(nc, in_maps, core_ids=list(range(NCORES)))
    return res.results


def tok_groups(T):
    gs = []
    t = 0
    while t < T:
        g = min(512, T - t)
        gs.append((t, g))
        t += g
    return gs


class Lin:
    def __init__(self, T, K, N, blocks=None):
        self.P = P = Prog()
        self.T, self.K, self.N = T, K, N
        self.KC = K // 128
        assert K % 128 == 0 and T % 128 == 0
        self.XT = P.din("XT", [K, T], BF16)
        self.W = P.din("W", [K, N], F32)
        self.blocks = blocks or [(n0, min(512, N - n0)) for n0 in range(0, N, 512)]
        self.pre_group = None
        self.blk_ep = None
        self.row_ep = None

    def load_weights(self):
        P = self.P
        Wv = self.W.rearrange("(k p) n -> p k n", p=128)
        self.wt = []
        for k in range(self.KC):
            w = P.sb("w", [128, self.N], BF16)
            P.dma("pool", w[:], Wv[:, k, :], (), (w,))
            self.wt.append(w)

    def run(self):
        P = self.P
        KC = self.KC
        XTv = self.XT.rearrange("(k p) t -> p k t", p=128)
        xb = [P.sb("xg", [128, KC, 512], BF16) for _ in range(2)]
        psb = [P.ps("ps", [128, 512]) for _ in range(4)]
        groups = tok_groups(self.T)

        def load(g):
            t0, gs = groups[g]
            P.dma("sp", xb[g % 2][:, :, 0:gs], XTv[:, :, t0:t0 + gs], (), (xb[g % 2],))
            if self.pre_group:
                self.pre_group(g, t0, gs)

        load(0)
        ctr = 0
        for g, (t0, gs) in enumerate(groups):
            if g + 1 < len(groups):
                load(g + 1)
            xg = xb[g % 2]
            for j in range(gs // 128):
                tt = t0 // 128 + j
                for nb, (n0, ns) in enumerate(self.blocks):
                    ps = psb[ctr % 4]
                    ctr += 1
                    for k in range(KC):
                        P.mm(ps[:, 0:ns], xg[:, k, j * 128:(j + 1) * 128], self.wt[k][:, n0:n0 + ns],
                             k == 0, k == KC - 1, (xg, self.wt[k]), (ps,))
                    self.blk_ep(tt, j, nb, n0, ns, ps)
                if self.row_ep:
                    self.row_ep(tt, j, t0 + j * 128)
        return P


def evac(P, i, out, in_, reads, writes, func=None, scale=None):
    if func is not None:
        return P.act(out, in_, func, reads, writes, scale=scale)
    if i % 2 == 0:
        return P.act(out, in_, AF.Copy, reads, writes)
    return P.copy(out, in_, reads, writes)


def build_plain(T, K, N, silu_blocks=(), out_dt=BF16):
    L = Lin(T, K, N)
    P = L.P
    Y = P.dout("Y", [T, N], out_dt)
    L.load_weights()
    ob = [P.sb("ob", [128, N], out_dt) for _ in range(2)]

    def blk_ep(tt, j, nb, n0, ns, ps):
        o = ob[tt % 2]
        evac(P, nb, o[:, n0:n0 + ns], ps[:, 0:ns], (ps,), (o,), func=AF.Silu if nb in silu_blocks else None)

    def row_ep(tt, j, t0):
        o = ob[tt % 2]
        P.dma("pool", Y[t0:t0 + 128, :], o[:], (o,), ())

    L.blk_ep, L.row_ep = blk_ep, row_ep
    return L.run()


def bcast_mid(ap, n):
    return ap.unsqueeze(1).broadcast_to([ap.shape[0], n, ap.shape[1]])


def build_ret_a(T):
    L = Lin(T, 1024, 6144)
    P = L.P
    Y = P.dout("Y", [T, 6144], BF16)
    COS = P.din("COS", [T, 128], F32)
    SIN = P.din("SIN", [T, 128], F32)
    L.load_weights()
    ob = [P.sb("ob", [128, 6144], BF16) for _ in range(2)]
    rb = [P.sb("rb", [128, 2048], F32) for _ in range(2)]
    cs = [P.sb("cs", [128, 4, 128], F32) for _ in range(2)]
    sn = [P.sb("sn", [128, 4, 128], F32) for _ in range(2)]
    t1 = P.sb("t1", [128, 8, 128], F32)
    t2 = P.sb("t2", [128, 8, 128], F32)
    t3 = P.sb("t3", [128, 8, 128], F32)
    t4 = P.sb("t4", [128, 8, 128], F32)

    def pre_group(g, t0, gs):
        nj = gs // 128
        P.dma("sp", cs[g % 2][:, 0:nj, :], COS[t0:t0 + gs, :].rearrange("(j p) f -> p j f", p=128), (), (cs[g % 2],))
        P.dma("sp", sn[g % 2][:, 0:nj, :], SIN[t0:t0 + gs, :].rearrange("(j p) f -> p j f", p=128), (), (sn[g % 2],))

    def blk_ep(tt, j, nb, n0, ns, ps):
        if nb < 4:
            r = rb[tt % 2]
            evac(P, nb, r[:, n0:n0 + ns], ps[:, 0:ns], (ps,), (r,))
        else:
            o = ob[tt % 2]
            evac(P, nb, o[:, n0:n0 + ns], ps[:, 0:ns], (ps,), (o,), func=AF.Silu if nb >= 8 else None)

    def row_ep(tt, j, t0):
        g = (t0 // 512)
        r = rb[tt % 2]
        o = ob[tt % 2]
        rv = r[:, :].rearrange("p (h two d) -> p h two d", h=8, two=2)
        ov = o[:, 0:2048].rearrange("p (h two d) -> p h two d", h=8, two=2)
        c = bcast_mid(cs[g % 2][:, j, :], 8)
        s = bcast_mid(sn[g % 2][:, j, :], 8)
        x1, x2 = rv[:, :, 0, :], rv[:, :, 1, :]
        P.tt(t1[:], x1, c, ALU.mult, (r, cs[g % 2]), (t1,))
        P.tt(t2[:], x2, s, ALU.mult, (r, sn[g % 2]), (t2,))
        P.tt(ov[:, :, 0, :], t1[:], t2[:], ALU.subtract, (t1, t2), (o,))
        P.tt(t3[:], x1, s, ALU.mult, (r, sn[g % 2]), (t3,), eng="pool")
        P.tt(t4[:], x2, c, ALU.mult, (r, cs[g % 2]), (t4,), eng="pool")
        P.tt(ov[:, :, 1, :], t3[:], t4[:], ALU.add, (t3, t4), (o,), eng="pool")
        P.dma("pool", Y[t0:t0 + 128, :], o[:], (o,), ())

    L.pre_group, L.blk_ep, L.row_ep = pre_group, blk_ep, row_ep
    return L.run()


RMS_EPS = 1e-6
LN_EPS = 1e-5
ALPHA = 8 ** 0.25


def build_mla_a1(T):
    L = Lin(T, 1024, 800)
    P = L.P
    Y = P.dout("Y", [T, 800], BF16)
    COS = P.din("COS", [T, 16], F32)
    SIN = P.din("SIN", [T, 16], F32)
    GN = P.din("GN", [128, 768], F32)
    L.load_weights()
    gn = P.sb("gn", [128, 768], F32)
    P.dma("sp", gn[:], GN, (), (gn,))
    ob = [P.sb("ob", [128, 800], BF16) for _ in range(2)]
    rb = [P.sb("rb", [128, 800], F32) for _ in range(2)]
    cs = [P.sb("cs", [128, 4, 16], F32) for _ in range(2)]
    sn = [P.sb("sn", [128, 4, 16], F32) for _ in range(2)]
    junk = P.sb("junk", [128, 512], F32)
    epsb = P.sb("epsb", [128, 1], F32)
    P.memset(epsb[:], RMS_EPS, (epsb,))
    st = [P.sb("st", [128, 4], F32) for _ in range(2)]
    tm = [P.sb("tm", [128, 4, 16], F32) for _ in range(2)]

    def pre_group(g, t0, gs):
        nj = gs // 128
        P.dma("sp", cs[g % 2][:, 0:nj, :], COS[t0:t0 + gs, :].rearrange("(j p) f -> p j f", p=128), (), (cs[g % 2],))
        P.dma("sp", sn[g % 2][:, 0:nj, :], SIN[t0:t0 + gs, :].rearrange("(j p) f -> p j f", p=128), (), (sn[g % 2],))

    def blk_ep(tt, j, nb, n0, ns, ps):
        r = rb[tt % 2]
        evac(P, nb + 1, r[:, n0:n0 + ns], ps[:, 0:ns], (ps,), (r,))

    def row_ep(tt, j, t0):
        g = t0 // 512
        r, o, s, t = rb[tt % 2], ob[tt % 2], st[tt % 2], tm[tt % 2]
        for idx, (c0, cn) in enumerate(((0, 512), (512, 256))):
            P.act(junk[:, 0:cn], r[:, c0:c0 + cn], AF.Square, (r,), (junk, s), accum_out=s[:, idx:idx + 1])
            P.act(s[:, 2 + idx:3 + idx], s[:, idx:idx + 1], AF.Sqrt, (s, epsb), (s,), bias=epsb[:, 0:1], scale=1.0 / cn)
            P.op("dve", lambda e, idx=idx: e.reciprocal(s[:, 2 + idx:3 + idx], s[:, 2 + idx:3 + idx]), (s,), (s,))
            P.stt(o[:, c0:c0 + cn], r[:, c0:c0 + cn], s[:, 2 + idx:3 + idx], gn[:, c0:c0 + cn], ALU.mult, ALU.mult,
                  (r, s, gn), (o,))
        c, sn_ = cs[g % 2][:, j, :], sn[g % 2][:, j, :]
        x1, x2 = r[:, 768:784], r[:, 784:800]
        P.tt(t[:, 0, :], x1, c, ALU.mult, (r, cs[g % 2]), (t,))
        P.tt(t[:, 1, :], x2, sn_, ALU.mult, (r, sn[g % 2]), (t,))
        P.tt(o[:, 768:784], t[:, 0, :], t[:, 1, :], ALU.subtract, (t,), (o,))
        P.tt(t[:, 2, :], x1, sn_, ALU.mult, (r, sn[g % 2]), (t,))
        P.tt(t[:, 3, :], x2, c, ALU.mult, (r, cs[g % 2]), (t,))
        P.tt(o[:, 784:800], t[:, 2, :], t[:, 3, :], ALU.add, (t,), (o,))
        P.dma("pool", Y[t0:t0 + 128, :], o[:], (o,), ())

    L.pre_group, L.blk_ep, L.row_ep = pre_group, blk_ep, row_ep
    return L.run()


def build_mla_a2q(T):
    SC = 96 ** -0.5
    L = Lin(T, 512, 1536)
    P = L.P
    Y = P.dout("Y", [T, 1536], BF16)
    COS = P.din("COS", [T, 16], F32)
    SIN = P.din("SIN", [T, 16], F32)
    L.load_weights()
    ob = [P.sb("ob", [128, 1536], BF16) for _ in range(2)]
    rb = [P.sb("rb", [128, 1536], F32) for _ in range(2)]
    cs = [P.sb("cs", [128, 4, 16], F32) for _ in range(2)]
    sn = [P.sb("sn", [128, 4, 16], F32) for _ in range(2)]
    tm = [P.sb("tm", [128, 4, 16, 16], F32) for _ in range(2)]

    def pre_group(g, t0, gs):
        nj = gs // 128
        P.dma("sp", cs[g % 2][:, 0:nj, :], COS[t0:t0 + gs, :].rearrange("(j p) f -> p j f", p=128), (), (cs[g % 2],))
        P.dma("sp", sn[g % 2][:, 0:nj, :], SIN[t0:t0 + gs, :].rearrange("(j p) f -> p j f", p=128), (), (sn[g % 2],))

    def blk_ep(tt, j, nb, n0, ns, ps):
        r = rb[tt % 2]
        evac(P, nb, r[:, n0:n0 + ns], ps[:, 0:ns], (ps,), (r,))

    def row_ep(tt, j, t0):
        g = t0 // 512
        r, o, t = rb[tt % 2], ob[tt % 2], tm[tt % 2]
        rv = r[:, :].rearrange("p (h d) -> p h d", h=16)
        ov = o[:, :].rearrange("p (h d) -> p h d", h=16)
        P.act(ov[:, :, 0:64], rv[:, :, 0:64], AF.Copy, (r,), (o,), scale=SC)
        c = bcast_mid(cs[g % 2][:, j, :], 16)
        s = bcast_mid(sn[g % 2][:, j, :], 16)
        x1, x2 = rv[:, :, 64:80], rv[:, :, 80:96]
        P.tt(t[:, 0], x1, c, ALU.mult, (r, cs[g % 2]), (t,))
        P.tt(t[:, 1], x2, s, ALU.mult, (r, sn[g % 2]), (t,))
        P.tt(ov[:, :, 64:80], t[:, 0], t[:, 1], ALU.subtract, (t,), (o,))
        P.tt(t[:, 2], x1, s, ALU.mult, (r, sn[g % 2]), (t,))
        P.tt(t[:, 3], x2, c, ALU.mult, (r, cs[g % 2]), (t,))
        P.tt(ov[:, :, 80:96], t[:, 2], t[:, 3], ALU.add, (t,), (o,))
        P.dma("pool", Y[t0:t0 + 128, :], o[:], (o,), ())

    L.pre_group, L.blk_ep, L.row_ep = pre_group, blk_ep, row_ep
    return L.run()


def build_swiglu(T):
    F = 2816
    blocks = []
    for n0 in range(0, F, 512):
        ns = min(512, F - n0)
        blocks += [(n0, ns), (F + n0, ns)]
    L = Lin(T, 1024, 2 * F, blocks)
    P = L.P
    Y = P.dout("Y", [T, F], BF16)
    L.load_weights()
    ob = [P.sb("ob", [128, F], BF16) for _ in range(2)]
    sg = [P.sb("sg", [128, 512], F32) for _ in range(2)]

    def blk_ep(tt, j, nb, n0, ns, ps):
        s = sg[(nb // 2) % 2]
        if nb % 2 == 0:
            P.act(s[:, 0:ns], ps[:, 0:ns], AF.Silu, (ps,), (s,))
        else:
            o = ob[tt % 2]
            g0 = n0 - F
            P.tt(o[:, g0:g0 + ns], ps[:, 0:ns], s[:, 0:ns], ALU.mult, (ps, s), (o,))

    def row_ep(tt, j, t0):
        o = ob[tt % 2]
        P.dma("pool", Y[t0:t0 + 128, :], o[:], (o,), ())

    L.blk_ep, L.row_ep = blk_ep, row_ep
    return L.run()


def build_resid_ln(T, K, T_lat, want_a=True):
    L = Lin(T, K, 1024)
    P = L.P
    H = P.din("H", [T, 1024], F32)
    VEC = P.din("VEC", [128, 8, 1024], F32)
    HO = P.dout("HO", [T, 1024], F32)
    AO = P.dout("AO", [T, 1024], BF16) if want_a else None
    L.load_weights()
    vec = [P.sb("vec", [128, 1024], F32) for _ in range(8)]
    for i in range(8):
        P.dma("sp", vec[i][:], VEC[:, i, :], (), (vec[i],))
    for i in (4, 6):
        P.ts(vec[i][:], vec[i][:], 1.0, None, ALU.add, None, (vec[i],), (vec[i],))
    hb = [P.sb("hb", [128, 4, 1024], F32) for _ in range(2)]
    zb = [P.sb("zb", [128, 1024], F32) for _ in range(2)]
    hn = [P.sb("hn", [128, 1024], F32) for _ in range(2)]
    ho = [P.sb("ho", [128, 1024], F32) for _ in range(2)]
    ao = [P.sb("ao", [128, 1024], BF16) for _ in range(2)]
    at = [P.sb("at", [128, 1024], F32) for _ in range(2)]
    st = [P.sb("st", [128, 2, 6], F32) for _ in range(2)]
    mv = [P.sb("mv", [128, 4], F32) for _ in range(2)]
    epsb = P.sb("epsb", [128, 1], F32)
    P.memset(epsb[:], LN_EPS, (epsb,))

    def pre_group(g, t0, gs):
        nj = gs // 128
        P.dma("sp", hb[g % 2][:, 0:nj, :], H[t0:t0 + gs, :].rearrange("(j p) f -> p j f", p=128), (), (hb[g % 2],))

    def blk_ep(tt, j, nb, n0, ns, ps):
        z = zb[tt % 2]
        gate = vec[0] if tt < T_lat // 128 else vec[1]
        P.tt(z[:, n0:n0 + ns], ps[:, 0:ns], gate[:, n0:n0 + ns], ALU.mult, (ps, gate), (z,))

    def row_ep(tt, j, t0):
        g = t0 // 512
        lat = tt < T_lat // 128
        z, h, n_, o, a, a_t, s, m = zb[tt % 2], hb[g % 2], hn[tt % 2], ho[tt % 2], ao[tt % 2], at[tt % 2], st[tt % 2], mv[tt % 2]
        P.stt(z[:], h[:, j, :], ALPHA, z[:], ALU.mult, ALU.add, (h, z), (z,))
        for c in range(2):
            P.op("dve", lambda e, c=c: e.bn_stats(s[:, c, :], z[:, c * 512:(c + 1) * 512]), (z,), (s,))
        P.op("dve", lambda e: e.bn_aggr(m[:, 0:2], s[:, :, :].rearrange("p a b -> p (a b)")), (s,), (m,))
        P.act(m[:, 2:3], m[:, 1:2], AF.Sqrt, (m, epsb), (m,), bias=epsb[:, 0:1], scale=1.0)
        P.op("dve", lambda e: e.reciprocal(m[:, 2:3], m[:, 2:3]), (m,), (m,))
        P.ts(m[:, 3:4], m[:, 0:1], -1.0, m[:, 2:3], ALU.mult, ALU.mult, (m,), (m,))
        P.act(n_[:], z[:], AF.Identity, (z, m), (n_,), bias=m[:, 3:4], scale=m[:, 2:3])
        P.tt(n_[:], n_[:], vec[2][:], ALU.mult, (n_, vec[2]), (n_,), eng="pool")
        P.tt(o[:], n_[:], vec[3][:], ALU.add, (n_, vec[3]), (o,), eng="pool")
        P.dma("pool", HO[t0:t0 + 128, :], o[:], (o,), ())
        if want_a:
            sc, sh = (vec[4], vec[5]) if lat else (vec[6], vec[7])
            P.tt(a_t[:], o[:], sc[:], ALU.mult, (o, sc), (a_t,))
            P.tt(a[:], a_t[:], sh[:], ALU.add, (a_t, sh), (a,))
            P.dma("pool", AO[t0:t0 + 128, :], a[:], (a,), ())

    L.pre_group, L.blk_ep, L.row_ep = pre_group, blk_ep, row_ep
    return L.run()


def build_mod0(T, T_lat):
    P = Prog()
    X = P.din("X", [T, 1024], F32)
    VEC = P.din("VEC", [128, 4, 1024], F32)
    AO = P.dout("AO", [T, 1024], BF16)
    vec = [P.sb("vec", [128, 1024], F32) for _ in range(4)]
    for i in range(4):
        P.dma("sp", vec[i][:], VEC[:, i, :], (), (vec[i],))
    for i in (0, 2):
        P.ts(vec[i][:], vec[i][:], 1.0, None, ALU.add, None, (vec[i],), (vec[i],))
    xb = [P.sb("xb", [128, 1024], F32) for _ in range(3)]
    at = [P.sb("at", [128, 1024], F32) for _ in range(2)]
    ao = [P.sb("ao", [128, 1024], BF16) for _ in range(2)]
    for tt in range(T // 128):
        x, a_t, a = xb[tt % 3], at[tt % 2], ao[tt % 2]
        sc, sh = (vec[0], vec[1]) if tt < T_lat // 128 else (vec[2], vec[3])
        P.dma("sp", x[:], X[tt * 128:(tt + 1) * 128, :], (), (x,))
        eng = "dve" if tt % 2 == 0 else "pool"
        P.tt(a_t[:], x[:], sc[:], ALU.mult, (x, sc), (a_t,), eng=eng)
        P.tt(a[:], a_t[:], sh[:], ALU.add, (a_t, sh), (a,), eng=eng)
        P.dma("pool", AO[tt * 128:(tt + 1) * 128, :], a[:], (a,), ())
    return P


def build_mla_b(NH, NQ_LAT, NCTX, DQK=96, DV=64):
    P = Prog()
    NK = NCTX + NQ_LAT
    NKT = NK // 128
    QT = P.din("QT", [NH, DQK, NK], BF16)
    KT = P.din("KT", [NH, DQK, NK], BF16)
    VA = P.din("VA", [NH, NK, DV + 1], BF16)
    O = P.dout("O", [NH, NK, DV], BF16)
    qb = [P.sb("q", [DQK, NK], BF16) for _ in range(2)]
    kb = [P.sb("k", [DQK, NK], BF16) for _ in range(2)]
    vb = [P.sb("v", [128, NKT, DV + 1], BF16) for _ in range(2)]
    pss = [P.ps("pss", [128, 512]) for _ in range(3)]
    pso = [P.ps("pso", [128, 512]) for _ in range(4)]
    pt = [P.sb("pt", [128, 512], BF16) for _ in range(3)]
    rc = [P.sb("rc", [128, 4, 1], F32) for _ in range(2)]
    ob = [P.sb("ob", [128, 4, DV], BF16) for _ in range(2)]

    def load(h):
        P.dma("sp", qb[h % 2][:], QT[h], (), (qb[h % 2],))
        P.dma("sp", kb[h % 2][:], KT[h], (), (kb[h % 2],))
        P.dma("sp", vb[h % 2][:], VA[h].rearrange("(t p) d -> p t d", p=128), (), (vb[h % 2],))

    qblocks = [(0, NCTX, NCTX // 128)] + [(NCTX + i * 512, 512, NKT) for i in range(NQ_LAT // 512)]
    load(0)
    ctr = 0
    bi = 0
    for h in range(NH):
        if h + 1 < NH:
            load(h + 1)
        q, k, v = qb[h % 2], kb[h % 2], vb[h % 2]
        for (q0, qn, nkt) in qblocks:
            nqi = qn // 128

            def S(kt, c):
                P.mm(pss[c % 3][:, 0:qn], k[:, kt * 128:(kt + 1) * 128], q[:, q0:q0 + qn], True, True, (k, q), (pss[c % 3],))

            S(0, ctr)
            for kt in range(nkt):
                c = ctr + kt
                if kt + 1 < nkt:
                    S(kt + 1, c + 1)
                P.act(pt[c % 3][:, 0:qn], pss[c % 3][:, 0:qn], AF.Exp, (pss[c % 3],), (pt[c % 3],))
                for qi in range(nqi):
                    P.mm(pso[qi][:, 0:DV + 1], pt[c % 3][:, qi * 128:(qi + 1) * 128], v[:, kt, :], kt == 0, kt == nkt - 1,
                         (pt[c % 3], v), (pso[qi],))
            ctr += nkt
            r, o = rc[bi % 2], ob[bi % 2]
            for qi in range(nqi):
                po = pso[qi]
                P.op("dve", lambda e, r=r, po=po, qi=qi: e.reciprocal(r[:, qi, :], po[:, DV:DV + 1]), (po,), (r,))
                P.ts(o[:, qi, :], po[:, 0:DV], r[:, qi, :], None, ALU.mult, None, (po, r), (o,))
            P.dma("pool", O[h, q0:q0 + qn, :].rearrange("(j p) d -> p j d", p=128), o[:, 0:nqi, :], (o,), ())
            bi += 1
    return P


def build_na_b(NH, NLT, NCT, DH=64):
    P = Prog()
    NT = NLT + NCT
    NTOK = NT * 128
    SC = DH ** -0.5
    QT = P.din("QT", [NH, DH, NTOK], BF16)
    KT = P.din("KT", [NH, DH, NTOK], BF16)
    VA = P.din("VA", [NH, NTOK, DH + 1], BF16)
    MI = P.din("MI", [NH, 128, 5, 128], F32)
    MB = P.din("MB", [NH, 128, 7, 128], F32)
    MKI = P.din("MKI", [128, 5, 128], F32)
    MKB = P.din("MKB", [128, 7, 128], F32)
    O = P.dout("O", [NH, NTOK, DH], BF16)
    qb = [P.sb("q", [DH, NTOK], BF16) for _ in range(2)]
    kb = [P.sb("k", [DH, NTOK], BF16) for _ in range(2)]
    vb = [P.sb("v", [128, NT, DH + 1], BF16) for _ in range(2)]
    mi = [P.sb("mi", [128, 5, 128], F32) for _ in range(2)]
    mb = [P.sb("mb", [128, 7, 128], F32) for _ in range(2)]
    mki = P.sb("mki", [128, 5, 128], F32)
    mkb = P.sb("mkb", [128, 7, 128], F32)
    P.dma("sp", mki[:], MKI, (), (mki,))
    P.dma("sp", mkb[:], MKB, (), (mkb,))
    psa = [P.ps("psa", [128, 4, 128]) for _ in range(2)]
    psb = [P.ps("psb", [128, 4, 128]) for _ in range(2)]
    pso = [P.ps("pso", [128, DH + 1]) for _ in range(2)]
    sa = [P.sb("sa", [128, 5, 128], F32) for _ in range(2)]
    pt = [P.sb("pt", [128, 8, 128], BF16) for _ in range(2)]
    rc = [P.sb("rc", [128, 1], F32) for _ in range(2)]
    ob = [P.sb("ob", [128, DH], BF16) for _ in range(2)]

    def load(h):
        P.dma("sp", qb[h % 2][:], QT[h], (), (qb[h % 2],))
        P.dma("sp", kb[h % 2][:], KT[h], (), (kb[h % 2],))
        P.dma("sp", vb[h % 2][:], VA[h].rearrange("(t p) d -> p t d", p=128), (), (vb[h % 2],))
        P.dma("sp", mi[h % 2][:], MI[h], (), (mi[h % 2],))
        P.dma("sp", mb[h % 2][:], MB[h], (), (mb[h % 2],))
        P.tt(mi[h % 2][:], mi[h % 2][:], mki[:], ALU.add, (mi[h % 2], mki), (mi[h % 2],), eng="pool")
        P.tt(mb[h % 2][:], mb[h % 2][:], mkb[:], ALU.add, (mb[h % 2], mkb), (mb[h % 2],), eng="pool")

    load(0)
    u = 0
    for h in range(NH):
        if h + 1 < NH:
            load(h + 1)
        q, k, v = qb[h % 2], kb[h % 2], vb[h % 2]
        for qt in range(NT):
            if qt < NLT:
                if 2 <= qt <= NLT - 3:
                    kts = list(range(qt - 2, qt + 3))
                    mask = mi[h % 2]
                    m0 = 0
                else:
                    k0 = 0 if qt < 2 else NLT - 4
                    kts = list(range(k0, k0 + 4))
                    mask = mb[h % 2]
                    m0 = (k0 - qt) + 3
                ctx = list(range(NLT, NT))
            else:
                kts, mask, m0 = [], None, 0
                ctx = list(range(NLT, NT))
            pa, pb, s, p_, r, o, po = psa[u % 2], psb[u % 2], sa[u % 2], pt[u % 2], rc[u % 2], ob[u % 2], pso[u % 2]
            qs = q[:, qt * 128:(qt + 1) * 128]
            nl = len(kts)
            for i, kt in enumerate(kts[:4]):
                P.mm(pa[:, i, :], k[:, kt * 128:(kt + 1) * 128], qs, True, True, (k, q), (pa,))
            rest = kts[4:] + ctx
            for i, kt in enumerate(rest):
                P.mm(pb[:, i, :], k[:, kt * 128:(kt + 1) * 128], qs, True, True, (k, q), (pb,))
            n4 = min(nl, 4)
            if nl:
                P.stt(s[:, 0:n4, :], pa[:, 0:n4, :], SC, mask[:, m0:m0 + n4, :], ALU.mult, ALU.add, (pa, mask), (s,))
                if nl > 4:
                    P.stt(s[:, 4:5, :], pb[:, 0:1, :], SC, mask[:, m0 + 4:m0 + 5, :], ALU.mult, ALU.add, (pb, mask), (s,))
                P.act(p_[:, 0:nl, :], s[:, 0:nl, :], AF.Exp, (s,), (p_,))
            nr = len(rest) - (nl - n4)
            P.act(p_[:, nl:nl + nr, :], pb[:, nl - n4:nl - n4 + nr, :], AF.Exp, (pb,), (p_,), scale=SC)
            allk = kts + ctx
            for i, kt in enumerate(allk):
                P.mm(po[:], p_[:, i, :], v[:, kt, :], i == 0, i == len(allk) - 1, (p_, v), (po,))
            P.op("dve", lambda e, r=r, po=po: e.reciprocal(r[:], po[:, DH:DH + 1]), (po,), (r,))
            P.ts(o[:], po[:, 0:DH], r[:, 0:1], None, ALU.mult, None, (po, r), (o,))
            P.dma("pool", O[h, qt * 128:(qt + 1) * 128, :], o[:], (o,), ())
            u += 1
    return P


def na_mask_tables(rpb):
    krl = (np.arange(128) // 64)[:, None, None]
    kc = (np.arange(128) % 64)[:, None, None]
    qrl = (np.arange(128) // 64)[None, None, :]
    qc = (np.arange(128) % 64)[None, None, :]
    ws = np.clip(qc - 8, 0, 48)
    col_ok = (kc >= ws) & (kc < ws + 16)
    ci = np.clip(kc - qc + 15, 0, 30)

    def tab(ds, row_window):
        d = np.asarray(ds)[None, :, None]
        rel = 2 * d + krl - qrl
        ok = col_ok & np.ones_like(rel, bool)
        if row_window:
            ok = ok & (rel >= -4) & (rel <= 3)
        ri = np.clip(rel + 7, 0, 14)
        cib = np.broadcast_to(ci, rel.shape)
        g = rpb[:, ri, cib]
        mk = np.where(ok, 0.0, -1e30).astype(np.float32)
        return np.ascontiguousarray(g.astype(np.float32)), np.ascontiguousarray(mk)

    MI, MKI = tab([-2, -1, 0, 1, 2], True)
    MB, MKB = tab([-3, -2, -1, 0, 1, 2, 3], False)
    return MI, MB, MKI, MKB


def build_gla(kind, NI, T, NCTX):
    ret = kind == "ret"
    DKC = 2 if ret else 1
    DK = 128 * DKC
    DV = 512 if ret else 128
    CQ = 1.0 if ret else 128 ** -0.5
    CK = 1.0 / 16 if ret else 1.0
    C = 128
    P = Prog()
    QT = P.din("QT", [NI, DK, T], BF16)
    if ret:
        KT = P.din("KT", [NI, DK, T], BF16)
        DEC = P.din("DEC", [128, NI, 2], F32)
    else:
        FT = P.din("FT", [NI, 2, DK, T], BF16)
        LBR = P.din("LBR", [128, NI, 4], F32)
        NG = P.din("NG", [128, DV], F32)
    V = P.din("V", [NI, T, DV], BF16)
    SG = P.din("SG", [NI, T, DV], BF16)
    IDN = P.din("IDN", [128, 128], BF16)
    TRI = P.din("TRI", [2, 128, 128], F32)
    U = P.dout("U", [NI, T, DV], BF16)
    OF = P.nc.dram_tensor("OF", [NI, T, DV], F32, kind="Internal").ap()
    NCH = T // C
    scr = [[Buf("scr", None) for _ in range(NCH)] for _ in range(NI)]

    idn = P.sb("idn", [128, 128], BF16)
    tri = [P.sb("tri", [128, 128], F32) for _ in range(2)]
    P.dma("sp", idn[:], IDN, (), (idn,))
    for d in range(2):
        P.dma("sp", tri[d][:], TRI[d], (), (tri[d],))
    ones = P.sb("ones", [128, C], F32)
    P.memset(ones[:], 1.0, (ones,))
    epsb = P.sb("epsb", [128, 1], F32)
    P.memset(epsb[:], RMS_EPS, (epsb,))
    if ret:
        dec = P.sb("dec", [128, NI, 2], F32)
        P.dma("sp", dec[:], DEC, (), (dec,))
        P.act(dec[:], dec[:], AF.Exp, (dec,), (dec,))
        P.ts(dec[:], dec[:], -1.0, None, ALU.mult, None, (dec,), (dec,))
    else:
        lbr = P.sb("lbr", [128, NI, 4], F32)
        lb = P.sb("lb", [128, NI, 4], F32)
        ng = P.sb("ng", [128, DV], F32)
        P.dma("sp", lbr[:], LBR, (), (lbr,))
        P.dma("sp", ng[:], NG, (), (ng,))
        P.act(lbr[:], lbr[:], AF.Exp, (lbr,), (lbr,))
        for it in range(NI):
            P.op("dve", lambda e, it=it: e.reduce_sum(lb[:, it, 2:3], lbr[:, it, :], axis=AX.X), (lbr,), (lb,))
            P.op("dve", lambda e, it=it: e.reciprocal(lb[:, it, 3:4], lb[:, it, 2:3]), (lb,), (lb,))
            P.tt(lb[:, it, 1:2], lbr[:, it, 0:1], lb[:, it, 3:4], ALU.mult, (lbr, lb), (lb,))
            P.ts(lb[:, it, 0:1], lb[:, it, 1:2], -1.0, 1.0, ALU.mult, ALU.add, (lb,), (lb,))

    qg = [P.sb("qg", [128, DKC, 512], BF16) for _ in range(2)]
    kg = [P.sb("kg", [128, DKC, 512], BF16) for _ in range(2)]
    vg = [P.sb("vg", [128, 4, DV], BF16) for _ in range(2)]
    sgg = [P.sb("sgg", [128, 4, DV], BF16) for _ in range(2)]
    ofg = [P.sb("ofg", [128, 4, DV], F32) for _ in range(2)]
    S = [P.sb("S", [128, DV], F32) for _ in range(DKC)]
    Sb = [P.sb("Sb", [128, DV], BF16) for _ in range(DKC)]
    lgT = P.sb("lgT", [128, C], F32)
    csT = P.sb("csT", [128, C], F32)
    bT = P.sb("bT", [128, C], F32)
    NDT = 1 if ret else 2
    eb = [P.sb("eb", [128, C], F32) for _ in range(NDT)]
    enb = [P.sb("enb", [128, C], F32) for _ in range(NDT)]
    ekk = [P.sb("ekk", [128, C], F32) for _ in range(NDT)]
    ebl = [P.sb("ebl", [128, 1], F32) for _ in range(NDT)]
    kTf = [P.sb("kTf", [128, C], F32) for _ in range(2)]
    tmp = [P.sb("tmp", [128, C], F32) for _ in range(2)]
    qd = [P.sb("qd", [128, DKC, C], BF16) for _ in range(2)]
    kd = [P.sb("kd", [128, DKC, C], BF16) for _ in range(2)]
    kkT = [P.sb("kkT", [128, DKC, C], BF16) for _ in range(2)]
    kk = [P.sb("kk", [128, DK], BF16) for _ in range(2)]
    att = [P.sb("att", [128, C], BF16) for _ in range(2)]
    osb = [P.sb("osb", [128, DV], F32) for _ in range(2)]
    junk = P.sb("junk", [128, DV], F32)
    st = [P.sb("st", [128, 2], F32) for _ in range(2)]
    ub = [P.sb("ub", [128, DV], BF16) for _ in range(2)]
    ps_t = P.ps("ps_t", [128, C], BF16)
    ps_a = [P.ps("ps_a", [128, 512]) for _ in range(2)]
    ps_o = [P.ps("ps_o", [128, 512]) for _ in range(2)]
    ps_d = [P.ps("ps_d", [128, 512]) for _ in range(DKC)]

    def decay_tiles(lg_ap, lg_reads, dirn, i):
        P.op("dve", lambda e: e.tensor_tensor_scan(csT[:], ones[:], lg_ap, 0.0, ALU.mult, ALU.add), (ones,) + lg_reads, (csT,))
        tot = csT[:, C - 1:C]
        if dirn == 0:
            b, breads = csT, (csT,)
        else:
            P.ts(bT[:], csT[:], tot, None, ALU.subtract, None, (csT,), (bT,))
            P.stt(bT[:], bT[:], -1.0, lg_ap, ALU.mult, ALU.add, (bT,) + lg_reads, (bT,))
            b, breads = bT, (bT, csT)
        P.act(eb[i][:], b[:], AF.Exp, breads, (eb[i],))
        P.act(enb[i][:], b[:], AF.Exp, breads, (enb[i],), scale=-1.0)
        P.act(ekk[i][:], b[:], AF.Exp, breads, (ekk[i],), scale=-1.0, bias=tot)
        P.act(ebl[i][:], tot, AF.Exp, (csT,), (ebl[i],))

    groups = [(0, NCTX)] + [(t0, 512) for t0 in range(NCTX, T, 512)]
    cc = 0
    for it in range(NI):
        QTv = QT[it].rearrange("(c p) t -> p c t", p=128)
        for dirn in range(2):
            for c in range(DKC):
                P.memset(S[c][:], 0.0, (S[c],))
                P.memset(Sb[c][:], 0.0, (Sb[c],), eng="pool")
            if ret:
                P.copy(lgT[:], dec[:, it, dirn:dirn + 1].broadcast_to([128, C]), (dec,), (lgT,))
                decay_tiles(lgT[:], (lgT,), dirn, 0)
            if dirn == 0:
                gorder = list(range(len(groups)))
            else:
                gorder = [0] + list(range(len(groups) - 1, 0, -1))
            KTv = (KT[it] if ret else FT[it, dirn]).rearrange("(c p) t -> p c t", p=128)

            def load(gi, n):
                t0, gs = groups[gi]
                nj = gs // 128
                P.dma("sp", qg[n % 2][:, :, 0:gs], QTv[:, :, t0:t0 + gs], (), (qg[n % 2],))
                P.dma("sp", kg[n % 2][:, :, 0:gs], KTv[:, :, t0:t0 + gs], (), (kg[n % 2],))
                P.dma("sp", vg[n % 2][:, 0:nj, :], V[it, t0:t0 + gs, :].rearrange("(j p) d -> p j d", p=128), (), (vg[n % 2],))
                if dirn == 1:
                    P.dma("sp", sgg[n % 2][:, 0:nj, :], SG[it, t0:t0 + gs, :].rearrange("(j p) d -> p j d", p=128), (), (sgg[n % 2],))
                    for j in range(nj):
                        ch = t0 // 128 + j
                        P.dma("sp", ofg[n % 2][:, j, :], OF[it, ch * 128:(ch + 1) * 128, :], (scr[it][ch],), (ofg[n % 2],))

            load(gorder[0], 0)
            for n, gi in enumerate(gorder):
                if n + 1 < len(gorder):
                    load(gorder[n + 1], n + 1)
                t0, gs = groups[gi]
                nj = gs // 128
                q_, k_, v_, sg_, of_ = qg[n % 2], kg[n % 2], vg[n % 2], sgg[n % 2], ofg[n % 2]
                for j in (range(nj) if dirn == 0 else range(nj - 1, -1, -1)):
                    ch = t0 // 128 + j
                    x = cc % 2
                    cc += 1
                    sl = slice(j * 128, (j + 1) * 128)
                    if ret:
                        di = 0
                        ksrc = [k_[:, c, sl] for c in range(DKC)]
                        kreads = (k_,)
                    else:
                        di = x
                        t_, kf = tmp[x], kTf[x]
                        P.act(t_[:], k_[:, 0, sl], AF.Exp, (k_,), (t_,), scale=-1.0)
                        P.ts(t_[:], t_[:], 1.0, None, ALU.add, None, (t_,), (t_,))
                        P.op("dve", lambda e, t_=t_: e.reciprocal(t_[:], t_[:]), (t_,), (t_,))
                        P.ts(t_[:], t_[:], lb[:, it, 1:2], lb[:, it, 0:1], ALU.mult, ALU.add, (t_, lb), (t_,))
                        P.ts(kf[:], t_[:], -1.0, 1.0, ALU.mult, ALU.add, (t_,), (kf,))
                        P.act(t_[:], t_[:], AF.Ln, (t_,), (t_,))
                        decay_tiles(t_[:], (t_,), dirn, di)
                        ksrc = [kf[:]]
                        kreads = (kf,)
                    qd_, kd_, kkT_, kk_, att_ = qd[x], kd[x], kkT[x], kk[x], att[x]
                    for c in range(DKC):
                        P.stt(qd_[:, c, :], q_[:, c, sl], CQ, eb[di][:], ALU.mult, ALU.mult, (q_, eb[di]), (qd_,))
                        P.stt(kd_[:, c, :], ksrc[c], CK, enb[di][:], ALU.mult, ALU.mult, kreads + (enb[di],), (kd_,))
                        P.stt(kkT_[:, c, :], ksrc[c], CK, ekk[di][:], ALU.mult, ALU.mult, kreads + (ekk[di],), (kkT_,))
                    pa = ps_a[x]
                    for c in range(DKC):
                        P.mm(pa[:, 0:C], kd_[:, c, :], qd_[:, c, :], c == 0, c == DKC - 1, (kd_, qd_), (pa,))
                    P.tt(att_[:], pa[:, 0:C], tri[dirn][:], ALU.mult, (pa, tri[dirn]), (att_,))
                    for c in range(DKC):
                        P.transpose(ps_t[:], kkT_[:, c, :], idn[:], (kkT_, idn), (ps_t,))
                        P.act(kk_[:, c * 128:(c + 1) * 128], ps_t[:], AF.Copy, (ps_t,), (kk_,))
                    po = ps_o[x]
                    P.mm(po[:, 0:DV], att_[:], v_[:, j, :], True, False, (att_, v_), (po,))
                    for c in range(DKC):
                        P.mm(po[:, 0:DV], qd_[:, c, :], Sb[c][:], False, c == DKC - 1, (qd_, Sb[c]), (po,))
                    o_ = osb[x]
                    if dirn == 0:
                        P.act(o_[:], po[:, 0:DV], AF.Copy, (po,), (o_,))
                        P.dma("pool", OF[it, ch * 128:(ch + 1) * 128, :], o_[:], (o_,), (scr[it][ch],))
                    else:
                        s_, u_ = st[x], ub[x]
                        P.tt(o_[:], po[:, 0:DV], of_[:, j, :], ALU.add, (po, of_), (o_,))
                        P.act(junk[:], o_[:], AF.Square, (o_,), (junk, s_), accum_out=s_[:, 0:1])
                        P.act(s_[:, 1:2], s_[:, 0:1], AF.Sqrt, (s_, epsb), (s_,), bias=epsb[:, 0:1], scale=1.0 / DV)
                        P.op("dve", lambda e, s_=s_: e.reciprocal(s_[:, 1:2], s_[:, 1:2]), (s_,), (s_,))
                        if ret:
                            P.stt(u_[:], o_[:], s_[:, 1:2], sg_[:, j, :], ALU.mult, ALU.mult, (o_, s_, sg_), (u_,))
                        else:
                            P.stt(o_[:], o_[:], s_[:, 1:2], ng[:], ALU.mult, ALU.mult, (o_, s_, ng), (o_,))
                            P.tt(u_[:], o_[:], sg_[:, j, :], ALU.mult, (o_, sg_), (u_,), eng="pool")
                        P.dma("pool", U[it, ch * 128:(ch + 1) * 128, :], u_[:], (u_,), ())
                    for c in range(DKC):
                        P.mm(ps_d[c][:, 0:DV], kk_[:, c * 128:(c + 1) * 128], v_[:, j, :], True, True, (kk_, v_), (ps_d[c],))
                        P.stt(S[c][:], S[c][:], ebl[di][:, 0:1], ps_d[c][:, 0:DV], ALU.mult, ALU.add, (S[c], ebl[di], ps_d[c]), (S[c],))
                        P.act(Sb[c][:], S[c][:], AF.Copy, (S[c],), (Sb[c],))
    return P


def build_k0():
    P = Prog()
    condT = P.din("condT", [128, 8, 5], F32)
    W = P.din("W", [1024, 3072], F32)
    bias = P.din("bias", [1, 3072], F32)
    out = P.dout("out", [5, 3072], F32)
    ct = P.sb("ct", [128, 8, 5], F32)
    cs = P.sb("cs", [128, 8, 5], F32)
    ones = P.sb("ones", [1, 5], F32)
    bt = P.sb("bt", [1, 3072], F32)
    ot = P.sb("ot", [5, 3072], F32)
    P.dma("sp", ct[:], condT, (), (ct,))
    P.dma("sp", bt[:], bias, (), (bt,))
    P.memset(ones[:], 1.0, (ones,))
    P.act(cs[:], ct[:], AF.Silu, (ct,), (cs,))
    wt = []
    Wv = W.rearrange("(k p) n -> p k n", p=128)
    for k in range(8):
        w = P.sb("w", [128, 3072], F32)
        P.dma("sp" if k % 2 == 0 else "pool", w[:], Wv[:, k, :], (), (w,))
        wt.append(w)
    pss = [P.ps("ps", [5, 512]) for _ in range(2)]
    for nb in range(6):
        ps = pss[nb % 2]
        sl = slice(nb * 512, (nb + 1) * 512)
        for k in range(8):
            P.mm(ps[:], cs[:, k, :], wt[k][:, sl], k == 0, False, (cs, wt[k]), (ps,))
        P.mm(ps[:], ones[:], bt[:, sl], False, True, (ones, bt), (ps,))
        P.copy(ot[:, sl], ps[:], (ps,), (ot,))
    P.dma("pool", out, ot[:], (ot,), ())
    return P


B_, L_, LC_, D_ = 4, 8192, 256, 1024
TL_ = L_ // 2
TC_ = LC_ // 2
TCORE = TL_ + TC_


def _rep(v):
    v = np.asarray(v)
    return np.ascontiguousarray(np.broadcast_to(v[None], (128,) + v.shape))


def _to_cores(lat, ctx):
    out = []
    for c in range(NCORES):
        b, hf = c // 2, c % 2
        out.append(np.concatenate([lat[b, hf * TL_:(hf + 1) * TL_], ctx[b, hf * TC_:(hf + 1) * TC_]], 0))
    return out


def _from_cores(arrs):
    lat = np.stack([np.concatenate([arrs[2 * b][:TL_], arrs[2 * b + 1][:TL_]], 0) for b in range(B_)])
    ctx = np.stack([np.concatenate([arrs[2 * b][TL_:], arrs[2 * b + 1][TL_:]], 0) for b in range(B_)])
    return lat, ctx


def _T(a):
    return np.ascontiguousarray(a.T)


def _rope_tables(rot_dim):
    t = np.arange(L_)
    rows = (t // 64).astype(np.float32)
    cols = (t % 64).astype(np.float32)
    nf = rot_dim // 4
    inv = (np.float32(10000.0) ** (-np.arange(nf, dtype=np.float32) / np.float32(nf))).astype(np.float32)
    ang = np.concatenate([rows[:, None] * inv, cols[:, None] * inv], -1).astype(np.float32)
    cos, sin = np.cos(ang).astype(np.float32), np.sin(ang).astype(np.float32)
    nc_ = rot_dim // 2
    cc = [np.concatenate([cos[(c % 2) * TL_:(c % 2 + 1) * TL_], np.ones((TC_, nc_), np.float32)], 0) for c in range(NCORES)]
    ss = [np.concatenate([sin[(c % 2) * TL_:(c % 2 + 1) * TL_], np.zeros((TC_, nc_), np.float32)], 0) for c in range(NCORES)]
    return cc, ss


def kernel(x, c, ctx, c_ctx, ada_w, ada_b, ln_g, ln_b, ffn_w13, ffn_w2,
           ret_w_in, ret_decay, ret_w_out, na_w_qkv, na_rpb, na_w_out,
           mla_w_down, mla_q_norm, mla_kv_norm, mla_w_uq, mla_w_ukv, mla_w_out,
           hg_w_in, hg_lower_bounds, hg_norm_g, hg_w_out):
    f32 = np.float32
    A = lambda a: np.ascontiguousarray(np.asarray(a, dtype=f32))
    x, c, ctx, c_ctx = A(x), A(c), A(ctx), A(c_ctx)
    ada_w, ada_b, ln_g, ln_b, ffn_w13, ffn_w2 = A(ada_w), A(ada_b), A(ln_g), A(ln_b), A(ffn_w13), A(ffn_w2)

    cond = np.concatenate([c, c_ctx[None]], 0)
    condT = np.ascontiguousarray(cond.T.reshape(8, 128, 5).transpose(1, 0, 2))
    res = run(build_k0(), [{"condT": condT, "W": np.ascontiguousarray(ada_w[i // 2][:, (i % 2) * 3072:(i % 2 + 1) * 3072]),
                            "bias": np.ascontiguousarray(ada_b[i // 2][None, (i % 2) * 3072:(i % 2 + 1) * 3072])} for i in range(NCORES)])
    mod = [np.concatenate([res[2 * i]["out"], res[2 * i + 1]["out"]], 1).reshape(5, 6, 1024) for i in range(4)]

    def vec8(i, gate_slot, lnk, nxt, b):
        if nxt is None:
            sc_l = sh_l = sc_c = sh_c = np.zeros(1024, f32)
        else:
            li, s_scale, s_shift = nxt
            sc_l, sh_l, sc_c, sh_c = mod[li][b, s_scale], mod[li][b, s_shift], mod[li][4, s_scale], mod[li][4, s_shift]
        return _rep(np.stack([mod[i][b, gate_slot], mod[i][4, gate_slot], ln_g[i, lnk], ln_b[i, lnk], sc_l, sh_l, sc_c, sh_c]))

    h = _to_cores(x, ctx)
    res = run(build_mod0(TCORE, TL_), [{"X": h[k], "VEC": _rep(np.stack([mod[0][k // 2, 1], mod[0][k // 2, 0], mod[0][4, 1], mod[0][4, 0]]))}
                                       for k in range(NCORES)])
    a = [r["AO"] for r in res]
    bf = a[0].dtype
    ones_col = np.ones((8448, 1), bf)

    def post(i, u_cores, K, w_out, h, last):
        res = run(build_resid_ln(TCORE, K, TL_), [{"XT": _T(u_cores[k]), "W": A(w_out), "H": h[k], "VEC": vec8(i, 2, 0, (i, 4, 3), k // 2)}
                                                   for k in range(NCORES)])
        h1 = [r["HO"] for r in res]
        a2 = [r["AO"] for r in res]
        res = run(build_swiglu(TCORE), [{"XT": _T(a2[k]), "W": ffn_w13[i]} for k in range(NCORES)])
        act = [r["Y"] for r in res]
        nxt = None if last else (i + 1, 1, 0)
        res = run(build_resid_ln(TCORE, 2816, TL_, want_a=not last),
                  [{"XT": _T(act[k]), "W": ffn_w2[i], "H": h1[k], "VEC": vec8(i, 5, 1, nxt, k // 2)} for k in range(NCORES)])
        return [r["HO"] for r in res], (None if last else [r["AO"] for r in res])

    tri = np.stack([np.triu(np.ones((128, 128), f32)), np.tril(np.ones((128, 128), f32))])
    idn = np.eye(128, dtype=f32).astype(bf)

    cs_, sn_ = _rope_tables(256)
    res = run(build_ret_a(TCORE), [{"XT": _T(a[k]), "W": A(ret_w_in), "COS": cs_[k], "SIN": sn_[k]} for k in range(NCORES)])
    lat, cx = _from_cores([r["Y"] for r in res])
    seq = np.concatenate([cx, lat], 1)
    ret_decay = A(ret_decay)
    ins = []
    for k in range(NCORES):
        b, hs = k // 2, [(k % 2) * 2, (k % 2) * 2 + 1]
        ins.append({"QT": np.stack([_T(seq[b][:, hh * 256:(hh + 1) * 256]) for hh in hs]),
                    "KT": np.stack([_T(seq[b][:, 1024 + hh * 256:1024 + (hh + 1) * 256]) for hh in hs]),
                    "V": np.stack([np.ascontiguousarray(seq[b][:, 2048 + hh * 512:2048 + (hh + 1) * 512]) for hh in hs]),
                    "SG": np.stack([np.ascontiguousarray(seq[b][:, 4096 + hh * 512:4096 + (hh + 1) * 512]) for hh in hs]),
                    "DEC": _rep(np.stack([ret_decay[:, hh] for hh in hs])), "IDN": idn, "TRI": tri})
    res = run(build_gla("ret", 2, 8448, LC_), ins)
    u = np.stack([np.concatenate([res[2 * b]["U"][0], res[2 * b]["U"][1], res[2 * b + 1]["U"][0], res[2 * b + 1]["U"][1]], -1) for b in range(B_)])
    h, a = post(0, _to_cores(u[:, LC_:], u[:, :LC_]), 2048, ret_w_out, h, False)

    res = run(build_plain(TCORE, 1024, 3072), [{"XT": _T(a[k]), "W": A(na_w_qkv)} for k in range(NCORES)])
    lat, cx = _from_cores([r["Y"] for r in res])
    seq = np.concatenate([lat, cx], 1)
    MI, MB, MKI, MKB = na_mask_tables(A(na_rpb))
    ins = []
    for k in range(NCORES):
        b, hs = k // 2, range((k % 2) * 8, (k % 2) * 8 + 8)
        ins.append({"QT": np.stack([_T(seq[b][:, hh * 64:(hh + 1) * 64]) for hh in hs]),
                    "KT": np.stack([_T(seq[b][:, 1024 + hh * 64:1024 + (hh + 1) * 64]) for hh in hs]),
                    "VA": np.stack([np.concatenate([seq[b][:, 2048 + hh * 64:2048 + (hh + 1) * 64], ones_col], -1) for hh in hs]),
                    "MI": np.ascontiguousarray(MI[list(hs)]), "MB": np.ascontiguousarray(MB[list(hs)]), "MKI": MKI, "MKB": MKB})
    res = run(build_na_b(8, L_ // 128, LC_ // 128), ins)
    o = np.stack([np.concatenate([res[2 * b + hf]["O"][j] for hf in range(2) for j in range(8)], -1) for b in range(B_)])
    h, a = post(1, _to_cores(o[:, :L_], o[:, L_:]), 1024, na_w_out, h, False)

    cs_, sn_ = _rope_tables(32)
    gn = _rep(np.concatenate([A(mla_q_norm), A(mla_kv_norm)]))
    res = run(build_mla_a1(TCORE), [{"XT": _T(a[k]), "W": A(mla_w_down), "COS": cs_[k], "SIN": sn_[k], "GN": gn} for k in range(NCORES)])
    y1 = [r["Y"] for r in res]
    SC = np.float32(96 ** -0.5)
    res = run(build_mla_a2q(TCORE), [{"XT": _T(y1[k][:, 0:512]), "W": A(mla_w_uq), "COS": cs_[k] * SC, "SIN": sn_[k] * SC} for k in range(NCORES)])
    latq, cxq = _from_cores([r["Y"] for r in res])
    res = run(build_plain(TCORE, 256, 2048), [{"XT": _T(y1[k][:, 512:768]), "W": A(mla_w_ukv)} for k in range(NCORES)])
    latkv, cxkv = _from_cores([r["Y"] for r in res])
    latr, cxr = _from_cores([np.ascontiguousarray(y1[k][:, 768:800]) for k in range(NCORES)])
    qs = np.concatenate([cxq, latq], 1)
    kvs = np.concatenate([cxkv, latkv], 1)
    krs = np.concatenate([cxr, latr], 1)
    ins = []
    for k in range(NCORES):
        b, hs = k // 2, range((k % 2) * 8, (k % 2) * 8 + 8)
        ins.append({"QT": np.stack([_T(qs[b][:, hh * 96:(hh + 1) * 96]) for hh in hs]),
                    "KT": np.stack([_T(np.concatenate([kvs[b][:, hh * 128:hh * 128 + 64], krs[b]], -1)) for hh in hs]),
                    "VA": np.stack([np.concatenate([kvs[b][:, hh * 128 + 64:(hh + 1) * 128], ones_col], -1) for hh in hs])})
    res = run(build_mla_b(8, L_, LC_), ins)
    o = np.stack([np.concatenate([res[2 * b + hf]["O"][j] for hf in range(2) for j in range(8)], -1) for b in range(B_)])
    h, a = post(2, _to_cores(o[:, LC_:], o[:, :LC_]), 1024, mla_w_out, h, False)

    res = run(build_plain(TCORE, 1024, 5120, silu_blocks=(0, 1, 8, 9)), [{"XT": _T(a[k]), "W": A(hg_w_in)} for k in range(NCORES)])
    lat, cx = _from_cores([r["Y"] for r in res])
    seq = np.concatenate([cx, lat], 1)
    lbw = A(hg_lower_bounds)
    ngr = _rep(A(hg_norm_g))
    ins = []
    for k in range(NCORES):
        b, hs = k // 2, range((k % 2) * 4, (k % 2) * 4 + 4)
        ins.append({"QT": np.stack([_T(seq[b][:, hh * 128:(hh + 1) * 128]) for hh in hs]),
                    "FT": np.stack([np.stack([_T(seq[b][:, 1024 + d * 1024 + hh * 128:1024 + d * 1024 + (hh + 1) * 128]) for d in range(2)]) for hh in hs]),
                    "V": np.stack([np.ascontiguousarray(seq[b][:, 3072 + hh * 128:3072 + (hh + 1) * 128]) for hh in hs]),
                    "SG": np.stack([np.ascontiguousarray(seq[b][:, 4096 + hh * 128:4096 + (hh + 1) * 128]) for hh in hs]),
                    "LBR": np.ascontiguousarray(np.stack([lbw[:, hh * 128:(hh + 1) * 128].T for hh in hs], 1)),
                    "NG": ngr, "IDN": idn, "TRI": tri})
    res = run(build_gla("hg", 4, 8448, LC_), ins)
    u = np.stack([np.concatenate([res[2 * b + hf]["U"][j] for hf in range(2) for j in range(4)], -1) for b in range(B_)])
    h, _ = post(3, _to_cores(u[:, LC_:], u[:, :LC_]), 1024, hg_w_out, h, True)

    lat, _ = _from_cores(h)
    return np.ascontiguousarray(lat.astype(np.float32))
```

```python
import numpy as np
from contextlib import ExitStack
import concourse.bass as bass
import concourse.mybir as mybir
from concourse.bass_utils import run_bass_kernel_spmd

F32 = mybir.dt.float32
BF16 = mybir.dt.bfloat16
AF = mybir.ActivationFunctionType
ALU = mybir.AluOpType
AX = mybir.AxisListType

NCORES = 8


class Buf:
    __slots__ = ("name", "t", "last_w", "readers")

    def __init__(self, name, t):
        self.name = name
        self.t = t
        self.last_w = None
        self.readers = []

    def __getitem__(self, idx):
        return self.t[idx]


class Op:
    __slots__ = ("eng", "fn", "deps", "signal", "token", "is_dma", "prewait", "is_cc")

    def __init__(self, eng, fn, is_dma):
        self.eng = eng
        self.fn = fn
        self.deps = []
        self.signal = is_dma
        self.token = None
        self.is_dma = is_dma
        self.prewait = None
        self.is_cc = False


ENGS = ("pe", "act", "dve", "pool", "sp")
NPOOL = {"sp": 48, "pool": 24, "act": 16}
CC_OUTSTANDING = 1


class Prog:
    def __init__(self):
        self.nc = nc = bass.Bass("TRN2", target_bir_lowering=False)
        self.gs = gs = ExitStack()
        self.esem = {e: gs.enter_context(nc.semaphore(f"s_{e}")) for e in ENGS if e != "sp"}
        self.dsem = {q: [gs.enter_context(nc.semaphore(f"d_{q}{i}")) for i in range(NPOOL[q])] for q in NPOOL}
        self.bsem = gs.enter_context(nc.semaphore("bar"))
        self.csem = gs.enter_context(nc.semaphore("cc"))
        self.ccnt = 0
        self.cnt = {e: 0 for e in ENGS}
        self.dcnt = {q: 0 for q in NPOOL}
        self.nbar = 0
        self.bt = {e: gs.enter_context(nc.sbuf_tensor(f"bt_{e}", [128, 8], F32)) for e in ("act", "dve", "pool")}
        self.btpe = gs.enter_context(nc.sbuf_tensor("bt_pe", [128, 8], BF16))
        self.bps = gs.enter_context(nc.psum_tensor("bps", [128, 8], F32))
        self.bdram = nc.dram_tensor("bar_d", [2, 16], F32, kind="Internal").ap()
        self.ss = ExitStack()
        self.ops = []
        self.n = 0

    def din(self, name, shape, dt):
        return self.nc.dram_tensor(name, list(shape), dt, kind="ExternalInput").ap()

    def dout(self, name, shape, dt):
        return self.nc.dram_tensor(name, list(shape), dt, kind="ExternalOutput").ap()

    def dscr(self, name, shape, dt):
        return self.nc.dram_tensor(name, list(shape), dt, kind="Internal").ap()

    def sb(self, name, shape, dt):
        self.n += 1
        t = self.ss.enter_context(self.nc.sbuf_tensor(f"{name}_{self.n}", list(shape), dt))
        return Buf(name, t)

    def ps(self, name, shape, dt=F32):
        self.n += 1
        t = self.ss.enter_context(self.nc.psum_tensor(f"{name}_{self.n}", list(shape), dt))
        return Buf(name, t)

    def op(self, eng, fn, reads=(), writes=(), is_dma=False):
        o = Op(eng, fn, is_dma)
        deps = []
        for b in reads:
            if b.last_w is not None:
                deps.append(b.last_w)
        for b in writes:
            w = b.last_w
            if w is not None and (is_dma or w.is_dma or w.eng != eng):
                deps.append(w)
            for r in b.readers:
                if is_dma or r.is_dma or r.eng != eng:
                    deps.append(r)
        for b in writes:
            b.last_w = o
            b.readers = []
        for b in reads:
            b.readers.append(o)
        seen = set()
        for d in deps:
            if id(d) not in seen and d is not o:
                seen.add(id(d))
                o.deps.append(d)
                d.signal = True
        self.ops.append(o)
        return o

    def dma(self, eng, out, in_, reads=(), writes=()):
        return self.op(eng, lambda e: e.dma_start(out=out, in_=in_), reads, writes, is_dma=True)

    def collective(self, kind, groups, in_ap, out_ap):
        o = self.op("pool", lambda e: e.collective_compute(kind, ALU.bypass, replica_groups=groups, ins=[in_ap], outs=[out_ap]),
                    (), (), is_dma=True)
        o.is_cc = True
        return o

    def dma_t(self, eng, out, in_, reads=(), writes=()):
        return self.op(eng, lambda e: e.dma_start_transpose(out=out, in_=in_), reads, writes, is_dma=True)

    def mm(self, out, lhsT, rhs, start, stop, reads, writes):
        return self.op("pe", lambda e: e.matmul(out, lhsT, rhs, start=start, stop=stop), reads, writes)

    def transpose(self, out, in_, ident, reads, writes):
        return self.op("pe", lambda e: e.transpose(out, in_, ident), reads, writes)

    def act(self, out, in_, func, reads, writes, bias=None, scale=None, accum_out=None, eng="act"):
        kw = {}
        if bias is not None:
            kw["bias"] = bias
        if scale is not None:
            kw["scale"] = scale
        if accum_out is not None:
            kw["accum_out"] = accum_out
        return self.op(eng, lambda e: e.activation(out, in_, func, **kw), reads, writes)

    def tt(self, out, in0, in1, op, reads, writes, eng="dve"):
        return self.op(eng, lambda e: e.tensor_tensor(out, in0, in1, op), reads, writes)

    def ts(self, out, in0, s1, s2, op0, op1, reads, writes, eng="dve", accum_out=None):
        if op1 is None:
            return self.op(eng, lambda e: e.tensor_scalar(out, in0, s1, None, op0), reads, writes)
        if accum_out is not None:
            return self.op(eng, lambda e: e.tensor_scalar(out, in0, s1, s2, op0, op1, accum_out=accum_out), reads, writes)
        return self.op(eng, lambda e: e.tensor_scalar(out, in0, s1, s2, op0, op1), reads, writes)

    def stt(self, out, in0, scalar, in1, op0, op1, reads, writes, eng="dve"):
        return self.op(eng, lambda e: e.scalar_tensor_tensor(out, in0, scalar, in1, op0, op1), reads, writes)

    def copy(self, out, in_, reads, writes, eng="dve"):
        return self.op(eng, lambda e: e.tensor_copy(out, in_), reads, writes)

    def memset(self, out, val, writes, eng="dve"):
        return self.op(eng, lambda e: e.memset(out, val), (), writes)

    def end_stage(self):
        nc = self.nc
        esem, dsem, cnt, dcnt = self.esem, self.dsem, self.cnt, self.dcnt
        for o in self.ops:
            if o.is_cc:
                self.ccnt += 1
                o.token = (self.csem, self.ccnt)
                if self.ccnt > CC_OUTSTANDING:
                    o.prewait = (self.csem, self.ccnt - CC_OUTSTANDING)
            elif o.is_dma:
                q = o.eng
                n = dcnt[q]
                dcnt[q] += 1
                slot = n % NPOOL[q]
                use = n // NPOOL[q]
                o.token = (dsem[q][slot], 16 * (use + 1))
                if use > 0:
                    o.prewait = (dsem[q][slot], 16 * use)
            elif o.signal:
                cnt[o.eng] += 1
                o.token = (esem[o.eng], cnt[o.eng])
        per = {e: [o for o in self.ops if o.eng == e] for e in ENGS}
        self.nbar += 1
        nbar = self.nbar
        drain = {}
        for q in NPOOL:
            drain[q] = []
            for slot in range(min(dcnt[q], NPOOL[q])):
                uses = (dcnt[q] - 1 - slot) // NPOOL[q] + 1
                drain[q].append((dsem[q][slot], 16 * uses))
        bsem = self.bsem

        def emit(eng_name, e):
            known = {}

            def wait(tok):
                s, v = tok
                if known.get(id(s), 0) >= v:
                    return
                known[id(s)] = v
                e.wait_ge(s, v)

            for o in per[eng_name]:
                if o.prewait is not None:
                    wait(o.prewait)
                for d in o.deps:
                    wait(d.token)
                inst = o.fn(e)
                if o.is_cc:
                    inst.then_inc(o.token[0], 1)
                elif o.is_dma:
                    inst.then_inc(o.token[0], 16)
                elif o.signal:
                    inst.then_inc(o.token[0], 1)
            for tok in drain.get(eng_name, ()):
                wait(tok)
            if eng_name == "pool" and self.ccnt:
                wait((self.csem, self.ccnt))
            if eng_name == "sp":
                e.dma_start(out=self.bdram[1:2, :], in_=self.bdram[0:1, :]).then_inc(bsem, 16)
            elif eng_name == "pe":
                e.matmul(self.bps[0:8, 0:8], self.btpe[:, 0:8], self.btpe[:, 0:8], start=True, stop=True).then_inc(bsem, 1)
            elif eng_name == "act":
                e.activation(self.bt["act"][:, 0:1], self.bt["act"][:, 1:2], AF.Copy).then_inc(bsem, 1)
            else:
                e.memset(self.bt[eng_name][:, 0:1], 0.0).then_inc(bsem, 1)
            e.wait_ge(bsem, 20 * nbar)

        with nc.Block() as block:
            @block.sync
            def _(e):
                emit("sp", e)

            @block.tensor
            def _(e):
                emit("pe", e)

            @block.scalar
            def _(e):
                emit("act", e)

            @block.vector
            def _(e):
                emit("dve", e)

            @block.gpsimd
            def _(e):
                emit("pool", e)
        self.ss.close()
        self.ss = ExitStack()
        self.ops = []

    def finish(self):
        if self.ops:
            self.end_stage()
        self.gs.close()
        return self.nc


def run(prog, in_maps):
    nc = prog.finish()
    res = run_bass_kernel_spmd(nc, in_maps, core_ids=list(range(NCORES)))
    return res.results


RMS_EPS = 1e-6
LN_EPS = 1e-5
ALPHA = 8 ** 0.25
T_ = 4224
TG_ = 8448
NCT_ = 1


def grow(c):
    if c < 2:
        return c * T_
    i = c - 2
    return (128 + i * 128) if i < 32 else (T_ + 128 + (i - 32) * 128)


def tok_groups(T):
    gs = []
    t = 0
    while t < T:
        g = min(512, T - t)
        gs.append((t, g))
        t += g
    return gs


def bc_load(P, tile, row_ap, n):
    P.dma("sp", tile[:, 0:n], row_ap.partition_broadcast(128), (), (tile,))


class Lin:
    def __init__(self, P, X, W, T, K, N, blocks=None, x_fm=False):
        self.P = P
        self.x_fm = x_fm
        self.finish_ep = None
        self.T, self.K, self.N = T, K, N
        self.KC = K // 128
        assert K % 128 == 0 and T % 128 == 0
        self.X, self.W = X, W
        self.blocks = blocks or [(n0, min(512, N - n0)) for n0 in range(0, N, 512)]
        self.pre_group = None
        self.blk_ep = None
        self.row_ep = None
        Wv = W.rearrange("(k p) n -> p k n", p=128)
        self.wt = []
        for k in range(self.KC):
            w = P.sb("w", [128, N], BF16)
            P.dma("pool", w[:], Wv[:, k, :], (), (w,))
            self.wt.append(w)

    def run(self):
        P = self.P
        KC = self.KC
        xb = [P.sb("xg", [128, KC, 512], BF16) for _ in range(2)]
        psb = [P.ps("ps", [128, 512]) for _ in range(4)]
        groups = tok_groups(self.T)

        def load(g):
            t0, gs = groups[g]
            if self.x_fm:
                P.dma("sp", xb[g % 2][:, :, 0:gs], self.X.rearrange("(k p) t -> p k t", p=128)[:, :, t0:t0 + gs], (), (xb[g % 2],))
            else:
                for k in range(KC):
                    P.dma_t("sp", xb[g % 2][:, k, 0:gs], self.X[t0:t0 + gs, k * 128:(k + 1) * 128], (), (xb[g % 2],))
            if self.pre_group:
                self.pre_group(g, t0, gs)

        load(0)
        ctr = 0
        for g, (t0, gs) in enumerate(groups):
            if g + 1 < len(groups):
                load(g + 1)
            xg = xb[g % 2]
            for j in range(gs // 128):
                tt = t0 // 128 + j
                for nb, (n0, ns) in enumerate(self.blocks):
                    ps = psb[ctr % 4]
                    ctr += 1
                    for k in range(KC):
                        P.mm(ps[:, 0:ns], xg[:, k, j * 128:(j + 1) * 128], self.wt[k][:, n0:n0 + ns],
                             k == 0, k == KC - 1, (xg, self.wt[k]), (ps,))
                    self.blk_ep(tt, j, nb, n0, ns, ps)
                if self.row_ep:
                    self.row_ep(tt, j, t0 + j * 128)
        if self.finish_ep:
            self.finish_ep()


def evac(P, i, out, in_, reads, writes, func=None, scale=None):
    if func is not None:
        return P.act(out, in_, func, reads, writes, scale=scale)
    if i % 2 == 0:
        return P.act(out, in_, AF.Copy, reads, writes)
    return P.copy(out, in_, reads, writes)


def bcast_mid(ap, n):
    return ap.unsqueeze(1).broadcast_to([ap.shape[0], n, ap.shape[1]])


def rope_loads(P, cs, sn, COS, SIN, g, t0, gs):
    nj = gs // 128
    P.dma("sp", cs[g % 2][:, 0:nj, :], COS[t0:t0 + gs, :].rearrange("(j p) f -> p j f", p=128), (), (cs[g % 2],))
    P.dma("sp", sn[g % 2][:, 0:nj, :], SIN[t0:t0 + gs, :].rearrange("(j p) f -> p j f", p=128), (), (sn[g % 2],))


def stage_plain(P, X, W, Y, T, K, N, silu_blocks=()):
    L = Lin(P, X, W, T, K, N)
    ob = [P.sb("ob", [128, N], BF16) for _ in range(2)]

    def blk_ep(tt, j, nb, n0, ns, ps):
        o = ob[tt % 2]
        evac(P, nb, o[:, n0:n0 + ns], ps[:, 0:ns], (ps,), (o,), func=AF.Silu if nb in silu_blocks else None)

    def row_ep(tt, j, t0):
        o = ob[tt % 2]
        P.dma("pool", Y[t0:t0 + 128, :], o[:], (o,), ())

    L.blk_ep, L.row_ep = blk_ep, row_ep
    L.run()
    P.end_stage()


def stage_ret_a(P, X, W, Y, COS, SIN, T):
    L = Lin(P, X, W, T, 1024, 6144)
    ob = [P.sb("ob", [128, 6144], BF16) for _ in range(2)]
    rb = [P.sb("rb", [128, 2048], F32) for _ in range(2)]
    cs = [P.sb("cs", [128, 4, 128], F32) for _ in range(2)]
    sn = [P.sb("sn", [128, 4, 128], F32) for _ in range(2)]
    t1 = P.sb("t1", [128, 8, 128], F32)
    t2 = P.sb("t2", [128, 8, 128], F32)
    t3 = P.sb("t3", [128, 8, 128], F32)
    t4 = P.sb("t4", [128, 8, 128], F32)

    def blk_ep(tt, j, nb, n0, ns, ps):
        if nb < 4:
            r = rb[tt % 2]
            evac(P, nb, r[:, n0:n0 + ns], ps[:, 0:ns], (ps,), (r,))
        else:
            o = ob[tt % 2]
            evac(P, nb, o[:, n0:n0 + ns], ps[:, 0:ns], (ps,), (o,), func=AF.Silu if nb >= 8 else None)

    def row_ep(tt, j, t0):
        g = (t0 // 512)
        r = rb[tt % 2]
        o = ob[tt % 2]
        rv = r[:, :].rearrange("p (h two d) -> p h two d", h=8, two=2)
        ov = o[:, 0:2048].rearrange("p (h two d) -> p h two d", h=8, two=2)
        c = bcast_mid(cs[g % 2][:, j, :], 8)
        s = bcast_mid(sn[g % 2][:, j, :], 8)
        x1, x2 = rv[:, :, 0, :], rv[:, :, 1, :]
        P.tt(t1[:], x1, c, ALU.mult, (r, cs[g % 2]), (t1,))
        P.tt(t2[:], x2, s, ALU.mult, (r, sn[g % 2]), (t2,))
        P.tt(ov[:, :, 0, :], t1[:], t2[:], ALU.subtract, (t1, t2), (o,))
        P.tt(t3[:], x1, s, ALU.mult, (r, sn[g % 2]), (t3,), eng="pool")
        P.tt(t4[:], x2, c, ALU.mult, (r, cs[g % 2]), (t4,), eng="pool")
        P.tt(ov[:, :, 1, :], t3[:], t4[:], ALU.add, (t3, t4), (o,), eng="pool")
        P.dma("pool", Y[t0:t0 + 128, :], o[:], (o,), ())

    L.pre_group = lambda g, t0, gs: rope_loads(P, cs, sn, COS, SIN, g, t0, gs)
    L.blk_ep, L.row_ep = blk_ep, row_ep
    L.run()
    P.end_stage()


def stage_mla_a1(P, X, W, Y, YKR, COS, SIN, GN, T):
    L = Lin(P, X, W, T, 1024, 800)
    gn = P.sb("gn", [128, 768], F32)
    P.dma("sp", gn[:], GN, (), (gn,))
    ob = [P.sb("ob", [128, 800], BF16) for _ in range(2)]
    rb = [P.sb("rb", [128, 800], F32) for _ in range(2)]
    cs = [P.sb("cs", [128, 4, 16], F32) for _ in range(2)]
    sn = [P.sb("sn", [128, 4, 16], F32) for _ in range(2)]
    junk = P.sb("junk", [128, 512], F32)
    epsb = P.sb("epsb", [128, 1], F32)
    P.memset(epsb[:], RMS_EPS, (epsb,))
    st = [P.sb("st", [128, 4], F32) for _ in range(2)]
    tm = [P.sb("tm", [128, 4, 16], F32) for _ in range(2)]
    okr = [P.sb("okr", [128, 128], BF16) for _ in range(2)]
    for o_ in okr:
        P.memset(o_[:], 0.0, (o_,), eng="pool")

    def blk_ep(tt, j, nb, n0, ns, ps):
        r = rb[tt % 2]
        evac(P, nb + 1, r[:, n0:n0 + ns], ps[:, 0:ns], (ps,), (r,))

    def row_ep(tt, j, t0):
        g = t0 // 512
        r, o, s, t = rb[tt % 2], ob[tt % 2], st[tt % 2], tm[tt % 2]
        for idx, (c0, cn) in enumerate(((0, 512), (512, 256))):
            P.act(junk[:, 0:cn], r[:, c0:c0 + cn], AF.Square, (r,), (junk, s), accum_out=s[:, idx:idx + 1])
            P.act(s[:, 2 + idx:3 + idx], s[:, idx:idx + 1], AF.Sqrt, (s, epsb), (s,), bias=epsb[:, 0:1], scale=1.0 / cn)
            P.op("dve", lambda e, idx=idx: e.reciprocal(s[:, 2 + idx:3 + idx], s[:, 2 + idx:3 + idx]), (s,), (s,))
            P.stt(o[:, c0:c0 + cn], r[:, c0:c0 + cn], s[:, 2 + idx:3 + idx], gn[:, c0:c0 + cn], ALU.mult, ALU.mult,
                  (r, s, gn), (o,))
        c, sn_ = cs[g % 2][:, j, :], sn[g % 2][:, j, :]
        x1, x2 = r[:, 768:784], r[:, 784:800]
        P.tt(t[:, 0, :], x1, c, ALU.mult, (r, cs[g % 2]), (t,))
        P.tt(t[:, 1, :], x2, sn_, ALU.mult, (r, sn[g % 2]), (t,))
        P.tt(o[:, 768:784], t[:, 0, :], t[:, 1, :], ALU.subtract, (t,), (o,))
        P.tt(t[:, 2, :], x1, sn_, ALU.mult, (r, sn[g % 2]), (t,))
        P.tt(t[:, 3, :], x2, c, ALU.mult, (r, cs[g % 2]), (t,))
        P.tt(o[:, 784:800], t[:, 2, :], t[:, 3, :], ALU.add, (t,), (o,))
        P.dma("pool", Y[t0:t0 + 128, :], o[:], (o,), ())
        P.copy(okr[tt % 2][:, 64:96], o[:, 768:800], (o,), (okr[tt % 2],), eng="pool")
        P.dma("pool", YKR[t0:t0 + 128, :], okr[tt % 2][:], (okr[tt % 2],), ())

    L.pre_group = lambda g, t0, gs: rope_loads(P, cs, sn, COS, SIN, g, t0, gs)
    L.blk_ep, L.row_ep = blk_ep, row_ep
    L.run()
    P.end_stage()


def stage_mla_a2q(P, X, W, Y, COS, SIN, T):
    SC = 96 ** -0.5
    L = Lin(P, X, W, T, 512, 1536)
    ob = [P.sb("ob", [128, 2048], BF16) for _ in range(2)]
    for o_ in ob:
        P.memset(o_[:], 0.0, (o_,), eng="pool")
    rb = [P.sb("rb", [128, 1536], F32) for _ in range(2)]
    cs = [P.sb("cs", [128, 4, 16], F32) for _ in range(2)]
    sn = [P.sb("sn", [128, 4, 16], F32) for _ in range(2)]
    tm = [P.sb("tm", [128, 4, 16, 16], F32) for _ in range(2)]

    def blk_ep(tt, j, nb, n0, ns, ps):
        r = rb[tt % 2]
        evac(P, nb, r[:, n0:n0 + ns], ps[:, 0:ns], (ps,), (r,))

    def row_ep(tt, j, t0):
        g = t0 // 512
        r, o, t = rb[tt % 2], ob[tt % 2], tm[tt % 2]
        rv = r[:, :].rearrange("p (h d) -> p h d", h=16)
        ov = o[:, :].rearrange("p (h d) -> p h d", h=16)
        P.act(ov[:, :, 0:64], rv[:, :, 0:64], AF.Copy, (r,), (o,), scale=SC)
        c = bcast_mid(cs[g % 2][:, j, :], 16)
        s = bcast_mid(sn[g % 2][:, j, :], 16)
        x1, x2 = rv[:, :, 64:80], rv[:, :, 80:96]
        P.tt(t[:, 0], x1, c, ALU.mult, (r, cs[g % 2]), (t,))
        P.tt(t[:, 1], x2, s, ALU.mult, (r, sn[g % 2]), (t,))
        P.tt(ov[:, :, 64:80], t[:, 0], t[:, 1], ALU.subtract, (t,), (o,))
        P.tt(t[:, 2], x1, s, ALU.mult, (r, sn[g % 2]), (t,))
        P.tt(t[:, 3], x2, c, ALU.mult, (r, cs[g % 2]), (t,))
        P.tt(ov[:, :, 80:96], t[:, 2], t[:, 3], ALU.add, (t,), (o,))
        P.dma("pool", Y[t0:t0 + 128, :], o[:], (o,), ())

    L.pre_group = lambda g, t0, gs: rope_loads(P, cs, sn, COS, SIN, g, t0, gs)
    L.blk_ep, L.row_ep = blk_ep, row_ep
    L.run()
    P.end_stage()


def stage_swiglu(P, X, W, YT, T):
    F = 2816
    FC = F // 128
    Wv = W.rearrange("(k p) n -> p k n", p=128)
    wt = []
    for k in range(8):
        w = P.sb("w", [128, 2 * F], BF16)
        P.dma("pool", w[:], Wv[:, k, :], (), (w,))
        wt.append(w)
    xb = [P.sb("xg", [128, 8, 512], BF16) for _ in range(2)]
    psg = [P.ps("psg", [128, 512]) for _ in range(2)]
    psu = [P.ps("psu", [128, 512]) for _ in range(2)]
    sg = [P.sb("sg", [128, 512], F32) for _ in range(2)]
    ob = [P.sb("ob", [128, 512], BF16) for _ in range(3)]
    groups = tok_groups(T)

    def load(g):
        t0, gs = groups[g]
        for k in range(8):
            P.dma_t("sp", xb[g % 2][:, k, 0:gs], X[t0:t0 + gs, k * 128:(k + 1) * 128], (), (xb[g % 2],))

    load(0)
    n = 0
    for g, (t0, gs) in enumerate(groups):
        if g + 1 < len(groups):
            load(g + 1)
        xg = xb[g % 2]
        for fc in range(FC):
            pg, pu, s_, o = psg[n % 2], psu[n % 2], sg[n % 2], ob[n % 3]
            for k in range(8):
                P.mm(pg[:, 0:gs], wt[k][:, fc * 128:(fc + 1) * 128], xg[:, k, 0:gs], k == 0, k == 7, (wt[k], xg), (pg,))
            for k in range(8):
                P.mm(pu[:, 0:gs], wt[k][:, F + fc * 128:F + (fc + 1) * 128], xg[:, k, 0:gs], k == 0, k == 7, (wt[k], xg), (pu,))
            P.act(s_[:, 0:gs], pg[:, 0:gs], AF.Silu, (pg,), (s_,))
            P.tt(o[:, 0:gs], pu[:, 0:gs], s_[:, 0:gs], ALU.mult, (pu, s_), (o,))
            P.dma("pool", YT[fc * 128:(fc + 1) * 128, t0:t0 + gs], o[:, 0:gs], (o,), ())
            n += 1
    P.end_stage()


def stage_resid_ln(P, X, W, H, HO, AO, vrows, T, K, x_fm=False):
    L = Lin(P, X, W, T, K, 1024, x_fm=x_fm)
    want_a = AO is not None
    vec = [P.sb("vec", [128, 1024], F32) for _ in range(8 if want_a else 4)]
    for i in range(len(vec)):
        bc_load(P, vec[i], vrows[i], 1024)
    if want_a:
        for i in (4, 6):
            P.ts(vec[i][:], vec[i][:], 1.0, None, ALU.add, None, (vec[i],), (vec[i],))
    hb = [P.sb("hb", [128, 4, 1024], F32) for _ in range(2)]
    zb = [P.sb("zb", [128, 1024], F32) for _ in range(3)]
    ho = [P.sb("ho", [128, 1024], F32) for _ in range(3)]
    ao = [P.sb("ao", [128, 1024], BF16) for _ in range(2)]
    at = [P.sb("at", [128, 1024], F32) for _ in range(2)]
    st = [P.sb("st", [128, 2, 6], F32) for _ in range(3)]
    mv = [P.sb("mv", [128, 4], F32) for _ in range(3)]
    epsb = P.sb("epsb", [128, 1], F32)
    P.memset(epsb[:], LN_EPS, (epsb,))

    def pre_group(g, t0, gs):
        nj = gs // 128
        P.dma("sp", hb[g % 2][:, 0:nj, :], H[t0:t0 + gs, :].rearrange("(j p) f -> p j f", p=128), (), (hb[g % 2],))

    def blk_ep(tt, j, nb, n0, ns, ps):
        z = zb[tt % 3]
        gate = vec[1] if tt < NCT_ else vec[0]
        P.tt(z[:, n0:n0 + ns], ps[:, 0:ns], gate[:, n0:n0 + ns], ALU.mult, (ps, gate), (z,))

    def phase_b(tt, t0):
        z, o, m = zb[tt % 3], ho[tt % 3], mv[tt % 3]
        P.op("dve", lambda e: e.reciprocal(m[:, 2:3], m[:, 2:3]), (m,), (m,))
        P.ts(m[:, 3:4], m[:, 0:1], -1.0, m[:, 2:3], ALU.mult, ALU.mult, (m,), (m,))
        P.act(z[:], z[:], AF.Identity, (z, m), (z,), bias=m[:, 3:4], scale=m[:, 2:3])
        P.tt(z[:], z[:], vec[2][:], ALU.mult, (z, vec[2]), (z,), eng="pool")
        P.tt(o[:], z[:], vec[3][:], ALU.add, (z, vec[3]), (o,), eng="pool")
        P.dma("pool", HO[t0:t0 + 128, :], o[:], (o,), ())

    def phase_c(tt, t0):
        if not want_a:
            return
        o, a = ho[tt % 3], ao[tt % 2]
        sc, sh = (vec[4], vec[5]) if tt >= NCT_ else (vec[6], vec[7])
        a_t = at[tt % 2]
        P.tt(a_t[:], o[:], sc[:], ALU.mult, (o, sc), (a_t,))
        P.tt(a[:], a_t[:], sh[:], ALU.add, (a_t, sh), (a,))
        P.dma("pool", AO[t0:t0 + 128, :], a[:], (a,), ())

    pend = []

    def row_ep(tt, j, t0):
        g = t0 // 512
        z, h, s, m = zb[tt % 3], hb[g % 2], st[tt % 3], mv[tt % 3]
        P.stt(z[:], h[:, j, :], ALPHA, z[:], ALU.mult, ALU.add, (h, z), (z,))
        for c in range(2):
            P.op("dve", lambda e, c=c: e.bn_stats(s[:, c, :], z[:, c * 512:(c + 1) * 512]), (z,), (s,))
        P.op("dve", lambda e: e.bn_aggr(m[:, 0:2], s[:, :, :].rearrange("p a b -> p (a b)")), (s,), (m,))
        P.act(m[:, 2:3], m[:, 1:2], AF.Sqrt, (m, epsb), (m,), bias=epsb[:, 0:1], scale=1.0)
        pend.append((tt, t0))
        if len(pend) >= 2:
            phase_b(*pend[-2])
        if len(pend) >= 3:
            phase_c(*pend[-3])

    def finish_ep():
        n = len(pend)
        phase_b(*pend[-1])
        if n >= 2:
            phase_c(*pend[-2])
        phase_c(*pend[-1])

    L.pre_group, L.blk_ep, L.row_ep, L.finish_ep = pre_group, blk_ep, row_ep, finish_ep
    L.run()
    P.end_stage()


def stage_mod0(P, X, vrows, AO, T):
    vec = [P.sb("vec", [128, 1024], F32) for _ in range(4)]
    for i in range(4):
        bc_load(P, vec[i], vrows[i], 1024)
    for i in (0, 2):
        P.ts(vec[i][:], vec[i][:], 1.0, None, ALU.add, None, (vec[i],), (vec[i],))
    xb = [P.sb("xb", [128, 1024], F32) for _ in range(3)]
    at = [P.sb("at", [128, 1024], F32) for _ in range(2)]
    ao = [P.sb("ao", [128, 1024], BF16) for _ in range(2)]
    for tt in range(T // 128):
        x, a_t, a = xb[tt % 3], at[tt % 2], ao[tt % 2]
        sc, sh = (vec[0], vec[1]) if tt >= NCT_ else (vec[2], vec[3])
        P.dma("sp", x[:], X[tt * 128:(tt + 1) * 128, :], (), (x,))
        eng = "dve" if tt % 2 == 0 else "pool"
        P.tt(a_t[:], x[:], sc[:], ALU.mult, (x, sc), (a_t,), eng=eng)
        P.tt(a[:], a_t[:], sh[:], ALU.add, (a_t, sh), (a,), eng=eng)
        P.dma("pool", AO[tt * 128:(tt + 1) * 128, :], a[:], (a,), ())
    P.end_stage()


def stage_k0(P, condT, ADA_W, ADA_B, MOD):
    ct = P.sb("ct", [128, 8, 2], F32)
    cs = P.sb("cs", [128, 8, 2], F32)
    ones = P.sb("ones", [1, 2], F32)
    P.dma("sp", ct[:], condT, (), (ct,))
    P.memset(ones[:], 1.0, (ones,))
    P.act(cs[:], ct[:], AF.Silu, (ct,), (cs,))
    wb = [P.sb("w", [128, 8, 512], F32) for _ in range(3)]
    bb = [P.sb("b", [1, 512], F32) for _ in range(3)]
    ob = [P.sb("o", [2, 512], F32) for _ in range(3)]
    pss = [P.ps("ps", [2, 512]) for _ in range(2)]
    n = 0
    for i in range(4):
        Wv = ADA_W[i].rearrange("(k p) n -> p k n", p=128)
        for nb in range(12):
            w, b, o, ps = wb[n % 3], bb[n % 3], ob[n % 3], pss[n % 2]
            sl = slice(nb * 512, (nb + 1) * 512)
            P.dma("sp", w[:], Wv[:, :, sl], (), (w,))
            P.dma("sp", b[:], ADA_B[i:i + 1, sl], (), (b,))
            for k in range(8):
                P.mm(ps[:], cs[:, k, :], w[:, k, :], k == 0, False, (cs, w), (ps,))
            P.mm(ps[:], ones[:], b[:], False, True, (ones, b), (ps,))
            P.copy(o[:], ps[:], (ps,), (o,))
            P.dma("pool", MOD[i, :, sl], o[:], (o,), ())
            n += 1
    P.end_stage()


RCH = 1408


def stage_mla_b(P, YQ, YKV, YKR, IDF, O, NHEADS=16, DQK=96, DV=64):
    NK = TG_
    NKT = NK // 128
    NQ = T_
    qb = [P.sb("q", [128, NQ], BF16) for _ in range(2)]
    kb = [P.sb("k", [128, NK], BF16) for _ in range(2)]
    krt = P.sb("krt", [128, NK], BF16)
    vb = [P.sb("v", [128, NKT, DV + 1], BF16) for _ in range(2)]
    idf = P.sb("idf", [128, 128], F32)
    P.dma("sp", idf[:], IDF, (), (idf,))
    pss = [P.ps("pss", [128, 1024]) for _ in range(2)]
    pso = [P.ps("pso", [128, 512]) for _ in range(2)]
    pst = P.ps("pst", [128, 4, DV + 1])
    pt = [P.sb("pt", [128, 1024], BF16) for _ in range(2)]
    osb = [P.sb("osb", [DV + 1, 512], F32) for _ in range(2)]
    rc = [P.sb("rc", [128, 4, 1], F32) for _ in range(2)]
    ob = [P.sb("ob", [128, 4, DV], BF16) for _ in range(2)]
    for r0 in range(0, NK, RCH):
        P.dma_t("sp", krt[:, r0:r0 + RCH], YKR[r0:r0 + RCH, :], (), (krt,))

    def load(h):
        q, k, v = qb[h % 2], kb[h % 2], vb[h % 2]
        for r0 in range(0, NQ, RCH):
            P.dma_t("sp", q[:, r0:r0 + RCH], YQ[r0:r0 + RCH, h * 128:(h + 1) * 128], (), (q,))
        for r0 in range(0, NK, RCH):
            P.dma_t("sp", k[:, r0:r0 + RCH], YKV[r0:r0 + RCH, h * 128:(h + 1) * 128], (), (k,))
        P.copy(k[64:96, :], krt[64:96, :], (krt, k), (k,), eng="pool")
        P.dma("sp", v[:, :, 0:DV], YKV[:, h * 128 + 64:(h + 1) * 128].rearrange("(t p) d -> p t d", p=128), (), (v,))
        P.memset(v[:, :, DV:DV + 1], 1.0, (v,), eng="pool")

    ctx_tiles = [0, T_ // 128]
    all_tiles = list(range(NKT))
    nlat = (NQ - 128) // 512
    qgroups = [([(0, 128)], ctx_tiles)] + [([(128 + (2 * g + b2) * 512, 512) for b2 in range(2)], all_tiles) for g in range(nlat // 2)]
    load(0)
    ctr = 0
    bi = 0
    for h in range(NHEADS):
        if h + 1 < NHEADS:
            load(h + 1)
        q, k, v = qb[h % 2], kb[h % 2], vb[h % 2]
        for (blocks, ktl) in qgroups:
            nb = len(blocks)
            qn = blocks[0][1]
            nkt = len(ktl)

            def S(i, c):
                kt = ktl[i]
                for b2, (q0, _) in enumerate(blocks):
                    P.mm(pss[c % 2][:, b2 * 512:b2 * 512 + qn], k[0:DQK, kt * 128:(kt + 1) * 128], q[0:DQK, q0:q0 + qn], True, True,
                         (k, q), (pss[c % 2],))

            S(0, ctr)
            for i in range(nkt):
                kt = ktl[i]
                c = ctr + i
                if i + 1 < nkt:
                    S(i + 1, c + 1)
                sv = pss[c % 2][:, :].rearrange("p (a b) -> p a b", a=2)[:, 0:nb, 0:qn]
                pv = pt[c % 2][:, :].rearrange("p (a b) -> p a b", a=2)[:, 0:nb, 0:qn]
                P.act(pv, sv, AF.Exp, (pss[c % 2],), (pt[c % 2],))
                for b2 in range(nb):
                    P.mm(pso[b2][0:DV + 1, 0:qn], v[:, kt, :], pt[c % 2][:, b2 * 512:b2 * 512 + qn], i == 0, i == nkt - 1,
                         (pt[c % 2], v), (pso[b2],))
            ctr += nkt
            nqi = qn // 128
            for b2, (q0, _) in enumerate(blocks):
                po = pso[b2]
                r, o, os_ = rc[bi % 2], ob[bi % 2], osb[bi % 2]
                P.copy(os_[:, 0:qn], po[0:DV + 1, 0:qn], (po,), (os_,))
                for qi in range(nqi):
                    P.transpose(pst[:, qi, :], os_[:, qi * 128:(qi + 1) * 128], idf[0:DV + 1, 0:DV + 1], (os_, idf), (pst,))
                P.op("dve", lambda e, r=r, nqi=nqi: e.reciprocal(r[:, 0:nqi, :], pst[:, 0:nqi, DV:DV + 1]), (pst,), (r,))
                P.tt(o[:, 0:nqi, :], pst[:, 0:nqi, 0:DV], r[:, 0:nqi, :].broadcast_to([128, nqi, DV]), ALU.mult, (pst, r), (o,))
                P.dma("sp", O[q0:q0 + qn, h * DV:(h + 1) * DV].rearrange("(j p) d -> p j d", p=128), o[:, 0:nqi, :], (o,), ())
                bi += 1
    P.end_stage()


def stage_na_b(P, Y, MI, MB, MKI, MKB, O, NHEADS=8, NLT=64, DH=64):
    NTOK = TG_
    NT = NTOK // 128
    HW = NHEADS * DH
    SC = DH ** -0.5
    gt = lambda c: grow(c) // 128
    qb = [P.sb("q", [128, NTOK], BF16) for _ in range(2)]
    kb = [P.sb("k", [128, NTOK], BF16) for _ in range(2)]
    vb = [P.sb("v", [128, NT, DH + 1], BF16) for _ in range(2)]
    mi = [P.sb("mi", [128, 5, 128], F32) for _ in range(2)]
    mb = [P.sb("mb", [128, 7, 128], F32) for _ in range(2)]
    mki = P.sb("mki", [128, 5, 128], F32)
    mkb = P.sb("mkb", [128, 7, 128], F32)
    P.dma("sp", mki[:], MKI, (), (mki,))
    P.dma("sp", mkb[:], MKB, (), (mkb,))
    psa = [P.ps("psa", [128, 4, 128]) for _ in range(2)]
    psb = [P.ps("psb", [128, 4, 128]) for _ in range(2)]
    pso = [P.ps("pso", [128, 512]) for _ in range(2)]
    sa = [P.sb("sa", [128, 5, 128], F32) for _ in range(2)]
    pt = [P.sb("pt", [128, 8, 128], BF16) for _ in range(2)]
    rc = [P.sb("rc", [128, 1], F32) for _ in range(2)]
    ob = [P.sb("ob", [128, DH], BF16) for _ in range(2)]

    def load(h):
        v = vb[h % 2]
        if h % 2 == 0:
            hp = h // 2
            q, k = qb[hp % 2], kb[hp % 2]
            for r0 in range(0, NTOK, RCH):
                P.dma_t("sp", q[:, r0:r0 + RCH], Y[r0:r0 + RCH, hp * 128:(hp + 1) * 128], (), (q,))
                P.dma_t("sp", k[:, r0:r0 + RCH], Y[r0:r0 + RCH, HW + hp * 128:HW + (hp + 1) * 128], (), (k,))
        P.dma("sp", v[:, :, 0:DH], Y[:, 2 * HW + h * DH:2 * HW + (h + 1) * DH].rearrange("(t p) d -> p t d", p=128), (), (v,))
        P.memset(v[:, :, DH:DH + 1], 1.0, (v,), eng="pool")
        P.dma("sp", mi[h % 2][:], MI[h], (), (mi[h % 2],))
        P.dma("sp", mb[h % 2][:], MB[h], (), (mb[h % 2],))
        P.tt(mi[h % 2][:], mi[h % 2][:], mki[:], ALU.add, (mi[h % 2], mki), (mi[h % 2],), eng="pool")
        P.tt(mb[h % 2][:], mb[h % 2][:], mkb[:], ALU.add, (mb[h % 2], mkb), (mb[h % 2],), eng="pool")

    load(0)
    if NHEADS > 1:
        load(1)
    ctx = [gt(0), gt(1)]
    units = []
    for h in range(NHEADS):
        for fq in range(NLT + 2):
            units.append((h, fq))
    pend = []

    def prep(h, fq, u):
        p0 = (h % 2) * DH
        q, k, v = qb[(h // 2) % 2], kb[(h // 2) % 2], vb[h % 2]
        gq = gt(fq)
        if fq >= 2:
            qt = fq - 2
            if 2 <= qt <= NLT - 3:
                lk = list(range(qt - 2, qt + 3))
                mask = mi[h % 2]
                m0 = 0
            else:
                k0 = 0 if qt < 2 else NLT - 4
                lk = list(range(k0, k0 + 4))
                mask = mb[h % 2]
                m0 = (k0 - qt) + 3
            kts = [gt(x + 2) for x in lk]
        else:
            kts, mask, m0 = [], None, 0
        pa, pb, s, p_ = psa[u % 2], psb[u % 2], sa[u % 2], pt[u % 2]
        qs = q[p0:p0 + DH, gq * 128:(gq + 1) * 128]
        nl = len(kts)
        for i, kt in enumerate(kts[:4]):
            P.mm(pa[:, i, :], k[p0:p0 + DH, kt * 128:(kt + 1) * 128], qs, True, True, (k, q), (pa,))
        rest = kts[4:] + ctx
        for i, kt in enumerate(rest):
            P.mm(pb[:, i, :], k[p0:p0 + DH, kt * 128:(kt + 1) * 128], qs, True, True, (k, q), (pb,))
        n4 = min(nl, 4)
        if nl:
            P.stt(s[:, 0:n4, :], pa[:, 0:n4, :], SC, mask[:, m0:m0 + n4, :], ALU.mult, ALU.add, (pa, mask), (s,))
            if nl > 4:
                P.stt(s[:, 4:5, :], pb[:, 0:1, :], SC, mask[:, m0 + 4:m0 + 5, :], ALU.mult, ALU.add, (pb, mask), (s,))
            P.act(p_[:, 0:nl, :], s[:, 0:nl, :], AF.Exp, (s,), (p_,))
        nr = len(rest) - (nl - n4)
        P.act(p_[:, nl:nl + nr, :], pb[:, nl - n4:nl - n4 + nr, :], AF.Exp, (pb,), (p_,), scale=SC)
        return (h, gq, kts + ctx, u)

    def fin(h, gq, allk, u):
        v = vb[h % 2]
        p_, r, o, po = pt[u % 2], rc[u % 2], ob[u % 2], pso[u % 2]
        for i, kt in enumerate(allk):
            P.mm(po[:, 0:DH + 1], p_[:, i, :], v[:, kt, :], i == 0, i == len(allk) - 1, (p_, v), (po,))
        P.op("dve", lambda e, r=r, po=po: e.reciprocal(r[:], po[:, DH:DH + 1]), (po,), (r,))
        P.ts(o[:], po[:, 0:DH], r[:, 0:1], None, ALU.mult, None, (po, r), (o,))
        P.dma("pool", O[gq * 128:(gq + 1) * 128, h * DH:(h + 1) * DH], o[:], (o,), ())

    nxt = prep(units[0][0], units[0][1], 0)
    for u, (h, fq) in enumerate(units):
        cur = nxt
        if u + 1 < len(units):
            h2, fq2 = units[u + 1]
            nxt = prep(h2, fq2, u + 1)
        fin(*cur)
        if (u + 1 == len(units) or units[u + 1][0] != h) and h + 2 < NHEADS:
            load(h + 2)
    P.end_stage()


def na_mask_tables(rpb):
    krl = (np.arange(128) // 64)[:, None, None]
    kc = (np.arange(128) % 64)[:, None, None]
    qrl = (np.arange(128) // 64)[None, None, :]
    qc = (np.arange(128) % 64)[None, None, :]
    ws = np.clip(qc - 8, 0, 48)
    col_ok = (kc >= ws) & (kc < ws + 16)
    ci = np.clip(kc - qc + 15, 0, 30)

    def tab(ds, row_window):
        d = np.asarray(ds)[None, :, None]
        rel = 2 * d + krl - qrl
        ok = col_ok & np.ones_like(rel, bool)
        if row_window:
            ok = ok & (rel >= -4) & (rel <= 3)
        ri = np.clip(rel + 7, 0, 14)
        cib = np.broadcast_to(ci, rel.shape)
        g = rpb[:, ri, cib]
        mk = np.where(ok, 0.0, -1e30).astype(np.float32)
        return np.ascontiguousarray(g.astype(np.float32)), np.ascontiguousarray(mk)

    MI, MKI = tab([-2, -1, 0, 1, 2], True)
    MB, MKB = tab([-3, -2, -1, 0, 1, 2, 3], False)
    return MI, MB, MKI, MKB


def stage_gla(P, kind, Y, U, OF, PAR, NGsrc, IDN, TRI):
    ret = kind == "ret"
    T = TG_
    NI = 2 if ret else 4
    DKC = 2 if ret else 1
    DK = 128 * DKC
    DV = 512 if ret else 128
    CQ = 1.0 if ret else 128 ** -0.5
    CK = 1.0 / 16 if ret else 1.0
    C = 128
    NCH = T // C
    scr = [[Buf("scr", None) for _ in range(NCH)] for _ in range(NI)]

    idn = P.sb("idn", [128, 128], BF16)
    tri = [P.sb("tri", [128, 128], F32) for _ in range(2)]
    P.dma("sp", idn[:], IDN, (), (idn,))
    for d in range(2):
        P.dma("sp", tri[d][:], TRI[d], (), (tri[d],))
    ones = P.sb("ones", [128, C], F32)
    P.memset(ones[:], 1.0, (ones,))
    epsb = P.sb("epsb", [128, 1], F32)
    P.memset(epsb[:], RMS_EPS, (epsb,))
    oneb = P.sb("oneb", [128, 1], F32)
    P.memset(oneb[:], 1.0, (oneb,))
    if ret:
        dec = P.sb("dec", [128, 4], F32)
        P.dma("sp", dec[:], PAR, (), (dec,))
        P.act(dec[:], dec[:], AF.Exp, (dec,), (dec,))
        P.ts(dec[:], dec[:], -1.0, None, ALU.mult, None, (dec,), (dec,))
    else:
        lbr = P.sb("lbr", [128, NI, 4], F32)
        lb = P.sb("lb", [128, NI, 4], F32)
        ng = P.sb("ng", [128, DV], F32)
        P.dma("sp", lbr[:], PAR, (), (lbr,))
        P.dma("sp", ng[:], NGsrc, (), (ng,))
        P.act(lbr[:], lbr[:], AF.Exp, (lbr,), (lbr,))
        for it in range(NI):
            P.op("dve", lambda e, it=it: e.reduce_sum(lb[:, it, 2:3], lbr[:, it, :], axis=AX.X), (lbr,), (lb,))
            P.op("dve", lambda e, it=it: e.reciprocal(lb[:, it, 3:4], lb[:, it, 2:3]), (lb,), (lb,))
            P.tt(lb[:, it, 1:2], lbr[:, it, 0:1], lb[:, it, 3:4], ALU.mult, (lbr, lb), (lb,))
            P.ts(lb[:, it, 0:1], lb[:, it, 1:2], -1.0, 1.0, ALU.mult, ALU.add, (lb,), (lb,))

    qg = [P.sb("qg", [128, DKC, 512], BF16) for _ in range(2)]
    kg = [P.sb("kg", [128, DKC, 512], BF16) for _ in range(2)]
    vg = [P.sb("vg", [128, 4, DV], BF16) for _ in range(2)]
    sgg = [P.sb("sgg", [128, 4, DV], BF16) for _ in range(2)]
    ofg = [P.sb("ofg", [128, 4, DV], F32) for _ in range(2)]
    S = [P.sb("S", [128, DV], F32) for _ in range(DKC)]
    Sb = [[P.sb("Sb", [128, DV], BF16) for _ in range(DKC)] for _ in range(3)]
    lgT = P.sb("lgT", [128, C], F32)
    csT = P.sb("csT", [128, C], F32)
    bT = P.sb("bT", [128, C], F32)
    NDT = 1 if ret else 2
    eb = [P.sb("eb", [128, C], F32) for _ in range(NDT)]
    enb = [P.sb("enb", [128, C], F32) for _ in range(NDT)]
    ekk = [P.sb("ekk", [128, C], F32) for _ in range(NDT)]
    ebl = [P.sb("ebl", [128, 1], F32) for _ in range(NDT)]
    GW = 512
    qdG = [P.sb("qdG", [128, DKC, GW], BF16) for _ in range(2)]
    kdG = [P.sb("kdG", [128, DKC, GW], BF16) for _ in range(2)]
    kkTG = [P.sb("kkTG", [128, DKC, GW], BF16) for _ in range(2)]
    if ret:
        eb4 = P.sb("eb4", [128, GW], F32)
        enb4 = P.sb("enb4", [128, GW], F32)
        ekk4 = P.sb("ekk4", [128, GW], F32)
    else:
        tG = [P.sb("tG", [128, GW], F32) for _ in range(2)]
        kfG = [P.sb("kfG", [128, GW], F32) for _ in range(2)]
        lgG = [P.sb("lgG", [128, GW], F32) for _ in range(2)]
        csG = [P.sb("csG", [128, GW], F32) for _ in range(2)]
        bG = [P.sb("bG", [128, GW], F32) for _ in range(2)]
        dG = [P.sb("dG", [128, GW], F32) for _ in range(2)]
        ebG = [P.sb("ebG", [128, GW], F32) for _ in range(2)]
        enbG = [P.sb("enbG", [128, GW], F32) for _ in range(2)]
        ekkG = [P.sb("ekkG", [128, GW], F32) for _ in range(2)]
        eblG = [P.sb("eblG", [128, 4], F32) for _ in range(2)]
        maskG = P.sb("maskG", [128, GW], F32)
        P.memset(maskG[:], 1.0, (maskG,))
        for c4 in range(4):
            P.memset(maskG[:, c4 * 128:c4 * 128 + 1], 0.0, (maskG,))
    kk = [P.sb("kk", [128, DK], BF16) for _ in range(2)]
    att = [P.sb("att", [128, C], BF16) for _ in range(2)]
    osb = [P.sb("osb", [128, DV], F32) for _ in range(2)]
    junk = P.sb("junk", [128, DV], F32)
    st = [P.sb("st", [128, 2], F32) for _ in range(2)]
    ub = [P.sb("ub", [128, DV], BF16) for _ in range(2)]
    ps_t = P.ps("ps_t", [128, C], BF16)
    ps_a = [P.ps("ps_a", [128, 512]) for _ in range(2)]
    ps_o = [P.ps("ps_o", [128, 512]) for _ in range(2)]
    ps_d = [P.ps("ps_d", [128, 512]) for _ in range(DKC)]

    def decay_tiles(lg_ap, lg_reads, dirn, i):
        P.op("dve", lambda e: e.tensor_tensor_scan(csT[:], ones[:], lg_ap, 0.0, ALU.mult, ALU.add), (ones,) + lg_reads, (csT,))
        tot = csT[:, C - 1:C]
        if dirn == 0:
            b, breads = csT, (csT,)
        else:
            P.ts(bT[:], csT[:], tot, None, ALU.subtract, None, (csT,), (bT,))
            P.stt(bT[:], bT[:], -1.0, lg_ap, ALU.mult, ALU.add, (bT,) + lg_reads, (bT,))
            b, breads = bT, (bT, csT)
        P.act(eb[i][:], b[:], AF.Exp, breads, (eb[i],))
        P.act(enb[i][:], b[:], AF.Exp, breads, (enb[i],), scale=-1.0)
        P.act(ekk[i][:], b[:], AF.Exp, breads, (ekk[i],), scale=-1.0, bias=tot)
        P.act(ebl[i][:], tot, AF.Exp, (csT,), (ebl[i],))

    groups = [[0, 1]] + [list(range(2 + 4 * g, 6 + 4 * g)) for g in range(16)]

    def runs(chs):
        out = []
        for c in chs:
            if out and grow(c) == out[-1][0] + out[-1][1] * 128:
                out[-1][1] += 1
            else:
                out.append([grow(c), 1])
        return out

    cc = 0
    for it in range(NI):
        if ret:
            qcol, kcols, vcol, sgcol = it * 256, [512 + it * 256] * 2, 1024 + it * 512, 2048 + it * 512
        else:
            qcol, kcols, vcol, sgcol = it * 128, [512 + it * 128, 1024 + it * 128], 1536 + it * 128, 2048 + it * 128
        for dirn in range(2):
            for c in range(DKC):
                P.memset(S[c][:], 0.0, (S[c],))
                P.memset(Sb[2][c][:], 0.0, (Sb[2][c],), eng="pool")
            if ret:
                P.copy(lgT[:], dec[:, dirn * 2 + it:dirn * 2 + it + 1].broadcast_to([128, C]), (dec,), (lgT,))
                decay_tiles(lgT[:], (lgT,), dirn, 0)
                for src, dst in ((eb[0], eb4), (enb[0], enb4), (ekk[0], ekk4)):
                    P.copy(dst[:, :].rearrange("p (c t) -> p c t", t=C), src[:].unsqueeze(1).broadcast_to([128, 4, C]), (src,), (dst,))
            if dirn == 0:
                gorder = list(range(len(groups)))
            else:
                gorder = [0] + list(range(len(groups) - 1, 0, -1))
            kcol = kcols[dirn]

            def load(gi, n):
                j0 = 0
                for (r0, nch) in runs(groups[gi]):
                    gs = nch * 128
                    for c in range(DKC):
                        P.dma_t("sp", qg[n % 2][:, c, j0 * 128:j0 * 128 + gs], Y[r0:r0 + gs, qcol + c * 128:qcol + (c + 1) * 128], (), (qg[n % 2],))
                        P.dma_t("sp", kg[n % 2][:, c, j0 * 128:j0 * 128 + gs], Y[r0:r0 + gs, kcol + c * 128:kcol + (c + 1) * 128], (), (kg[n % 2],))
                    P.dma("sp", vg[n % 2][:, j0:j0 + nch, :], Y[r0:r0 + gs, vcol:vcol + DV].rearrange("(j p) d -> p j d", p=128), (), (vg[n % 2],))
                    if dirn == 1:
                        P.dma("sp", sgg[n % 2][:, j0:j0 + nch, :], Y[r0:r0 + gs, sgcol:sgcol + DV].rearrange("(j p) d -> p j d", p=128), (), (sgg[n % 2],))
                    j0 += nch
                if dirn == 1:
                    for j, ch in enumerate(groups[gi]):
                        P.dma("sp", ofg[n % 2][:, j, :], OF[grow(ch):grow(ch) + 128, it * DV:(it + 1) * DV], (scr[it][ch],), (ofg[n % 2],))

            seq = []
            for n, gi in enumerate(gorder):
                nj = len(groups[gi])
                for j in (range(nj) if dirn == 0 else range(nj - 1, -1, -1)):
                    seq.append((n, gi, j, groups[gi][j], len(seq)))

            def gprep(n, gi):
                nj = len(groups[gi])
                W = nj * C
                q_, k_ = qg[n % 2], kg[n % 2]
                g2 = n % 2
                if ret:
                    e3 = lambda t: bcast_mid(t[:, 0:W], DKC)
                    P.stt(qdG[g2][:, :, 0:W], q_[:, :, 0:W], CQ, e3(eb4), ALU.mult, ALU.mult, (q_, eb4), (qdG[g2],))
                    P.stt(kdG[g2][:, :, 0:W], k_[:, :, 0:W], CK, e3(enb4), ALU.mult, ALU.mult, (k_, enb4), (kdG[g2],))
                    P.stt(kkTG[g2][:, :, 0:W], k_[:, :, 0:W], CK, e3(ekk4), ALU.mult, ALU.mult, (k_, ekk4), (kkTG[g2],))
                    return
                t_, kf, lg, cs, b_, d_ = tG[g2], kfG[g2], lgG[g2], csG[g2], bG[g2], dG[g2]
                e_b, e_nb, e_kk, e_bl = ebG[g2], enbG[g2], ekkG[g2], eblG[g2]
                P.act(t_[:, 0:W], k_[:, 0, 0:W], AF.Exp, (k_,), (t_,), scale=-1.0)
                P.act(t_[:, 0:W], t_[:, 0:W], AF.Ln, (t_, oneb), (t_,), bias=oneb[:, 0:1], scale=1.0)
                P.act(t_[:, 0:W], t_[:, 0:W], AF.Exp, (t_,), (t_,), scale=-1.0)
                P.ts(t_[:, 0:W], t_[:, 0:W], lb[:, it, 1:2], lb[:, it, 0:1], ALU.mult, ALU.add, (t_, lb), (t_,))
                P.ts(kf[:, 0:W], t_[:, 0:W], -1.0, 1.0, ALU.mult, ALU.add, (t_,), (kf,))
                P.act(lg[:, 0:W], t_[:, 0:W], AF.Ln, (t_,), (lg,))
                P.op("dve", lambda e: e.tensor_tensor_scan(cs[:, 0:W], maskG[:, 0:W], lg[:, 0:W], 0.0, ALU.mult, ALU.add), (maskG, lg), (cs,))
                v3 = lambda t: t[:, 0:W].rearrange("p (c t) -> p c t", t=C)
                tot = v3(cs)[:, :, C - 1:C]
                totb = tot.broadcast_to([128, nj, C])
                if dirn == 0:
                    bsrc, breads = cs, (cs,)
                else:
                    P.tt(v3(b_), v3(cs), totb, ALU.subtract, (cs,), (b_,))
                    P.stt(b_[:, 0:W], b_[:, 0:W], -1.0, lg[:, 0:W], ALU.mult, ALU.add, (b_, lg), (b_,))
                    bsrc, breads = b_, (b_, cs)
                P.act(e_b[:, 0:W], bsrc[:, 0:W], AF.Exp, breads, (e_b,))
                P.act(e_nb[:, 0:W], bsrc[:, 0:W], AF.Exp, breads, (e_nb,), scale=-1.0)
                P.tt(v3(d_), totb, v3(bsrc), ALU.subtract, breads + (cs,), (d_,))
                P.act(e_kk[:, 0:W], d_[:, 0:W], AF.Exp, (d_,), (e_kk,))
                P.act(e_bl[:, 0:nj].unsqueeze(2), tot, AF.Exp, (cs,), (e_bl,))
                P.stt(qdG[g2][:, 0, 0:W], q_[:, 0, 0:W], CQ, e_b[:, 0:W], ALU.mult, ALU.mult, (q_, e_b), (qdG[g2],))
                P.stt(kdG[g2][:, 0, 0:W], kf[:, 0:W], CK, e_nb[:, 0:W], ALU.mult, ALU.mult, (kf, e_nb), (kdG[g2],))
                P.stt(kkTG[g2][:, 0, 0:W], kf[:, 0:W], CK, e_kk[:, 0:W], ALU.mult, ALU.mult, (kf, e_kk), (kkTG[g2],))

            def prep(n, gi, j, ch, i):
                nonlocal cc
                if i == 0 or seq[i - 1][0] != n:
                    gprep(n, gi)
                v_ = vg[n % 2]
                x = cc % 2
                cc += 1
                g2 = n % 2
                sl = slice(j * 128, (j + 1) * 128)
                qd_, kd_, kkT_ = qdG[g2], kdG[g2], kkTG[g2]
                kk_, att_ = kk[x], att[x]
                ebl_ap = ebl[0][:, 0:1] if ret else eblG[g2][:, j:j + 1]
                ebl_buf = ebl[0] if ret else eblG[g2]
                pa = ps_a[x]
                for c in range(DKC):
                    P.mm(pa[:, 0:C], kd_[:, c, sl], qd_[:, c, sl], c == 0, c == DKC - 1, (kd_, qd_), (pa,))
                P.tt(att_[:], pa[:, 0:C], tri[dirn][:], ALU.mult, (pa, tri[dirn]), (att_,))
                for c in range(DKC):
                    P.transpose(ps_t[:], kkT_[:, c, sl], idn[:], (kkT_, idn), (ps_t,))
                    P.act(kk_[:, c * 128:(c + 1) * 128], ps_t[:], AF.Copy, (ps_t,), (kk_,))
                for c in range(DKC):
                    P.mm(ps_d[c][:, 0:DV], kk_[:, c * 128:(c + 1) * 128], v_[:, j, :], True, True, (kk_, v_), (ps_d[c],))
                    P.stt(S[c][:], S[c][:], ebl_ap, ps_d[c][:, 0:DV], ALU.mult, ALU.add, (S[c], ebl_buf, ps_d[c]), (S[c],))
                    P.act(Sb[i % 3][c][:], S[c][:], AF.Copy, (S[c],), (Sb[i % 3][c],))
                return (n, j, ch, x, i)

            def state(n, j, ch, x, i):
                v_, sg_, of_ = vg[n % 2], sgg[n % 2], ofg[n % 2]
                qd_, att_ = qdG[n % 2], att[x]
                sl = slice(j * 128, (j + 1) * 128)
                Sprev = Sb[(i - 1) % 3]
                po = ps_o[x]
                P.mm(po[:, 0:DV], att_[:], v_[:, j, :], True, False, (att_, v_), (po,))
                for c in range(DKC):
                    P.mm(po[:, 0:DV], qd_[:, c, sl], Sprev[c][:], False, c == DKC - 1, (qd_, Sprev[c]), (po,))
                o_ = osb[x]
                if dirn == 0:
                    P.act(o_[:], po[:, 0:DV], AF.Copy, (po,), (o_,))
                    P.dma("pool", OF[grow(ch):grow(ch) + 128, it * DV:(it + 1) * DV], o_[:], (o_,), (scr[it][ch],))
                else:
                    s_, u_ = st[x], ub[x]
                    P.tt(o_[:], po[:, 0:DV], of_[:, j, :], ALU.add, (po, of_), (o_,))
                    P.act(junk[:], o_[:], AF.Square, (o_,), (junk, s_), accum_out=s_[:, 0:1])
                    P.act(s_[:, 1:2], s_[:, 0:1], AF.Ln, (s_, epsb), (s_,), bias=epsb[:, 0:1], scale=1.0 / DV)
                    P.act(s_[:, 1:2], s_[:, 1:2], AF.Exp, (s_,), (s_,), scale=-0.5)
                    if ret:
                        P.stt(u_[:], o_[:], s_[:, 1:2], sg_[:, j, :], ALU.mult, ALU.mult, (o_, s_, sg_), (u_,))
                    else:
                        P.stt(o_[:], o_[:], s_[:, 1:2], ng[:], ALU.mult, ALU.mult, (o_, s_, ng), (o_,))
                        P.tt(u_[:], o_[:], sg_[:, j, :], ALU.mult, (o_, sg_), (u_,), eng="pool")
                    P.dma("pool", U[grow(ch):grow(ch) + 128, it * DV:(it + 1) * DV], u_[:], (u_,), ())

            load(gorder[0], 0)
            if len(gorder) > 1:
                load(gorder[1], 1)
            nxt = prep(*seq[0])
            for i in range(len(seq)):
                cur = nxt
                if i + 1 < len(seq):
                    nxt = prep(*seq[i + 1])
                state(*cur)
                n_cur = seq[i][0]
                if (i + 1 == len(seq) or seq[i + 1][0] != n_cur) and n_cur + 2 < len(gorder):
                    load(gorder[n_cur + 2], n_cur + 2)
    P.end_stage()


PAIRS = [[0, 1], [2, 3], [4, 5], [6, 7]]


def stage_gather(P, SRC, GC, nrows, rc):
    for j in range(nrows // rc):
        P.collective("AllGather", PAIRS, SRC[j * rc:(j + 1) * rc, :], GC[j * 2 * rc:(j + 1) * 2 * rc, :])
    P.end_stage()


def gc_row(rc, r, t):
    return (t // rc) * 2 * rc + r * rc + (t % rc)


def stage_blend(P, OUT, nrows, segs, SEL):
    sel = P.sb("sel", [128, 2], F32)
    P.dma("sp", sel[:], SEL, (), (sel,))
    NB, LA = 4, 3
    ab = [[P.sb("ba", [128, w], BF16) for _ in range(NB)] for (_, w, _, _) in segs]
    bb = [[P.sb("bb", [128, w], BF16) for _ in range(NB)] for (_, w, _, _) in segs]
    ob = [[P.sb("bo", [128, w], BF16) for _ in range(NB)] for (_, w, _, _) in segs]
    jobs = [(t, si) for t in range(nrows // 128) for si in range(len(segs))]

    def load(n):
        t, si = jobs[n]
        c0, w, fa, fb = segs[si]
        a, b = ab[si][t % NB], bb[si][t % NB]
        P.dma("sp", a[:], fa(t * 128), (), (a,))
        P.dma("act", b[:], fb(t * 128), (), (b,))

    for n in range(min(LA, len(jobs))):
        load(n)
    for n, (t, si) in enumerate(jobs):
        if n + LA < len(jobs):
            load(n + LA)
        c0, w, fa, fb = segs[si]
        a, b, o = ab[si][t % NB], bb[si][t % NB], ob[si][t % NB]
        P.ts(a[:], a[:], sel[:, 0:1], None, ALU.mult, None, (a, sel), (a,))
        P.stt(o[:], b[:], sel[:, 1:2], a[:], ALU.mult, ALU.add, (b, sel, a), (o,))
        P.dma("pool", OUT[t * 128:(t + 1) * 128, c0:c0 + w], o[:], (o,), ())
    P.end_stage()


def head_select(P, GC, rc, M, segcols, SEL):
    segs = []
    c = 0
    for (g0, wt) in segcols:
        w = wt // 2

        def fa(r0, g0=g0, w=w, off=0):
            g = gc_row(rc, r0 // T_, r0 % T_)
            return GC[g:g + 128, g0 + off:g0 + off + w]

        segs.append((c, w, fa, (lambda r0, fa=fa, w=w: fa(r0, off=w))))
        c += w
    stage_blend(P, M, TG_, segs, SEL)


def token_select(P, GC, rc, L, w, SEL):
    segs = []
    for r in range(2):
        segs.append((r * w, w, (lambda r0, r=r: GC[gc_row(rc, r, r0):gc_row(rc, r, r0) + 128, :]),
                     (lambda r0, r=r: GC[gc_row(rc, r, T_ + r0):gc_row(rc, r, T_ + r0) + 128, :])))
    stage_blend(P, L, T_, segs, SEL)


def stage_reorder(P, GC, rc, G, nrows):
    for j in range(nrows // rc):
        for r in range(2):
            P.dma("sp", G[r * nrows + j * rc:r * nrows + (j + 1) * rc, :], GC[gc_row(rc, r, j * rc):gc_row(rc, r, j * rc) + rc, :], (), ())
    P.end_stage()


def build_fused():
    P = Prog()
    T, TG = T_, TG_
    f = lambda name, shape, dt=F32: P.din(name, shape, dt)
    scr = P.dscr

    XIN = f("XIN", [T, 1024])
    condT = f("condT", [128, 8, 2])
    SEL = f("SEL", [128, 2])
    ADA_W, ADA_B = f("ADA_W", [4, 1024, 6144]), f("ADA_B", [4, 6144])
    LN_G, LN_B = f("LN_G", [4, 2, 1024]), f("LN_B", [4, 2, 1024])
    W13, W2 = f("W13", [4, 1024, 5632]), f("W2", [4, 2816, 1024])
    RET_IN, RET_OUT, DEC = f("RET_IN", [1024, 6144]), f("RET_OUT", [2048, 1024]), f("DEC", [128, 4])
    NA_QKV, NA_OUT = f("NA_QKV", [1024, 3072]), f("NA_OUT", [1024, 1024])
    MI, MB, MKI, MKB = f("MI", [8, 128, 5, 128]), f("MB", [8, 128, 7, 128]), f("MKI", [128, 5, 128]), f("MKB", [128, 7, 128])
    MLA_DOWN, MLA_UQ, MLA_UKV, MLA_OUT = f("MLA_DOWN", [1024, 800]), f("MLA_UQ", [512, 1536]), f("MLA_UKV", [256, 2048]), f("MLA_OUT", [1024, 1024])
    GN = f("GN", [128, 768])
    HG_IN, HG_OUT, LBR, NG = f("HG_IN", [1024, 5120]), f("HG_OUT", [1024, 1024]), f("LBR", [128, 4, 4]), f("NG", [128, 128])
    COS256, SIN256 = f("COS256", [T, 128]), f("SIN256", [T, 128])
    COS32, SIN32 = f("COS32", [T, 16]), f("SIN32", [T, 16])
    COS32S, SIN32S = f("COS32S", [T, 16]), f("SIN32S", [T, 16])
    IDN, TRI, IDF = f("IDN", [128, 128], BF16), f("TRI", [2, 128, 128]), f("IDF", [128, 128])
    OUT = P.dout("OUT", [T, 1024], F32)

    MOD = scr("MOD", [4, 2, 6144], F32)
    A0 = scr("A0", [T, 1024], BF16)
    A2 = scr("A2", [T, 1024], BF16)
    FF = scr("FF", [2816, T], BF16)
    HA = scr("HA", [T, 1024], F32)
    HB = scr("HB", [T, 1024], F32)

    def m(i, r, s):
        return MOD[i, r, s * 1024:(s + 1) * 1024]

    stage_k0(P, condT, ADA_W, ADA_B, MOD)
    stage_mod0(P, XIN, [m(0, 0, 1), m(0, 0, 0), m(0, 1, 1), m(0, 1, 0)], A0, T)

    def post(i, Usrc, K, WOUT, Hin, last):
        v1 = [m(i, 0, 2), m(i, 1, 2), LN_G[i, 0], LN_B[i, 0], m(i, 0, 4), m(i, 0, 3), m(i, 1, 4), m(i, 1, 3)]
        stage_resid_ln(P, Usrc, WOUT, Hin, HB, A2, v1, T, K)
        stage_swiglu(P, A2, W13[i], FF, T)
        if last:
            v2 = [m(i, 0, 5), m(i, 1, 5), LN_G[i, 1], LN_B[i, 1]]
            stage_resid_ln(P, FF, W2[i], HB, OUT, None, v2, T, 2816, x_fm=True)
        else:
            v2 = [m(i, 0, 5), m(i, 1, 5), LN_G[i, 1], LN_B[i, 1], m(i + 1, 0, 1), m(i + 1, 0, 0), m(i + 1, 1, 1), m(i + 1, 1, 0)]
            stage_resid_ln(P, FF, W2[i], HB, HA, A0, v2, T, 2816, x_fm=True)

    Y0L, Y0G, Y0M = scr("Y0L", [T, 6144], BF16), scr("Y0G", [TG, 6144], BF16), scr("Y0M", [TG, 3072], BF16)
    U0M, OF0, U0G, U0L = scr("U0M", [TG, 1024], BF16), scr("OF0", [TG, 1024], F32), scr("U0G", [2 * TG, 1024], BF16), scr("U0L", [T, 2048], BF16)
    stage_ret_a(P, A0, RET_IN, Y0L, COS256, SIN256, T)
    stage_gather(P, Y0L, Y0G, T, 128)
    head_select(P, Y0G, 128, Y0M, [(0, 1024), (1024, 1024), (2048, 2048), (4096, 2048)], SEL)
    stage_gla(P, "ret", Y0M, U0M, OF0, DEC, None, IDN, TRI)
    stage_gather(P, U0M, U0G, TG, 768)
    token_select(P, U0G, 768, U0L, 1024, SEL)
    post(0, U0L, 2048, RET_OUT, XIN, False)
    Y1L, Y1G, Y1M = scr("Y1L", [T, 3072], BF16), scr("Y1G", [TG, 3072], BF16), scr("Y1M", [TG, 1536], BF16)
    O1M, O1G, O1L = scr("O1M", [TG, 512], BF16), scr("O1G", [2 * TG, 512], BF16), scr("O1L", [T, 1024], BF16)
    stage_plain(P, A0, NA_QKV, Y1L, T, 1024, 3072)
    stage_gather(P, Y1L, Y1G, T, 128)
    head_select(P, Y1G, 128, Y1M, [(0, 1024), (1024, 1024), (2048, 1024)], SEL)
    stage_na_b(P, Y1M, MI, MB, MKI, MKB, O1M)
    stage_gather(P, O1M, O1G, TG, 1408)
    token_select(P, O1G, 1408, O1L, 512, SEL)
    post(1, O1L, 1024, NA_OUT, HA, False)
    Y2A, YKRL, YKRG = scr("Y2A", [T, 800], BF16), scr("YKRL", [T, 128], BF16), scr("YKRG", [TG, 128], BF16)
    Y2Q, Y2KVL, Y2KVG, O2L = scr("Y2Q", [T, 2048], BF16), scr("Y2KVL", [T, 2048], BF16), scr("Y2KVG", [TG, 2048], BF16), scr("O2L", [T, 1024], BF16)
    stage_mla_a1(P, A0, MLA_DOWN, Y2A, YKRL, COS32, SIN32, GN, T)
    stage_mla_a2q(P, Y2A[:, 0:512], MLA_UQ, Y2Q, COS32S, SIN32S, T)
    stage_plain(P, Y2A[:, 512:768], MLA_UKV, Y2KVL, T, 256, 2048)
    Y2KVC = scr("Y2KVC", [TG, 2048], BF16)
    stage_gather(P, Y2KVL, Y2KVC, T, 384)
    stage_reorder(P, Y2KVC, 384, Y2KVG, T)
    stage_gather(P, YKRL, YKRG, T, T)
    stage_mla_b(P, Y2Q, Y2KVG, YKRG, IDF, O2L)
    post(2, O2L, 1024, MLA_OUT, HA, False)
    Y3L, Y3G, Y3M = scr("Y3L", [T, 5120], BF16), scr("Y3G", [TG, 5120], BF16), scr("Y3M", [TG, 2560], BF16)
    U3M, OF3, U3G, U3L = scr("U3M", [TG, 512], BF16), scr("OF3", [TG, 512], F32), scr("U3G", [2 * TG, 512], BF16), scr("U3L", [T, 1024], BF16)
    stage_plain(P, A0, HG_IN, Y3L, T, 1024, 5120, silu_blocks=(0, 1, 8, 9))
    stage_gather(P, Y3L, Y3G, T, 128)
    head_select(P, Y3G, 128, Y3M, [(j * 1024, 1024) for j in range(5)], SEL)
    stage_gla(P, "hg", Y3M, U3M, OF3, LBR, NG, IDN, TRI)
    stage_gather(P, U3M, U3G, TG, 1408)
    token_select(P, U3G, 1408, U3L, 512, SEL)
    post(3, U3L, 1024, HG_OUT, HA, True)
    return P


def _rep(v):
    v = np.asarray(v)
    return np.ascontiguousarray(np.broadcast_to(v[None], (128,) + v.shape))


def _rope_tables(rot_dim, half, L=8192):
    t = np.arange(L)
    rows = (t // 64).astype(np.float32)
    cols = (t % 64).astype(np.float32)
    nf = rot_dim // 4
    inv = (np.float32(10000.0) ** (-np.arange(nf, dtype=np.float32) / np.float32(nf))).astype(np.float32)
    ang = np.concatenate([rows[:, None] * inv, cols[:, None] * inv], -1).astype(np.float32)
    nc_ = rot_dim // 2
    sl = slice(half * (L // 2), (half + 1) * (L // 2))
    cos = np.concatenate([np.ones((128, nc_), np.float32), np.cos(ang).astype(np.float32)[sl]], 0)
    sin = np.concatenate([np.zeros((128, nc_), np.float32), np.sin(ang).astype(np.float32)[sl]], 0)
    return np.ascontiguousarray(cos), np.ascontiguousarray(sin)


def kernel(x, c, ctx, c_ctx, ada_w, ada_b, ln_g, ln_b, ffn_w13, ffn_w2,
           ret_w_in, ret_decay, ret_w_out, na_w_qkv, na_rpb, na_w_out,
           mla_w_down, mla_q_norm, mla_kv_norm, mla_w_uq, mla_w_ukv, mla_w_out,
           hg_w_in, hg_lower_bounds, hg_norm_g, hg_w_out):
    f32 = np.float32
    A = lambda a: np.ascontiguousarray(np.asarray(a, dtype=f32))
    x, c, ctx, c_ctx = A(x), A(c), A(ctx), A(c_ctx)
    import ml_dtypes
    bf = ml_dtypes.bfloat16
    MI, MB, MKI, MKB = na_mask_tables(A(na_rpb))
    SC = np.float32(96 ** -0.5)
    lbw = A(hg_lower_bounds).reshape(4, 8, 128)
    dec = A(ret_decay)
    shared = {
        "ADA_W": A(ada_w), "ADA_B": A(ada_b), "LN_G": A(ln_g), "LN_B": A(ln_b), "W13": A(ffn_w13), "W2": A(ffn_w2),
        "RET_IN": A(ret_w_in), "RET_OUT": A(ret_w_out),
        "NA_QKV": A(na_w_qkv), "NA_OUT": A(na_w_out), "MKI": MKI, "MKB": MKB,
        "MLA_DOWN": A(mla_w_down), "MLA_UQ": A(mla_w_uq), "MLA_UKV": A(mla_w_ukv), "MLA_OUT": A(mla_w_out),
        "GN": _rep(np.concatenate([A(mla_q_norm), A(mla_kv_norm)])),
        "HG_IN": A(hg_w_in), "HG_OUT": A(hg_w_out), "NG": _rep(A(hg_norm_g)),
        "IDN": np.eye(128, dtype=f32).astype(bf), "IDF": np.eye(128, dtype=f32), "TRI": np.stack([np.triu(np.ones((128, 128), f32)), np.tril(np.ones((128, 128), f32))]),
    }
    halfd = []
    for p in range(2):
        c256, s256 = _rope_tables(256, p)
        c32, s32 = _rope_tables(32, p)
        halfd.append({
            "COS256": c256, "SIN256": s256, "COS32": c32, "SIN32": s32, "COS32S": c32 * SC, "SIN32S": s32 * SC,
            "SEL": _rep(np.array([1.0 - p, float(p)], f32)),
            "MI": np.ascontiguousarray(MI[p * 8:(p + 1) * 8]), "MB": np.ascontiguousarray(MB[p * 8:(p + 1) * 8]),
            "DEC": _rep(np.array([dec[d, 2 * p + i] for d in range(2) for i in range(2)], f32)),
            "LBR": np.ascontiguousarray(lbw[:, 4 * p:4 * p + 4, :].transpose(2, 1, 0)),
        })
    in_maps = []
    for k in range(NCORES):
        b, p = k // 2, k % 2
        cond = np.stack([c[b], c_ctx], 0)
        d = dict(shared)
        d.update(halfd[p])
        d["XIN"] = np.ascontiguousarray(np.concatenate([ctx[b, p * 128:(p + 1) * 128], x[b, p * 4096:(p + 1) * 4096]], 0))
        d["condT"] = np.ascontiguousarray(cond.T.reshape(8, 128, 2).transpose(1, 0, 2))
        in_maps.append(d)
    res = run(build_fused(), in_maps)
    return np.ascontiguousarray(np.stack([np.concatenate([res[2 * b]["OUT"][128:], res[2 * b + 1]["OUT"][128:]], 0)
                                          for b in range(4)]).astype(np.float32))
```

```python
import numpy as np
from contextlib import ExitStack
import concourse.bass as bass
import concourse.mybir as mybir
from concourse.bass_utils import run_bass_kernel_spmd

F32 = mybir.dt.float32
BF16 = mybir.dt.bfloat16
AF = mybir.ActivationFunctionType
ALU = mybir.AluOpType
AX = mybir.AxisListType

NCORES = 8


class Buf:
    __slots__ = ("name", "t", "last_w", "readers")

    def __init__(self, name, t):
        self.name = name
        self.t = t
        self.last_w = None
        self.readers = []

    def __getitem__(self, idx):
        return self.t[idx]


class Op:
    __slots__ = ("eng", "fn", "deps", "signal", "token", "is_dma", "prewait", "is_cc")

    def __init__(self, eng, fn, is_dma):
        self.eng = eng
        self.fn = fn
        self.deps = []
        self.signal = is_dma
        self.token = None
        self.is_dma = is_dma
        self.prewait = None
        self.is_cc = False


ENGS = ("pe", "act", "dve", "pool", "sp")
NPOOL = {"sp": 48, "pool": 24, "act": 16}
CC_OUTSTANDING = 1


class Prog:
    def __init__(self):
        self.nc = nc = bass.Bass("TRN2", target_bir_lowering=False)
        self.gs = gs = ExitStack()
        self.esem = {e: gs.enter_context(nc.semaphore(f"s_{e}")) for e in ENGS if e != "sp"}
        self.dsem = {q: [gs.enter_context(nc.semaphore(f"d_{q}{i}")) for i in range(NPOOL[q])] for q in NPOOL}
        self.bsem = gs.enter_context(nc.semaphore("bar"))
        self.csem = gs.enter_context(nc.semaphore("cc"))
        self.ccnt = 0
        self.cnt = {e: 0 for e in ENGS}
        self.dcnt = {q: 0 for q in NPOOL}
        self.nbar = 0
        self.bt = {e: gs.enter_context(nc.sbuf_tensor(f"bt_{e}", [128, 8], F32)) for e in ("act", "dve", "pool")}
        self.btpe = gs.enter_context(nc.sbuf_tensor("bt_pe", [128, 8], BF16))
        self.bps = gs.enter_context(nc.psum_tensor("bps", [128, 8], F32))
        self.bdram = nc.dram_tensor("bar_d", [2, 16], F32, kind="Internal").ap()
        self.ss = ExitStack()
        self.ops = []
        self.n = 0

    def din(self, name, shape, dt):
        return self.nc.dram_tensor(name, list(shape), dt, kind="ExternalInput").ap()

    def dout(self, name, shape, dt):
        return self.nc.dram_tensor(name, list(shape), dt, kind="ExternalOutput").ap()

    def dscr(self, name, shape, dt):
        return self.nc.dram_tensor(name, list(shape), dt, kind="Internal").ap()

    def sb(self, name, shape, dt):
        self.n += 1
        t = self.ss.enter_context(self.nc.sbuf_tensor(f"{name}_{self.n}", list(shape), dt))
        return Buf(name, t)

    def ps(self, name, shape, dt=F32):
        self.n += 1
        t = self.ss.enter_context(self.nc.psum_tensor(f"{name}_{self.n}", list(shape), dt))
        return Buf(name, t)

    def op(self, eng, fn, reads=(), writes=(), is_dma=False):
        o = Op(eng, fn, is_dma)
        deps = []
        for b in reads:
            if b.last_w is not None:
                deps.append(b.last_w)
        for b in writes:
            w = b.last_w
            if w is not None and (is_dma or w.is_dma or w.eng != eng):
                deps.append(w)
            for r in b.readers:
                if is_dma or r.is_dma or r.eng != eng:
                    deps.append(r)
        for b in writes:
            b.last_w = o
            b.readers = []
        for b in reads:
            b.readers.append(o)
        seen = set()
        for d in deps:
            if id(d) not in seen and d is not o:
                seen.add(id(d))
                o.deps.append(d)
                d.signal = True
        self.ops.append(o)
        return o

    def dma(self, eng, out, in_, reads=(), writes=()):
        return self.op(eng, lambda e: e.dma_start(out=out, in_=in_), reads, writes, is_dma=True)

    def collective(self, kind, groups, in_ap, out_ap):
        o = self.op("pool", lambda e: e.collective_compute(kind, ALU.bypass, replica_groups=groups, ins=[in_ap], outs=[out_ap]),
                    (), (), is_dma=True)
        o.is_cc = True
        return o

    def dma_t(self, eng, out, in_, reads=(), writes=()):
        return self.op(eng, lambda e: e.dma_start_transpose(out=out, in_=in_), reads, writes, is_dma=True)

    def mm(self, out, lhsT, rhs, start, stop, reads, writes):
        return self.op("pe", lambda e: e.matmul(out, lhsT, rhs, start=start, stop=stop), reads, writes)

    def transpose(self, out, in_, ident, reads, writes):
        return self.op("pe", lambda e: e.transpose(out, in_, ident), reads, writes)

    def act(self, out, in_, func, reads, writes, bias=None, scale=None, accum_out=None, eng="act"):
        kw = {}
        if bias is not None:
            kw["bias"] = bias
        if scale is not None:
            kw["scale"] = scale
        if accum_out is not None:
            kw["accum_out"] = accum_out
        return self.op(eng, lambda e: e.activation(out, in_, func, **kw), reads, writes)

    def tt(self, out, in0, in1, op, reads, writes, eng="dve"):
        return self.op(eng, lambda e: e.tensor_tensor(out, in0, in1, op), reads, writes)

    def ts(self, out, in0, s1, s2, op0, op1, reads, writes, eng="dve", accum_out=None):
        if op1 is None:
            return self.op(eng, lambda e: e.tensor_scalar(out, in0, s1, None, op0), reads, writes)
        if accum_out is not None:
            return self.op(eng, lambda e: e.tensor_scalar(out, in0, s1, s2, op0, op1, accum_out=accum_out), reads, writes)
        return self.op(eng, lambda e: e.tensor_scalar(out, in0, s1, s2, op0, op1), reads, writes)

    def stt(self, out, in0, scalar, in1, op0, op1, reads, writes, eng="dve"):
        return self.op(eng, lambda e: e.scalar_tensor_tensor(out, in0, scalar, in1, op0, op1), reads, writes)

    def copy(self, out, in_, reads, writes, eng="dve"):
        return self.op(eng, lambda e: e.tensor_copy(out, in_), reads, writes)

    def memset(self, out, val, writes, eng="dve"):
        return self.op(eng, lambda e: e.memset(out, val), (), writes)

    def end_stage(self):
        nc = self.nc
        esem, dsem, cnt, dcnt = self.esem, self.dsem, self.cnt, self.dcnt
        for o in self.ops:
            if o.is_cc:
                self.ccnt += 1
                o.token = (self.csem, self.ccnt)
                if self.ccnt > CC_OUTSTANDING:
                    o.prewait = (self.csem, self.ccnt - CC_OUTSTANDING)
            elif o.is_dma:
                q = o.eng
                n = dcnt[q]
                dcnt[q] += 1
                slot = n % NPOOL[q]
                use = n // NPOOL[q]
                o.token = (dsem[q][slot], 16 * (use + 1))
                if use > 0:
                    o.prewait = (dsem[q][slot], 16 * use)
            elif o.signal:
                cnt[o.eng] += 1
                o.token = (esem[o.eng], cnt[o.eng])
        per = {e: [o for o in self.ops if o.eng == e] for e in ENGS}
        self.nbar += 1
        nbar = self.nbar
        drain = {}
        for q in NPOOL:
            drain[q] = []
            for slot in range(min(dcnt[q], NPOOL[q])):
                uses = (dcnt[q] - 1 - slot) // NPOOL[q] + 1
                drain[q].append((dsem[q][slot], 16 * uses))
        bsem = self.bsem

        def emit(eng_name, e):
            known = {}

            def wait(tok):
                s, v = tok
                if known.get(id(s), 0) >= v:
                    return
                known[id(s)] = v
                e.wait_ge(s, v)

            for o in per[eng_name]:
                if o.prewait is not None:
                    wait(o.prewait)
                for d in o.deps:
                    wait(d.token)
                inst = o.fn(e)
                if o.is_cc:
                    inst.then_inc(o.token[0], 1)
                elif o.is_dma:
                    inst.then_inc(o.token[0], 16)
                elif o.signal:
                    inst.then_inc(o.token[0], 1)
            for tok in drain.get(eng_name, ()):
                wait(tok)
            if eng_name == "pool" and self.ccnt:
                wait((self.csem, self.ccnt))
            if eng_name == "sp":
                e.dma_start(out=self.bdram[1:2, :], in_=self.bdram[0:1, :]).then_inc(bsem, 16)
            elif eng_name == "pe":
                e.matmul(self.bps[0:8, 0:8], self.btpe[:, 0:8], self.btpe[:, 0:8], start=True, stop=True).then_inc(bsem, 1)
            elif eng_name == "act":
                e.activation(self.bt["act"][:, 0:1], self.bt["act"][:, 1:2], AF.Copy).then_inc(bsem, 1)
            else:
                e.memset(self.bt[eng_name][:, 0:1], 0.0).then_inc(bsem, 1)
            e.wait_ge(bsem, 20 * nbar)

        with nc.Block() as block:
            @block.sync
            def _(e):
                emit("sp", e)

            @block.tensor
            def _(e):
                emit("pe", e)

            @block.scalar
            def _(e):
                emit("act", e)

            @block.vector
            def _(e):
                emit("dve", e)

            @block.gpsimd
            def _(e):
                emit("pool", e)
        self.ss.close()
        self.ss = ExitStack()
        self.ops = []

    def finish(self):
        if self.ops:
            self.end_stage()
        self.gs.close()
        return self.nc


def run(prog, in_maps):
    nc = prog.finish()
    res = run_bass_kernel_spmd(nc, in_maps, core_ids=list(range(NCORES)))
    return res.results


RMS_EPS = 1e-6
LN_EPS = 1e-5
ALPHA = 8 ** 0.25
T_ = 4224
TG_ = 8448
NCT_ = 1


def grow(c):
    if c < 2:
        return c * T_
    i = c - 2
    return (128 + i * 128) if i < 32 else (T_ + 128 + (i - 32) * 128)


def tok_groups(T):
    gs = []
    t = 0
    while t < T:
        g = min(512, T - t)
        gs.append((t, g))
        t += g
    return gs


def bc_load(P, tile, row_ap, n):
    P.dma("sp", tile[:, 0:n], row_ap.partition_broadcast(128), (), (tile,))


class Lin:
    def __init__(self, P, X, W, T, K, N, blocks=None, x_fm=False):
        self.P = P
        self.x_fm = x_fm
        self.finish_ep = None
        self.T, self.K, self.N = T, K, N
        self.KC = K // 128
        assert K % 128 == 0 and T % 128 == 0
        self.X, self.W = X, W
        self.blocks = blocks or [(n0, min(512, N - n0)) for n0 in range(0, N, 512)]
        self.pre_group = None
        self.blk_ep = None
        self.row_ep = None
        Wv = W.rearrange("(k p) n -> p k n", p=128)
        self.wt = []
        for k in range(self.KC):
            w = P.sb("w", [128, N], BF16)
            P.dma("pool", w[:], Wv[:, k, :], (), (w,))
            self.wt.append(w)

    def run(self):
        P = self.P
        KC = self.KC
        xb = [P.sb("xg", [128, KC, 512], BF16) for _ in range(2)]
        psb = [P.ps("ps", [128, 512]) for _ in range(4)]
        groups = tok_groups(self.T)

        def load(g):
            t0, gs = groups[g]
            if self.x_fm:
                P.dma("sp", xb[g % 2][:, :, 0:gs], self.X.rearrange("(k p) t -> p k t", p=128)[:, :, t0:t0 + gs], (), (xb[g % 2],))
            else:
                for k in range(KC):
                    P.dma_t("sp", xb[g % 2][:, k, 0:gs], self.X[t0:t0 + gs, k * 128:(k + 1) * 128], (), (xb[g % 2],))
            if self.pre_group:
                self.pre_group(g, t0, gs)

        load(0)
        ctr = 0
        for g, (t0, gs) in enumerate(groups):
            if g + 1 < len(groups):
                load(g + 1)
            xg = xb[g % 2]
            for j in range(gs // 128):
                tt = t0 // 128 + j
                for nb, (n0, ns) in enumerate(self.blocks):
                    ps = psb[ctr % 4]
                    ctr += 1
                    for k in range(KC):
                        P.mm(ps[:, 0:ns], xg[:, k, j * 128:(j + 1) * 128], self.wt[k][:, n0:n0 + ns],
                             k == 0, k == KC - 1, (xg, self.wt[k]), (ps,))
                    self.blk_ep(tt, j, nb, n0, ns, ps)
                if self.row_ep:
                    self.row_ep(tt, j, t0 + j * 128)
        if self.finish_ep:
            self.finish_ep()


def evac(P, i, out, in_, reads, writes, func=None, scale=None):
    if func is not None:
        return P.act(out, in_, func, reads, writes, scale=scale)
    if i % 2 == 0:
        return P.act(out, in_, AF.Copy, reads, writes)
    return P.copy(out, in_, reads, writes)


def bcast_mid(ap, n):
    return ap.unsqueeze(1).broadcast_to([ap.shape[0], n, ap.shape[1]])


def rope_loads(P, cs, sn, COS, SIN, g, t0, gs):
    nj = gs // 128
    P.dma("sp", cs[g % 2][:, 0:nj, :], COS[t0:t0 + gs, :].rearrange("(j p) f -> p j f", p=128), (), (cs[g % 2],))
    P.dma("sp", sn[g % 2][:, 0:nj, :], SIN[t0:t0 + gs, :].rearrange("(j p) f -> p j f", p=128), (), (sn[g % 2],))


def stage_plain(P, X, W, Y, T, K, N, silu_blocks=()):
    L = Lin(P, X, W, T, K, N)
    ob = [P.sb("ob", [128, N], BF16) for _ in range(2)]

    def blk_ep(tt, j, nb, n0, ns, ps):
        o = ob[tt % 2]
        evac(P, nb, o[:, n0:n0 + ns], ps[:, 0:ns], (ps,), (o,), func=AF.Silu if nb in silu_blocks else None)

    def row_ep(tt, j, t0):
        o = ob[tt % 2]
        P.dma("pool", Y[t0:t0 + 128, :], o[:], (o,), ())

    L.blk_ep, L.row_ep = blk_ep, row_ep
    L.run()
    P.end_stage()


def stage_ret_a(P, X, W, Y, COS, SIN, T):
    L = Lin(P, X, W, T, 1024, 6144)
    ob = [P.sb("ob", [128, 6144], BF16) for _ in range(2)]
    rb = [P.sb("rb", [128, 2048], F32) for _ in range(2)]
    cs = [P.sb("cs", [128, 4, 128], F32) for _ in range(2)]
    sn = [P.sb("sn", [128, 4, 128], F32) for _ in range(2)]
    t1 = P.sb("t1", [128, 8, 128], F32)
    t2 = P.sb("t2", [128, 8, 128], F32)
    t3 = P.sb("t3", [128, 8, 128], F32)
    t4 = P.sb("t4", [128, 8, 128], F32)

    def blk_ep(tt, j, nb, n0, ns, ps):
        if nb < 4:
            r = rb[tt % 2]
            evac(P, nb, r[:, n0:n0 + ns], ps[:, 0:ns], (ps,), (r,))
        else:
            o = ob[tt % 2]
            evac(P, nb, o[:, n0:n0 + ns], ps[:, 0:ns], (ps,), (o,), func=AF.Silu if nb >= 8 else None)

    def row_ep(tt, j, t0):
        g = (t0 // 512)
        r = rb[tt % 2]
        o = ob[tt % 2]
        rv = r[:, :].rearrange("p (h two d) -> p h two d", h=8, two=2)
        ov = o[:, 0:2048].rearrange("p (h two d) -> p h two d", h=8, two=2)
        c = bcast_mid(cs[g % 2][:, j, :], 8)
        s = bcast_mid(sn[g % 2][:, j, :], 8)
        x1, x2 = rv[:, :, 0, :], rv[:, :, 1, :]
        P.tt(t1[:], x1, c, ALU.mult, (r, cs[g % 2]), (t1,))
        P.tt(t2[:], x2, s, ALU.mult, (r, sn[g % 2]), (t2,))
        P.tt(ov[:, :, 0, :], t1[:], t2[:], ALU.subtract, (t1, t2), (o,))
        P.tt(t3[:], x1, s, ALU.mult, (r, sn[g % 2]), (t3,), eng="pool")
        P.tt(t4[:], x2, c, ALU.mult, (r, cs[g % 2]), (t4,), eng="pool")
        P.tt(ov[:, :, 1, :], t3[:], t4[:], ALU.add, (t3, t4), (o,), eng="pool")
        P.dma("pool", Y[t0:t0 + 128, :], o[:], (o,), ())

    L.pre_group = lambda g, t0, gs: rope_loads(P, cs, sn, COS, SIN, g, t0, gs)
    L.blk_ep, L.row_ep = blk_ep, row_ep
    L.run()
    P.end_stage()


def stage_mla_a1(P, X, W, Y, YKR, COS, SIN, GN, T):
    L = Lin(P, X, W, T, 1024, 800)
    gn = P.sb("gn", [128, 768], F32)
    P.dma("sp", gn[:], GN, (), (gn,))
    ob = [P.sb("ob", [128, 800], BF16) for _ in range(2)]
    rb = [P.sb("rb", [128, 800], F32) for _ in range(2)]
    cs = [P.sb("cs", [128, 4, 16], F32) for _ in range(2)]
    sn = [P.sb("sn", [128, 4, 16], F32) for _ in range(2)]
    junk = P.sb("junk", [128, 512], F32)
    epsb = P.sb("epsb", [128, 1], F32)
    P.memset(epsb[:], RMS_EPS, (epsb,))
    st = [P.sb("st", [128, 4], F32) for _ in range(2)]
    tm = [P.sb("tm", [128, 4, 16], F32) for _ in range(2)]
    okr = [P.sb("okr", [128, 128], BF16) for _ in range(2)]
    for o_ in okr:
        P.memset(o_[:], 0.0, (o_,), eng="pool")

    def blk_ep(tt, j, nb, n0, ns, ps):
        r = rb[tt % 2]
        evac(P, nb + 1, r[:, n0:n0 + ns], ps[:, 0:ns], (ps,), (r,))

    def row_ep(tt, j, t0):
        g = t0 // 512
        r, o, s, t = rb[tt % 2], ob[tt % 2], st[tt % 2], tm[tt % 2]
        for idx, (c0, cn) in enumerate(((0, 512), (512, 256))):
            P.act(junk[:, 0:cn], r[:, c0:c0 + cn], AF.Square, (r,), (junk, s), accum_out=s[:, idx:idx + 1])
            P.act(s[:, 2 + idx:3 + idx], s[:, idx:idx + 1], AF.Sqrt, (s, epsb), (s,), bias=epsb[:, 0:1], scale=1.0 / cn)
            P.op("dve", lambda e, idx=idx: e.reciprocal(s[:, 2 + idx:3 + idx], s[:, 2 + idx:3 + idx]), (s,), (s,))
            P.stt(o[:, c0:c0 + cn], r[:, c0:c0 + cn], s[:, 2 + idx:3 + idx], gn[:, c0:c0 + cn], ALU.mult, ALU.mult,
                  (r, s, gn), (o,))
        c, sn_ = cs[g % 2][:, j, :], sn[g % 2][:, j, :]
        x1, x2 = r[:, 768:784], r[:, 784:800]
        P.tt(t[:, 0, :], x1, c, ALU.mult, (r, cs[g % 2]), (t,))
        P.tt(t[:, 1, :], x2, sn_, ALU.mult, (r, sn[g % 2]), (t,))
        P.tt(o[:, 768:784], t[:, 0, :], t[:, 1, :], ALU.subtract, (t,), (o,))
        P.tt(t[:, 2, :], x1, sn_, ALU.mult, (r, sn[g % 2]), (t,))
        P.tt(t[:, 3, :], x2, c, ALU.mult, (r, cs[g % 2]), (t,))
        P.tt(o[:, 784:800], t[:, 2, :], t[:, 3, :], ALU.add, (t,), (o,))
        P.dma("pool", Y[t0:t0 + 128, :], o[:], (o,), ())
        P.copy(okr[tt % 2][:, 64:96], o[:, 768:800], (o,), (okr[tt % 2],), eng="pool")
        P.dma("pool", YKR[t0:t0 + 128, :], okr[tt % 2][:], (okr[tt % 2],), ())

    L.pre_group = lambda g, t0, gs: rope_loads(P, cs, sn, COS, SIN, g, t0, gs)
    L.blk_ep, L.row_ep = blk_ep, row_ep
    L.run()
    P.end_stage()


def stage_mla_a2q(P, X, W, Y, COS, SIN, T):
    SC = 96 ** -0.5
    L = Lin(P, X, W, T, 512, 1536)
    ob = [P.sb("ob", [128, 2048], BF16) for _ in range(2)]
    for o_ in ob:
        P.memset(o_[:], 0.0, (o_,), eng="pool")
    rb = [P.sb("rb", [128, 1536], F32) for _ in range(2)]
    cs = [P.sb("cs", [128, 4, 16], F32) for _ in range(2)]
    sn = [P.sb("sn", [128, 4, 16], F32) for _ in range(2)]
    tm = [P.sb("tm", [128, 4, 16, 16], F32) for _ in range(2)]

    def blk_ep(tt, j, nb, n0, ns, ps):
        r = rb[tt % 2]
        evac(P, nb, r[:, n0:n0 + ns], ps[:, 0:ns], (ps,), (r,))

    def row_ep(tt, j, t0):
        g = t0 // 512
        r, o, t = rb[tt % 2], ob[tt % 2], tm[tt % 2]
        rv = r[:, :].rearrange("p (h d) -> p h d", h=16)
        ov = o[:, :].rearrange("p (h d) -> p h d", h=16)
        P.act(ov[:, :, 0:64], rv[:, :, 0:64], AF.Copy, (r,), (o,), scale=SC)
        c = bcast_mid(cs[g % 2][:, j, :], 16)
        s = bcast_mid(sn[g % 2][:, j, :], 16)
        x1, x2 = rv[:, :, 64:80], rv[:, :, 80:96]
        P.tt(t[:, 0], x1, c, ALU.mult, (r, cs[g % 2]), (t,))
        P.tt(t[:, 1], x2, s, ALU.mult, (r, sn[g % 2]), (t,))
        P.tt(ov[:, :, 64:80], t[:, 0], t[:, 1], ALU.subtract, (t,), (o,))
        P.tt(t[:, 2], x1, s, ALU.mult, (r, sn[g % 2]), (t,))
        P.tt(t[:, 3], x2, c, ALU.mult, (r, cs[g % 2]), (t,))
        P.tt(ov[:, :, 80:96], t[:, 2], t[:, 3], ALU.add, (t,), (o,))
        P.dma("pool", Y[t0:t0 + 128, :], o[:], (o,), ())

    L.pre_group = lambda g, t0, gs: rope_loads(P, cs, sn, COS, SIN, g, t0, gs)
    L.blk_ep, L.row_ep = blk_ep, row_ep
    L.run()
    P.end_stage()


def stage_swiglu(P, X, W, YT, T):
    F = 2816
    FC = F // 128
    Wv = W.rearrange("(k p) n -> p k n", p=128)
    wt = []
    for k in range(8):
        w = P.sb("w", [128, 2 * F], BF16)
        P.dma("pool", w[:], Wv[:, k, :], (), (w,))
        wt.append(w)
    xb = [P.sb("xg", [128, 8, 512], BF16) for _ in range(2)]
    psg = [P.ps("psg", [128, 512]) for _ in range(2)]
    psu = [P.ps("psu", [128, 512]) for _ in range(2)]
    sg = [P.sb("sg", [128, 512], F32) for _ in range(2)]
    ob = [P.sb("ob", [128, 512], BF16) for _ in range(3)]
    groups = tok_groups(T)

    def load(g):
        t0, gs = groups[g]
        for k in range(8):
            P.dma_t("sp", xb[g % 2][:, k, 0:gs], X[t0:t0 + gs, k * 128:(k + 1) * 128], (), (xb[g % 2],))

    load(0)
    n = 0
    for g, (t0, gs) in enumerate(groups):
        if g + 1 < len(groups):
            load(g + 1)
        xg = xb[g % 2]
        for fc in range(FC):
            pg, pu, s_, o = psg[n % 2], psu[n % 2], sg[n % 2], ob[n % 3]
            for k in range(8):
                P.mm(pg[:, 0:gs], wt[k][:, fc * 128:(fc + 1) * 128], xg[:, k, 0:gs], k == 0, k == 7, (wt[k], xg), (pg,))
            for k in range(8):
                P.mm(pu[:, 0:gs], wt[k][:, F + fc * 128:F + (fc + 1) * 128], xg[:, k, 0:gs], k == 0, k == 7, (wt[k], xg), (pu,))
            P.act(s_[:, 0:gs], pg[:, 0:gs], AF.Silu, (pg,), (s_,))
            P.tt(o[:, 0:gs], pu[:, 0:gs], s_[:, 0:gs], ALU.mult, (pu, s_), (o,))
            P.dma("pool", YT[fc * 128:(fc + 1) * 128, t0:t0 + gs], o[:, 0:gs], (o,), ())
            n += 1
    P.end_stage()


def stage_resid_ln(P, X, W, H, HO, AO, vrows, T, K, x_fm=False):
    L = Lin(P, X, W, T, K, 1024, x_fm=x_fm)
    want_a = AO is not None
    vec = [P.sb("vec", [128, 1024], F32) for _ in range(8 if want_a else 4)]
    for i in range(len(vec)):
        bc_load(P, vec[i], vrows[i], 1024)
    if want_a:
        for i in (4, 6):
            P.ts(vec[i][:], vec[i][:], 1.0, None, ALU.add, None, (vec[i],), (vec[i],))
    hb = [P.sb("hb", [128, 4, 1024], F32) for _ in range(2)]
    zb = [P.sb("zb", [128, 1024], F32) for _ in range(3)]
    ho = [P.sb("ho", [128, 1024], F32) for _ in range(3)]
    ao = [P.sb("ao", [128, 1024], BF16) for _ in range(2)]
    at = [P.sb("at", [128, 1024], F32) for _ in range(2)]
    st = [P.sb("st", [128, 2, 6], F32) for _ in range(3)]
    mv = [P.sb("mv", [128, 4], F32) for _ in range(3)]
    epsb = P.sb("epsb", [128, 1], F32)
    P.memset(epsb[:], LN_EPS, (epsb,))

    def pre_group(g, t0, gs):
        nj = gs // 128
        P.dma("sp", hb[g % 2][:, 0:nj, :], H[t0:t0 + gs, :].rearrange("(j p) f -> p j f", p=128), (), (hb[g % 2],))

    def blk_ep(tt, j, nb, n0, ns, ps):
        z = zb[tt % 3]
        gate = vec[1] if tt < NCT_ else vec[0]
        P.tt(z[:, n0:n0 + ns], ps[:, 0:ns], gate[:, n0:n0 + ns], ALU.mult, (ps, gate), (z,))

    def phase_b(tt, t0):
        z, o, m = zb[tt % 3], ho[tt % 3], mv[tt % 3]
        P.op("dve", lambda e: e.reciprocal(m[:, 2:3], m[:, 2:3]), (m,), (m,))
        P.ts(m[:, 3:4], m[:, 0:1], -1.0, m[:, 2:3], ALU.mult, ALU.mult, (m,), (m,))
        P.act(z[:], z[:], AF.Identity, (z, m), (z,), bias=m[:, 3:4], scale=m[:, 2:3])
        P.tt(z[:], z[:], vec[2][:], ALU.mult, (z, vec[2]), (z,), eng="pool")
        P.tt(o[:], z[:], vec[3][:], ALU.add, (z, vec[3]), (o,), eng="pool")
        P.dma("pool", HO[t0:t0 + 128, :], o[:], (o,), ())

    def phase_c(tt, t0):
        if not want_a:
            return
        o, a = ho[tt % 3], ao[tt % 2]
        sc, sh = (vec[4], vec[5]) if tt >= NCT_ else (vec[6], vec[7])
        a_t = at[tt % 2]
        P.tt(a_t[:], o[:], sc[:], ALU.mult, (o, sc), (a_t,))
        P.tt(a[:], a_t[:], sh[:], ALU.add, (a_t, sh), (a,))
        P.dma("pool", AO[t0:t0 + 128, :], a[:], (a,), ())

    pend = []

    def row_ep(tt, j, t0):
        g = t0 // 512
        z, h, s, m = zb[tt % 3], hb[g % 2], st[tt % 3], mv[tt % 3]
        P.stt(z[:], h[:, j, :], ALPHA, z[:], ALU.mult, ALU.add, (h, z), (z,))
        for c in range(2):
            P.op("dve", lambda e, c=c: e.bn_stats(s[:, c, :], z[:, c * 512:(c + 1) * 512]), (z,), (s,))
        P.op("dve", lambda e: e.bn_aggr(m[:, 0:2], s[:, :, :].rearrange("p a b -> p (a b)")), (s,), (m,))
        P.act(m[:, 2:3], m[:, 1:2], AF.Sqrt, (m, epsb), (m,), bias=epsb[:, 0:1], scale=1.0)
        pend.append((tt, t0))
        if len(pend) >= 2:
            phase_b(*pend[-2])
        if len(pend) >= 3:
            phase_c(*pend[-3])

    def finish_ep():
        n = len(pend)
        phase_b(*pend[-1])
        if n >= 2:
            phase_c(*pend[-2])
        phase_c(*pend[-1])

    L.pre_group, L.blk_ep, L.row_ep, L.finish_ep = pre_group, blk_ep, row_ep, finish_ep
    L.run()
    P.end_stage()


def stage_mod0(P, X, vrows, AO, T):
    vec = [P.sb("vec", [128, 1024], F32) for _ in range(4)]
    for i in range(4):
        bc_load(P, vec[i], vrows[i], 1024)
    for i in (0, 2):
        P.ts(vec[i][:], vec[i][:], 1.0, None, ALU.add, None, (vec[i],), (vec[i],))
    xb = [P.sb("xb", [128, 1024], F32) for _ in range(3)]
    at = [P.sb("at", [128, 1024], F32) for _ in range(2)]
    ao = [P.sb("ao", [128, 1024], BF16) for _ in range(2)]
    for tt in range(T // 128):
        x, a_t, a = xb[tt % 3], at[tt % 2], ao[tt % 2]
        sc, sh = (vec[0], vec[1]) if tt >= NCT_ else (vec[2], vec[3])
        P.dma("sp", x[:], X[tt * 128:(tt + 1) * 128, :], (), (x,))
        eng = "dve" if tt % 2 == 0 else "pool"
        P.tt(a_t[:], x[:], sc[:], ALU.mult, (x, sc), (a_t,), eng=eng)
        P.tt(a[:], a_t[:], sh[:], ALU.add, (a_t, sh), (a,), eng=eng)
        P.dma("pool", AO[tt * 128:(tt + 1) * 128, :], a[:], (a,), ())
    P.end_stage()


def stage_k0(P, condT, ADA_W, ADA_B, MOD):
    ct = P.sb("ct", [128, 8, 2], F32)
    cs = P.sb("cs", [128, 8, 2], F32)
    ones = P.sb("ones", [1, 2], F32)
    P.dma("sp", ct[:], condT, (), (ct,))
    P.memset(ones[:], 1.0, (ones,))
    P.act(cs[:], ct[:], AF.Silu, (ct,), (cs,))
    wb = [P.sb("w", [128, 8, 512], F32) for _ in range(3)]
    bb = [P.sb("b", [1, 512], F32) for _ in range(3)]
    ob = [P.sb("o", [2, 512], F32) for _ in range(3)]
    pss = [P.ps("ps", [2, 512]) for _ in range(2)]
    n = 0
    for i in range(4):
        Wv = ADA_W[i].rearrange("(k p) n -> p k n", p=128)
        for nb in range(12):
            w, b, o, ps = wb[n % 3], bb[n % 3], ob[n % 3], pss[n % 2]
            sl = slice(nb * 512, (nb + 1) * 512)
            P.dma("sp", w[:], Wv[:, :, sl], (), (w,))
            P.dma("sp", b[:], ADA_B[i:i + 1, sl], (), (b,))
            for k in range(8):
                P.mm(ps[:], cs[:, k, :], w[:, k, :], k == 0, False, (cs, w), (ps,))
            P.mm(ps[:], ones[:], b[:], False, True, (ones, b), (ps,))
            P.copy(o[:], ps[:], (ps,), (o,))
            P.dma("pool", MOD[i, :, sl], o[:], (o,), ())
            n += 1
    P.end_stage()


RCH = 1408


def stage_mla_b(P, YQ, YKV, YKR, IDF, O, NHEADS=16, DQK=96, DV=64):
    NK = TG_
    NKT = NK // 128
    NQ = T_
    qb = [P.sb("q", [128, NQ], BF16) for _ in range(2)]
    kb = [P.sb("k", [128, NK], BF16) for _ in range(2)]
    krt = P.sb("krt", [128, NK], BF16)
    vb = [P.sb("v", [128, NKT, DV + 1], BF16) for _ in range(2)]
    idf = P.sb("idf", [128, 128], F32)
    P.dma("sp", idf[:], IDF, (), (idf,))
    pss = [P.ps("pss", [128, 1024]) for _ in range(2)]
    pso = [P.ps("pso", [128, 512]) for _ in range(2)]
    pst = P.ps("pst", [128, 4, DV + 1])
    pt = [P.sb("pt", [128, 1024], BF16) for _ in range(2)]
    osb = [P.sb("osb", [DV + 1, 512], F32) for _ in range(2)]
    rc = [P.sb("rc", [128, 4, 1], F32) for _ in range(2)]
    ob = [P.sb("ob", [128, 4, DV], BF16) for _ in range(2)]
    for r0 in range(0, NK, RCH):
        P.dma_t("sp", krt[:, r0:r0 + RCH], YKR[r0:r0 + RCH, :], (), (krt,))

    def load(h):
        q, k, v = qb[h % 2], kb[h % 2], vb[h % 2]
        for r0 in range(0, NQ, RCH):
            P.dma_t("sp", q[:, r0:r0 + RCH], YQ[r0:r0 + RCH, h * 128:(h + 1) * 128], (), (q,))
        for r0 in range(0, NK, RCH):
            P.dma_t("sp", k[:, r0:r0 + RCH], YKV[r0:r0 + RCH, h * 128:(h + 1) * 128], (), (k,))
        P.copy(k[64:96, :], krt[64:96, :], (krt, k), (k,), eng="pool")
        P.dma("sp", v[:, :, 0:DV], YKV[:, h * 128 + 64:(h + 1) * 128].rearrange("(t p) d -> p t d", p=128), (), (v,))
        P.memset(v[:, :, DV:DV + 1], 1.0, (v,), eng="pool")

    ctx_tiles = [0, T_ // 128]
    all_tiles = list(range(NKT))
    nlat = (NQ - 128) // 512
    qgroups = [([(0, 128)], ctx_tiles)] + [([(128 + (2 * g + b2) * 512, 512) for b2 in range(2)], all_tiles) for g in range(nlat // 2)]
    load(0)
    ctr = 0
    bi = 0
    for h in range(NHEADS):
        if h + 1 < NHEADS:
            load(h + 1)
        q, k, v = qb[h % 2], kb[h % 2], vb[h % 2]
        for (blocks, ktl) in qgroups:
            nb = len(blocks)
            qn = blocks[0][1]
            nkt = len(ktl)

            def S(i, c):
                kt = ktl[i]
                for b2, (q0, _) in enumerate(blocks):
                    P.mm(pss[c % 2][:, b2 * 512:b2 * 512 + qn], k[0:DQK, kt * 128:(kt + 1) * 128], q[0:DQK, q0:q0 + qn], True, True,
                         (k, q), (pss[c % 2],))

            S(0, ctr)
            for i in range(nkt):
                kt = ktl[i]
                c = ctr + i
                if i + 1 < nkt:
                    S(i + 1, c + 1)
                sv = pss[c % 2][:, :].rearrange("p (a b) -> p a b", a=2)[:, 0:nb, 0:qn]
                pv = pt[c % 2][:, :].rearrange("p (a b) -> p a b", a=2)[:, 0:nb, 0:qn]
                P.act(pv, sv, AF.Exp, (pss[c % 2],), (pt[c % 2],))
                for b2 in range(nb):
                    P.mm(pso[b2][0:DV + 1, 0:qn], v[:, kt, :], pt[c % 2][:, b2 * 512:b2 * 512 + qn], i == 0, i == nkt - 1,
                         (pt[c % 2], v), (pso[b2],))
            ctr += nkt
            nqi = qn // 128
            for b2, (q0, _) in enumerate(blocks):
                po = pso[b2]
                r, o, os_ = rc[bi % 2], ob[bi % 2], osb[bi % 2]
                P.copy(os_[:, 0:qn], po[0:DV + 1, 0:qn], (po,), (os_,))
                for qi in range(nqi):
                    P.transpose(pst[:, qi, :], os_[:, qi * 128:(qi + 1) * 128], idf[0:DV + 1, 0:DV + 1], (os_, idf), (pst,))
                P.op("dve", lambda e, r=r, nqi=nqi: e.reciprocal(r[:, 0:nqi, :], pst[:, 0:nqi, DV:DV + 1]), (pst,), (r,))
                P.tt(o[:, 0:nqi, :], pst[:, 0:nqi, 0:DV], r[:, 0:nqi, :].broadcast_to([128, nqi, DV]), ALU.mult, (pst, r), (o,))
                P.dma("sp", O[q0:q0 + qn, h * DV:(h + 1) * DV].rearrange("(j p) d -> p j d", p=128), o[:, 0:nqi, :], (o,), ())
                bi += 1
    P.end_stage()


def stage_na_b(P, Y, MI, MB, MKI, MKB, O, NHEADS=8, NLT=64, DH=64):
    NTOK = TG_
    NT = NTOK // 128
    HW = NHEADS * DH
    SC = DH ** -0.5
    gt = lambda c: grow(c) // 128
    qb = [P.sb("q", [128, NTOK], BF16) for _ in range(2)]
    kb = [P.sb("k", [128, NTOK], BF16) for _ in range(2)]
    vb = [P.sb("v", [128, NT, DH + 1], BF16) for _ in range(2)]
    mi = [P.sb("mi", [128, 5, 128], F32) for _ in range(2)]
    mb = [P.sb("mb", [128, 7, 128], F32) for _ in range(2)]
    mki = P.sb("mki", [128, 5, 128], F32)
    mkb = P.sb("mkb", [128, 7, 128], F32)
    P.dma("sp", mki[:], MKI, (), (mki,))
    P.dma("sp", mkb[:], MKB, (), (mkb,))
    psa = [P.ps("psa", [128, 4, 128]) for _ in range(2)]
    psb = [P.ps("psb", [128, 4, 128]) for _ in range(2)]
    pso = [P.ps("pso", [128, 512]) for _ in range(2)]
    sa = [P.sb("sa", [128, 5, 128], F32) for _ in range(2)]
    pt = [P.sb("pt", [128, 8, 128], BF16) for _ in range(2)]
    rc = [P.sb("rc", [128, 1], F32) for _ in range(2)]
    ob = [P.sb("ob", [128, DH], BF16) for _ in range(2)]

    def load(h):
        v = vb[h % 2]
        if h % 2 == 0:
            hp = h // 2
            q, k = qb[hp % 2], kb[hp % 2]
            for r0 in range(0, NTOK, RCH):
                P.dma_t("sp", q[:, r0:r0 + RCH], Y[r0:r0 + RCH, hp * 128:(hp + 1) * 128], (), (q,))
                P.dma_t("sp", k[:, r0:r0 + RCH], Y[r0:r0 + RCH, HW + hp * 128:HW + (hp + 1) * 128], (), (k,))
        P.dma("sp", v[:, :, 0:DH], Y[:, 2 * HW + h * DH:2 * HW + (h + 1) * DH].rearrange("(t p) d -> p t d", p=128), (), (v,))
        P.memset(v[:, :, DH:DH + 1], 1.0, (v,), eng="pool")
        P.dma("sp", mi[h % 2][:], MI[h], (), (mi[h % 2],))
        P.dma("sp", mb[h % 2][:], MB[h], (), (mb[h % 2],))
        P.tt(mi[h % 2][:], mi[h % 2][:], mki[:], ALU.add, (mi[h % 2], mki), (mi[h % 2],), eng="pool")
        P.tt(mb[h % 2][:], mb[h % 2][:], mkb[:], ALU.add, (mb[h % 2], mkb), (mb[h % 2],), eng="pool")

    load(0)
    if NHEADS > 1:
        load(1)
    ctx = [gt(0), gt(1)]
    units = []
    for h in range(NHEADS):
        for fq in range(NLT + 2):
            units.append((h, fq))
    pend = []

    def prep(h, fq, u):
        p0 = (h % 2) * DH
        q, k, v = qb[(h // 2) % 2], kb[(h // 2) % 2], vb[h % 2]
        gq = gt(fq)
        if fq >= 2:
            qt = fq - 2
            if 2 <= qt <= NLT - 3:
                lk = list(range(qt - 2, qt + 3))
                mask = mi[h % 2]
                m0 = 0
            else:
                k0 = 0 if qt < 2 else NLT - 4
                lk = list(range(k0, k0 + 4))
                mask = mb[h % 2]
                m0 = (k0 - qt) + 3
            kts = [gt(x + 2) for x in lk]
        else:
            kts, mask, m0 = [], None, 0
        pa, pb, s, p_ = psa[u % 2], psb[u % 2], sa[u % 2], pt[u % 2]
        qs = q[p0:p0 + DH, gq * 128:(gq + 1) * 128]
        nl = len(kts)
        for i, kt in enumerate(kts[:4]):
            P.mm(pa[:, i, :], k[p0:p0 + DH, kt * 128:(kt + 1) * 128], qs, True, True, (k, q), (pa,))
        rest = kts[4:] + ctx
        for i, kt in enumerate(rest):
            P.mm(pb[:, i, :], k[p0:p0 + DH, kt * 128:(kt + 1) * 128], qs, True, True, (k, q), (pb,))
        n4 = min(nl, 4)
        if nl:
            P.stt(s[:, 0:n4, :], pa[:, 0:n4, :], SC, mask[:, m0:m0 + n4, :], ALU.mult, ALU.add, (pa, mask), (s,))
            if nl > 4:
                P.stt(s[:, 4:5, :], pb[:, 0:1, :], SC, mask[:, m0 + 4:m0 + 5, :], ALU.mult, ALU.add, (pb, mask), (s,))
            P.act(p_[:, 0:nl, :], s[:, 0:nl, :], AF.Exp, (s,), (p_,))
        nr = len(rest) - (nl - n4)
        P.act(p_[:, nl:nl + nr, :], pb[:, nl - n4:nl - n4 + nr, :], AF.Exp, (pb,), (p_,), scale=SC)
        return (h, gq, kts + ctx, u)

    def fin(h, gq, allk, u):
        v = vb[h % 2]
        p_, r, o, po = pt[u % 2], rc[u % 2], ob[u % 2], pso[u % 2]
        for i, kt in enumerate(allk):
            P.mm(po[:, 0:DH + 1], p_[:, i, :], v[:, kt, :], i == 0, i == len(allk) - 1, (p_, v), (po,))
        P.op("dve", lambda e, r=r, po=po: e.reciprocal(r[:], po[:, DH:DH + 1]), (po,), (r,))
        P.ts(o[:], po[:, 0:DH], r[:, 0:1], None, ALU.mult, None, (po, r), (o,))
        P.dma("pool", O[gq * 128:(gq + 1) * 128, h * DH:(h + 1) * DH], o[:], (o,), ())

    nxt = prep(units[0][0], units[0][1], 0)
    for u, (h, fq) in enumerate(units):
        cur = nxt
        if u + 1 < len(units):
            h2, fq2 = units[u + 1]
            nxt = prep(h2, fq2, u + 1)
        fin(*cur)
        if (u + 1 == len(units) or units[u + 1][0] != h) and h + 2 < NHEADS:
            load(h + 2)
    P.end_stage()


def na_mask_tables(rpb):
    krl = (np.arange(128) // 64)[:, None, None]
    kc = (np.arange(128) % 64)[:, None, None]
    qrl = (np.arange(128) // 64)[None, None, :]
    qc = (np.arange(128) % 64)[None, None, :]
    ws = np.clip(qc - 8, 0, 48)
    col_ok = (kc >= ws) & (kc < ws + 16)
    ci = np.clip(kc - qc + 15, 0, 30)

    def tab(ds, row_window):
        d = np.asarray(ds)[None, :, None]
        rel = 2 * d + krl - qrl
        ok = col_ok & np.ones_like(rel, bool)
        if row_window:
            ok = ok & (rel >= -4) & (rel <= 3)
        ri = np.clip(rel + 7, 0, 14)
        cib = np.broadcast_to(ci, rel.shape)
        g = rpb[:, ri, cib]
        mk = np.where(ok, 0.0, -1e30).astype(np.float32)
        return np.ascontiguousarray(g.astype(np.float32)), np.ascontiguousarray(mk)

    MI, MKI = tab([-2, -1, 0, 1, 2], True)
    MB, MKB = tab([-3, -2, -1, 0, 1, 2, 3], False)
    return MI, MB, MKI, MKB


def stage_gla(P, kind, Y, U, OF, PAR, NGsrc, IDN, TRI):
    ret = kind == "ret"
    T = TG_
    NI = 2 if ret else 4
    DKC = 2 if ret else 1
    DK = 128 * DKC
    DV = 512 if ret else 128
    CQ = 1.0 if ret else 128 ** -0.5
    CK = 1.0 / 16 if ret else 1.0
    C = 128
    NCH = T // C
    scr = [[Buf("scr", None) for _ in range(NCH)] for _ in range(NI)]

    idn = P.sb("idn", [128, 128], BF16)
    tri = [P.sb("tri", [128, 128], F32) for _ in range(2)]
    P.dma("sp", idn[:], IDN, (), (idn,))
    for d in range(2):
        P.dma("sp", tri[d][:], TRI[d], (), (tri[d],))
    ones = P.sb("ones", [128, C], F32)
    P.memset(ones[:], 1.0, (ones,))
    epsb = P.sb("epsb", [128, 1], F32)
    P.memset(epsb[:], RMS_EPS, (epsb,))
    oneb = P.sb("oneb", [128, 1], F32)
    P.memset(oneb[:], 1.0, (oneb,))
    if ret:
        dec = P.sb("dec", [128, 4], F32)
        P.dma("sp", dec[:], PAR, (), (dec,))
        P.act(dec[:], dec[:], AF.Exp, (dec,), (dec,))
        P.ts(dec[:], dec[:], -1.0, None, ALU.mult, None, (dec,), (dec,))
    else:
        lbr = P.sb("lbr", [128, NI, 4], F32)
        lb = P.sb("lb", [128, NI, 4], F32)
        ng = P.sb("ng", [128, DV], F32)
        P.dma("sp", lbr[:], PAR, (), (lbr,))
        P.dma("sp", ng[:], NGsrc, (), (ng,))
        P.act(lbr[:], lbr[:], AF.Exp, (lbr,), (lbr,))
        for it in range(NI):
            P.op("dve", lambda e, it=it: e.reduce_sum(lb[:, it, 2:3], lbr[:, it, :], axis=AX.X), (lbr,), (lb,))
            P.op("dve", lambda e, it=it: e.reciprocal(lb[:, it, 3:4], lb[:, it, 2:3]), (lb,), (lb,))
            P.tt(lb[:, it, 1:2], lbr[:, it, 0:1], lb[:, it, 3:4], ALU.mult, (lbr, lb), (lb,))
            P.ts(lb[:, it, 0:1], lb[:, it, 1:2], -1.0, 1.0, ALU.mult, ALU.add, (lb,), (lb,))

    qg = [P.sb("qg", [128, DKC, 512], BF16) for _ in range(2)]
    kg = [P.sb("kg", [128, DKC, 512], BF16) for _ in range(2)]
    vg = [P.sb("vg", [128, 4, DV], BF16) for _ in range(2)]
    sgg = [P.sb("sgg", [128, 4, DV], BF16) for _ in range(2)]
    ofg = [P.sb("ofg", [128, 4, DV], F32) for _ in range(2)]
    S = [P.sb("S", [128, DV], F32) for _ in range(DKC)]
    Sb = [[P.sb("Sb", [128, DV], BF16) for _ in range(DKC)] for _ in range(3)]
    lgT = P.sb("lgT", [128, C], F32)
    csT = P.sb("csT", [128, C], F32)
    bT = P.sb("bT", [128, C], F32)
    NDT = 1 if ret else 2
    eb = [P.sb("eb", [128, C], F32) for _ in range(NDT)]
    enb = [P.sb("enb", [128, C], F32) for _ in range(NDT)]
    ekk = [P.sb("ekk", [128, C], F32) for _ in range(NDT)]
    ebl = [P.sb("ebl", [128, 1], F32) for _ in range(NDT)]
    GW = 512
    qdG = [P.sb("qdG", [128, DKC, GW], BF16) for _ in range(2)]
    kdG = [P.sb("kdG", [128, DKC, GW], BF16) for _ in range(2)]
    kkTG = [P.sb("kkTG", [128, DKC, GW], BF16) for _ in range(2)]
    if ret:
        eb4 = P.sb("eb4", [128, GW], F32)
        enb4 = P.sb("enb4", [128, GW], F32)
        ekk4 = P.sb("ekk4", [128, GW], F32)
    else:
        tG = [P.sb("tG", [128, GW], F32) for _ in range(2)]
        kfG = [P.sb("kfG", [128, GW], F32) for _ in range(2)]
        lgG = [P.sb("lgG", [128, GW], F32) for _ in range(2)]
        csG = [P.sb("csG", [128, GW], F32) for _ in range(2)]
        bG = [P.sb("bG", [128, GW], F32) for _ in range(2)]
        dG = [P.sb("dG", [128, GW], F32) for _ in range(2)]
        ebG = [P.sb("ebG", [128, GW], F32) for _ in range(2)]
        enbG = [P.sb("enbG", [128, GW], F32) for _ in range(2)]
        ekkG = [P.sb("ekkG", [128, GW], F32) for _ in range(2)]
        eblG = [P.sb("eblG", [128, 4], F32) for _ in range(2)]
        maskG = P.sb("maskG", [128, GW], F32)
        P.memset(maskG[:], 1.0, (maskG,))
        for c4 in range(4):
            P.memset(maskG[:, c4 * 128:c4 * 128 + 1], 0.0, (maskG,))
    kk = [P.sb("kk", [128, DK], BF16) for _ in range(3)]
    att = [P.sb("att", [128, C], BF16) for _ in range(3)]
    osb = [P.sb("osb", [128, DV], F32) for _ in range(2)]
    junk = P.sb("junk", [128, DV], F32)
    st = [P.sb("st", [128, 2], F32) for _ in range(2)]
    ub = [P.sb("ub", [128, DV], BF16) for _ in range(2)]
    ps_t = P.ps("ps_t", [128, C], BF16)
    ps_a = [P.ps("ps_a", [128, 512]) for _ in range(2)]
    ps_o = [P.ps("ps_o", [128, 512]) for _ in range(2)]
    ps_d = [P.ps("ps_d", [128, 512]) for _ in range(DKC)]

    def decay_tiles(lg_ap, lg_reads, dirn, i):
        P.op("dve", lambda e: e.tensor_tensor_scan(csT[:], ones[:], lg_ap, 0.0, ALU.mult, ALU.add), (ones,) + lg_reads, (csT,))
        tot = csT[:, C - 1:C]
        if dirn == 0:
            b, breads = csT, (csT,)
        else:
            P.ts(bT[:], csT[:], tot, None, ALU.subtract, None, (csT,), (bT,))
            P.stt(bT[:], bT[:], -1.0, lg_ap, ALU.mult, ALU.add, (bT,) + lg_reads, (bT,))
            b, breads = bT, (bT, csT)
        P.act(eb[i][:], b[:], AF.Exp, breads, (eb[i],))
        P.act(enb[i][:], b[:], AF.Exp, breads, (enb[i],), scale=-1.0)
        P.act(ekk[i][:], b[:], AF.Exp, breads, (ekk[i],), scale=-1.0, bias=tot)
        P.act(ebl[i][:], tot, AF.Exp, (csT,), (ebl[i],))

    groups = [[0, 1]] + [list(range(2 + 4 * g, 6 + 4 * g)) for g in range(16)]

    def runs(chs):
        out = []
        for c in chs:
            if out and grow(c) == out[-1][0] + out[-1][1] * 128:
                out[-1][1] += 1
            else:
                out.append([grow(c), 1])
        return out

    cc = 0
    for it in range(NI):
        if ret:
            qcol, kcols, vcol, sgcol = it * 256, [512 + it * 256] * 2, 1024 + it * 512, 2048 + it * 512
        else:
            qcol, kcols, vcol, sgcol = it * 128, [512 + it * 128, 1024 + it * 128], 1536 + it * 128, 2048 + it * 128
        for dirn in range(2):
            for c in range(DKC):
                P.memset(S[c][:], 0.0, (S[c],))
                P.memset(Sb[2][c][:], 0.0, (Sb[2][c],), eng="pool")
            if ret:
                P.copy(lgT[:], dec[:, dirn * 2 + it:dirn * 2 + it + 1].broadcast_to([128, C]), (dec,), (lgT,))
                decay_tiles(lgT[:], (lgT,), dirn, 0)
                for src, dst in ((eb[0], eb4), (enb[0], enb4), (ekk[0], ekk4)):
                    P.copy(dst[:, :].rearrange("p (c t) -> p c t", t=C), src[:].unsqueeze(1).broadcast_to([128, 4, C]), (src,), (dst,))
            if dirn == 0:
                gorder = list(range(len(groups)))
            else:
                gorder = [0] + list(range(len(groups) - 1, 0, -1))
            kcol = kcols[dirn]

            def load(gi, n):
                j0 = 0
                for (r0, nch) in runs(groups[gi]):
                    gs = nch * 128
                    for c in range(DKC):
                        P.dma_t("sp", qg[n % 2][:, c, j0 * 128:j0 * 128 + gs], Y[r0:r0 + gs, qcol + c * 128:qcol + (c + 1) * 128], (), (qg[n % 2],))
                        P.dma_t("sp", kg[n % 2][:, c, j0 * 128:j0 * 128 + gs], Y[r0:r0 + gs, kcol + c * 128:kcol + (c + 1) * 128], (), (kg[n % 2],))
                    P.dma("sp", vg[n % 2][:, j0:j0 + nch, :], Y[r0:r0 + gs, vcol:vcol + DV].rearrange("(j p) d -> p j d", p=128), (), (vg[n % 2],))
                    if dirn == 1:
                        P.dma("sp", sgg[n % 2][:, j0:j0 + nch, :], Y[r0:r0 + gs, sgcol:sgcol + DV].rearrange("(j p) d -> p j d", p=128), (), (sgg[n % 2],))
                    j0 += nch
                if dirn == 1:
                    for j, ch in enumerate(groups[gi]):
                        P.dma("sp", ofg[n % 2][:, j, :], OF[grow(ch):grow(ch) + 128, it * DV:(it + 1) * DV], (scr[it][ch],), (ofg[n % 2],))

            seq = []
            for n, gi in enumerate(gorder):
                nj = len(groups[gi])
                for j in (range(nj) if dirn == 0 else range(nj - 1, -1, -1)):
                    seq.append((n, gi, j, groups[gi][j], len(seq)))

            def gprep(n, gi):
                nj = len(groups[gi])
                W = nj * C
                q_, k_ = qg[n % 2], kg[n % 2]
                g2 = n % 2
                if ret:
                    e3 = lambda t: bcast_mid(t[:, 0:W], DKC)
                    P.stt(qdG[g2][:, :, 0:W], q_[:, :, 0:W], CQ, e3(eb4), ALU.mult, ALU.mult, (q_, eb4), (qdG[g2],))
                    P.stt(kdG[g2][:, :, 0:W], k_[:, :, 0:W], CK, e3(enb4), ALU.mult, ALU.mult, (k_, enb4), (kdG[g2],))
                    P.stt(kkTG[g2][:, :, 0:W], k_[:, :, 0:W], CK, e3(ekk4), ALU.mult, ALU.mult, (k_, ekk4), (kkTG[g2],))
                    return
                t_, kf, lg, cs, b_, d_ = tG[g2], kfG[g2], lgG[g2], csG[g2], bG[g2], dG[g2]
                e_b, e_nb, e_kk, e_bl = ebG[g2], enbG[g2], ekkG[g2], eblG[g2]
                P.act(t_[:, 0:W], k_[:, 0, 0:W], AF.Exp, (k_,), (t_,), scale=-1.0)
                P.act(t_[:, 0:W], t_[:, 0:W], AF.Ln, (t_, oneb), (t_,), bias=oneb[:, 0:1], scale=1.0)
                P.act(t_[:, 0:W], t_[:, 0:W], AF.Exp, (t_,), (t_,), scale=-1.0)
                P.ts(t_[:, 0:W], t_[:, 0:W], lb[:, it, 1:2], lb[:, it, 0:1], ALU.mult, ALU.add, (t_, lb), (t_,))
                P.ts(kf[:, 0:W], t_[:, 0:W], -1.0, 1.0, ALU.mult, ALU.add, (t_,), (kf,))
                P.act(lg[:, 0:W], t_[:, 0:W], AF.Ln, (t_,), (lg,))
                P.op("dve", lambda e: e.tensor_tensor_scan(cs[:, 0:W], maskG[:, 0:W], lg[:, 0:W], 0.0, ALU.mult, ALU.add), (maskG, lg), (cs,))
                v3 = lambda t: t[:, 0:W].rearrange("p (c t) -> p c t", t=C)
                tot = v3(cs)[:, :, C - 1:C]
                totb = tot.broadcast_to([128, nj, C])
                if dirn == 0:
                    bsrc, breads = cs, (cs,)
                else:
                    P.tt(v3(b_), v3(cs), totb, ALU.subtract, (cs,), (b_,))
                    P.stt(b_[:, 0:W], b_[:, 0:W], -1.0, lg[:, 0:W], ALU.mult, ALU.add, (b_, lg), (b_,))
                    bsrc, breads = b_, (b_, cs)
                P.act(e_b[:, 0:W], bsrc[:, 0:W], AF.Exp, breads, (e_b,))
                P.act(e_nb[:, 0:W], bsrc[:, 0:W], AF.Exp, breads, (e_nb,), scale=-1.0)
                P.tt(v3(d_), totb, v3(bsrc), ALU.subtract, breads + (cs,), (d_,))
                P.act(e_kk[:, 0:W], d_[:, 0:W], AF.Exp, (d_,), (e_kk,))
                P.act(e_bl[:, 0:nj].unsqueeze(2), tot, AF.Exp, (cs,), (e_bl,))
                P.stt(qdG[g2][:, 0, 0:W], q_[:, 0, 0:W], CQ, e_b[:, 0:W], ALU.mult, ALU.mult, (q_, e_b), (qdG[g2],))
                P.stt(kdG[g2][:, 0, 0:W], kf[:, 0:W], CK, e_nb[:, 0:W], ALU.mult, ALU.mult, (kf, e_nb), (kdG[g2],))
                P.stt(kkTG[g2][:, 0, 0:W], kf[:, 0:W], CK, e_kk[:, 0:W], ALU.mult, ALU.mult, (kf, e_kk), (kkTG[g2],))

            def prep(n, gi, j, ch, i):
                nonlocal cc
                if i == 0 or seq[i - 1][0] != n:
                    gprep(n, gi)
                v_ = vg[n % 2]
                x = cc % 3
                cc += 1
                g2 = n % 2
                sl = slice(j * 128, (j + 1) * 128)
                qd_, kd_, kkT_ = qdG[g2], kdG[g2], kkTG[g2]
                kk_, att_ = kk[x], att[x]
                ebl_ap = ebl[0][:, 0:1] if ret else eblG[g2][:, j:j + 1]
                ebl_buf = ebl[0] if ret else eblG[g2]
                pa = ps_a[x % 2]
                for c in range(DKC):
                    P.mm(pa[:, 0:C], kd_[:, c, sl], qd_[:, c, sl], c == 0, c == DKC - 1, (kd_, qd_), (pa,))
                P.tt(att_[:], pa[:, 0:C], tri[dirn][:], ALU.mult, (pa, tri[dirn]), (att_,))
                for c in range(DKC):
                    P.transpose(ps_t[:], kkT_[:, c, sl], idn[:], (kkT_, idn), (ps_t,))
                    P.act(kk_[:, c * 128:(c + 1) * 128], ps_t[:], AF.Copy, (ps_t,), (kk_,))
                return (n, j, ch, x, i, ebl_ap, ebl_buf)

            def prepB(n, j, ch, x, i, ebl_ap, ebl_buf):
                v_ = vg[n % 2]
                kk_ = kk[x]
                for c in range(DKC):
                    P.mm(ps_d[c][:, 0:DV], kk_[:, c * 128:(c + 1) * 128], v_[:, j, :], True, True, (kk_, v_), (ps_d[c],))
                    P.stt(S[c][:], S[c][:], ebl_ap, ps_d[c][:, 0:DV], ALU.mult, ALU.add, (S[c], ebl_buf, ps_d[c]), (S[c],))
                    P.act(Sb[i % 3][c][:], S[c][:], AF.Copy, (S[c],), (Sb[i % 3][c],))
                return (n, j, ch, x, i)

            def state(n, j, ch, x, i):
                v_, sg_, of_ = vg[n % 2], sgg[n % 2], ofg[n % 2]
                qd_, att_ = qdG[n % 2], att[x]
                sl = slice(j * 128, (j + 1) * 128)
                Sprev = Sb[(i - 1) % 3]
                x = i % 2
                po = ps_o[x]
                P.mm(po[:, 0:DV], att_[:], v_[:, j, :], True, False, (att_, v_), (po,))
                for c in range(DKC):
                    P.mm(po[:, 0:DV], qd_[:, c, sl], Sprev[c][:], False, c == DKC - 1, (qd_, Sprev[c]), (po,))
                o_ = osb[x]
                if dirn == 0:
                    P.act(o_[:], po[:, 0:DV], AF.Copy, (po,), (o_,))
                    P.dma("pool", OF[grow(ch):grow(ch) + 128, it * DV:(it + 1) * DV], o_[:], (o_,), (scr[it][ch],))
                else:
                    s_, u_ = st[x], ub[x]
                    P.tt(o_[:], po[:, 0:DV], of_[:, j, :], ALU.add, (po, of_), (o_,))
                    P.act(junk[:], o_[:], AF.Square, (o_,), (junk, s_), accum_out=s_[:, 0:1])
                    P.act(s_[:, 1:2], s_[:, 0:1], AF.Ln, (s_, epsb), (s_,), bias=epsb[:, 0:1], scale=1.0 / DV)
                    P.act(s_[:, 1:2], s_[:, 1:2], AF.Exp, (s_,), (s_,), scale=-0.5)
                    if ret:
                        P.stt(u_[:], o_[:], s_[:, 1:2], sg_[:, j, :], ALU.mult, ALU.mult, (o_, s_, sg_), (u_,))
                    else:
                        P.stt(o_[:], o_[:], s_[:, 1:2], ng[:], ALU.mult, ALU.mult, (o_, s_, ng), (o_,))
                        P.tt(u_[:], o_[:], sg_[:, j, :], ALU.mult, (o_, sg_), (u_,), eng="pool")
                    P.dma("pool", U[grow(ch):grow(ch) + 128, it * DV:(it + 1) * DV], u_[:], (u_,), ())

            load(gorder[0], 0)
            if len(gorder) > 1:
                load(gorder[1], 1)
            infos = [None] * len(seq)
            infos[0] = prep(*seq[0])
            if len(seq) > 1:
                infos[1] = prep(*seq[1])
            prepB(*infos[0])
            for i in range(len(seq)):
                if i + 2 < len(seq):
                    infos[i + 2] = prep(*seq[i + 2])
                if i + 1 < len(seq):
                    prepB(*infos[i + 1])
                state(*infos[i][0:5])
                n_cur = seq[i][0]
                if (i + 1 == len(seq) or seq[i + 1][0] != n_cur) and n_cur + 2 < len(gorder):
                    load(gorder[n_cur + 2], n_cur + 2)
    P.end_stage()


PAIRS = [[0, 1], [2, 3], [4, 5], [6, 7]]


def stage_gather(P, SRC, GC, nrows, rc):
    for j in range(nrows // rc):
        P.collective("AllGather", PAIRS, SRC[j * rc:(j + 1) * rc, :], GC[j * 2 * rc:(j + 1) * 2 * rc, :])
    P.end_stage()


def gc_row(rc, r, t):
    return (t // rc) * 2 * rc + r * rc + (t % rc)


def stage_blend(P, OUT, nrows, segs, SEL):
    sel = P.sb("sel", [128, 2], F32)
    P.dma("sp", sel[:], SEL, (), (sel,))
    NB, LA = 4, 3
    ab = [[P.sb("ba", [128, w], BF16) for _ in range(NB)] for (_, w, _, _) in segs]
    bb = [[P.sb("bb", [128, w], BF16) for _ in range(NB)] for (_, w, _, _) in segs]
    ob = [[P.sb("bo", [128, w], BF16) for _ in range(NB)] for (_, w, _, _) in segs]
    jobs = [(t, si) for t in range(nrows // 128) for si in range(len(segs))]

    def load(n):
        t, si = jobs[n]
        c0, w, fa, fb = segs[si]
        a, b = ab[si][t % NB], bb[si][t % NB]
        P.dma("sp", a[:], fa(t * 128), (), (a,))
        P.dma("act", b[:], fb(t * 128), (), (b,))

    for n in range(min(LA, len(jobs))):
        load(n)
    for n, (t, si) in enumerate(jobs):
        if n + LA < len(jobs):
            load(n + LA)
        c0, w, fa, fb = segs[si]
        a, b, o = ab[si][t % NB], bb[si][t % NB], ob[si][t % NB]
        P.ts(a[:], a[:], sel[:, 0:1], None, ALU.mult, None, (a, sel), (a,))
        P.stt(o[:], b[:], sel[:, 1:2], a[:], ALU.mult, ALU.add, (b, sel, a), (o,))
        P.dma("pool", OUT[t * 128:(t + 1) * 128, c0:c0 + w], o[:], (o,), ())
    P.end_stage()


def head_select(P, GC, rc, M, segcols, SEL):
    segs = []
    c = 0
    for (g0, wt) in segcols:
        w = wt // 2

        def fa(r0, g0=g0, w=w, off=0):
            g = gc_row(rc, r0 // T_, r0 % T_)
            return GC[g:g + 128, g0 + off:g0 + off + w]

        segs.append((c, w, fa, (lambda r0, fa=fa, w=w: fa(r0, off=w))))
        c += w
    stage_blend(P, M, TG_, segs, SEL)


def token_select(P, GC, rc, L, w, SEL):
    segs = []
    for r in range(2):
        segs.append((r * w, w, (lambda r0, r=r: GC[gc_row(rc, r, r0):gc_row(rc, r, r0) + 128, :]),
                     (lambda r0, r=r: GC[gc_row(rc, r, T_ + r0):gc_row(rc, r, T_ + r0) + 128, :])))
    stage_blend(P, L, T_, segs, SEL)


def stage_reorder(P, GC, rc, G, nrows):
    for j in range(nrows // rc):
        for r in range(2):
            P.dma("sp", G[r * nrows + j * rc:r * nrows + (j + 1) * rc, :], GC[gc_row(rc, r, j * rc):gc_row(rc, r, j * rc) + rc, :], (), ())
    P.end_stage()


def build_fused():
    P = Prog()
    T, TG = T_, TG_
    f = lambda name, shape, dt=F32: P.din(name, shape, dt)
    scr = P.dscr

    XIN = f("XIN", [T, 1024])
    condT = f("condT", [128, 8, 2])
    SEL = f("SEL", [128, 2])
    ADA_W, ADA_B = f("ADA_W", [4, 1024, 6144]), f("ADA_B", [4, 6144])
    LN_G, LN_B = f("LN_G", [4, 2, 1024]), f("LN_B", [4, 2, 1024])
    W13, W2 = f("W13", [4, 1024, 5632]), f("W2", [4, 2816, 1024])
    RET_IN, RET_OUT, DEC = f("RET_IN", [1024, 6144]), f("RET_OUT", [2048, 1024]), f("DEC", [128, 4])
    NA_QKV, NA_OUT = f("NA_QKV", [1024, 3072]), f("NA_OUT", [1024, 1024])
    MI, MB, MKI, MKB = f("MI", [8, 128, 5, 128]), f("MB", [8, 128, 7, 128]), f("MKI", [128, 5, 128]), f("MKB", [128, 7, 128])
    MLA_DOWN, MLA_UQ, MLA_UKV, MLA_OUT = f("MLA_DOWN", [1024, 800]), f("MLA_UQ", [512, 1536]), f("MLA_UKV", [256, 2048]), f("MLA_OUT", [1024, 1024])
    GN = f("GN", [128, 768])
    HG_IN, HG_OUT, LBR, NG = f("HG_IN", [1024, 5120]), f("HG_OUT", [1024, 1024]), f("LBR", [128, 4, 4]), f("NG", [128, 128])
    COS256, SIN256 = f("COS256", [T, 128]), f("SIN256", [T, 128])
    COS32, SIN32 = f("COS32", [T, 16]), f("SIN32", [T, 16])
    COS32S, SIN32S = f("COS32S", [T, 16]), f("SIN32S", [T, 16])
    IDN, TRI, IDF = f("IDN", [128, 128], BF16), f("TRI", [2, 128, 128]), f("IDF", [128, 128])
    OUT = P.dout("OUT", [T, 1024], F32)

    MOD = scr("MOD", [4, 2, 6144], F32)
    A0 = scr("A0", [T, 1024], BF16)
    A2 = scr("A2", [T, 1024], BF16)
    FF = scr("FF", [2816, T], BF16)
    HA = scr("HA", [T, 1024], F32)
    HB = scr("HB", [T, 1024], F32)

    def m(i, r, s):
        return MOD[i, r, s * 1024:(s + 1) * 1024]

    stage_k0(P, condT, ADA_W, ADA_B, MOD)
    stage_mod0(P, XIN, [m(0, 0, 1), m(0, 0, 0), m(0, 1, 1), m(0, 1, 0)], A0, T)

    def post(i, Usrc, K, WOUT, Hin, last):
        v1 = [m(i, 0, 2), m(i, 1, 2), LN_G[i, 0], LN_B[i, 0], m(i, 0, 4), m(i, 0, 3), m(i, 1, 4), m(i, 1, 3)]
        stage_resid_ln(P, Usrc, WOUT, Hin, HB, A2, v1, T, K)
        stage_swiglu(P, A2, W13[i], FF, T)
        if last:
            v2 = [m(i, 0, 5), m(i, 1, 5), LN_G[i, 1], LN_B[i, 1]]
            stage_resid_ln(P, FF, W2[i], HB, OUT, None, v2, T, 2816, x_fm=True)
        else:
            v2 = [m(i, 0, 5), m(i, 1, 5), LN_G[i, 1], LN_B[i, 1], m(i + 1, 0, 1), m(i + 1, 0, 0), m(i + 1, 1, 1), m(i + 1, 1, 0)]
            stage_resid_ln(P, FF, W2[i], HB, HA, A0, v2, T, 2816, x_fm=True)

    Y0L, Y0G, Y0M = scr("Y0L", [T, 6144], BF16), scr("Y0G", [TG, 6144], BF16), scr("Y0M", [TG, 3072], BF16)
    U0M, OF0, U0G, U0L = scr("U0M", [TG, 1024], BF16), scr("OF0", [TG, 1024], F32), scr("U0G", [2 * TG, 1024], BF16), scr("U0L", [T, 2048], BF16)
    stage_ret_a(P, A0, RET_IN, Y0L, COS256, SIN256, T)
    stage_gather(P, Y0L, Y0G, T, 128)
    head_select(P, Y0G, 128, Y0M, [(0, 1024), (1024, 1024), (2048, 2048), (4096, 2048)], SEL)
    stage_gla(P, "ret", Y0M, U0M, OF0, DEC, None, IDN, TRI)
    stage_gather(P, U0M, U0G, TG, 768)
    token_select(P, U0G, 768, U0L, 1024, SEL)
    post(0, U0L, 2048, RET_OUT, XIN, False)
    Y1L, Y1G, Y1M = scr("Y1L", [T, 3072], BF16), scr("Y1G", [TG, 3072], BF16), scr("Y1M", [TG, 1536], BF16)
    O1M, O1G, O1L = scr("O1M", [TG, 512], BF16), scr("O1G", [2 * TG, 512], BF16), scr("O1L", [T, 1024], BF16)
    stage_plain(P, A0, NA_QKV, Y1L, T, 1024, 3072)
    stage_gather(P, Y1L, Y1G, T, 128)
    head_select(P, Y1G, 128, Y1M, [(0, 1024), (1024, 1024), (2048, 1024)], SEL)
    stage_na_b(P, Y1M, MI, MB, MKI, MKB, O1M)
    stage_gather(P, O1M, O1G, TG, 1408)
    token_select(P, O1G, 1408, O1L, 512, SEL)
    post(1, O1L, 1024, NA_OUT, HA, False)
    Y2A, YKRL, YKRG = scr("Y2A", [T, 800], BF16), scr("YKRL", [T, 128], BF16), scr("YKRG", [TG, 128], BF16)
    Y2Q, Y2KVL, Y2KVG, O2L = scr("Y2Q", [T, 2048], BF16), scr("Y2KVL", [T, 2048], BF16), scr("Y2KVG", [TG, 2048], BF16), scr("O2L", [T, 1024], BF16)
    stage_mla_a1(P, A0, MLA_DOWN, Y2A, YKRL, COS32, SIN32, GN, T)
    stage_mla_a2q(P, Y2A[:, 0:512], MLA_UQ, Y2Q, COS32S, SIN32S, T)
    stage_plain(P, Y2A[:, 512:768], MLA_UKV, Y2KVL, T, 256, 2048)
    Y2KVC = scr("Y2KVC", [TG, 2048], BF16)
    stage_gather(P, Y2KVL, Y2KVC, T, 384)
    stage_reorder(P, Y2KVC, 384, Y2KVG, T)
    stage_gather(P, YKRL, YKRG, T, T)
    stage_mla_b(P, Y2Q, Y2KVG, YKRG, IDF, O2L)
    post(2, O2L, 1024, MLA_OUT, HA, False)
    Y3L, Y3G, Y3M = scr("Y3L", [T, 5120], BF16), scr("Y3G", [TG, 5120], BF16), scr("Y3M", [TG, 2560], BF16)
    U3M, OF3, U3G, U3L = scr("U3M", [TG, 512], BF16), scr("OF3", [TG, 512], F32), scr("U3G", [2 * TG, 512], BF16), scr("U3L", [T, 1024], BF16)
    stage_plain(P, A0, HG_IN, Y3L, T, 1024, 5120, silu_blocks=(0, 1, 8, 9))
    stage_gather(P, Y3L, Y3G, T, 128)
    head_select(P, Y3G, 128, Y3M, [(j * 1024, 1024) for j in range(5)], SEL)
    stage_gla(P, "hg", Y3M, U3M, OF3, LBR, NG, IDN, TRI)
    stage_gather(P, U3M, U3G, TG, 1408)
    token_select(P, U3G, 1408, U3L, 512, SEL)
    post(3, U3L, 1024, HG_OUT, HA, True)
    return P


def _rep(v):
    v = np.asarray(v)
    return np.ascontiguousarray(np.broadcast_to(v[None], (128,) + v.shape))


def _rope_tables(rot_dim, half, L=8192):
    t = np.arange(L)
    rows = (t // 64).astype(np.float32)
    cols = (t % 64).astype(np.float32)
    nf = rot_dim // 4
    inv = (np.float32(10000.0) ** (-np.arange(nf, dtype=np.float32) / np.float32(nf))).astype(np.float32)
    ang = np.concatenate([rows[:, None] * inv, cols[:, None] * inv], -1).astype(np.float32)
    nc_ = rot_dim // 2
    sl = slice(half * (L // 2), (half + 1) * (L // 2))
    cos = np.concatenate([np.ones((128, nc_), np.float32), np.cos(ang).astype(np.float32)[sl]], 0)
    sin = np.concatenate([np.zeros((128, nc_), np.float32), np.sin(ang).astype(np.float32)[sl]], 0)
    return np.ascontiguousarray(cos), np.ascontiguousarray(sin)


def kernel(x, c, ctx, c_ctx, ada_w, ada_b, ln_g, ln_b, ffn_w13, ffn_w2,
           ret_w_in, ret_decay, ret_w_out, na_w_qkv, na_rpb, na_w_out,
           mla_w_down, mla_q_norm, mla_kv_norm, mla_w_uq, mla_w_ukv, mla_w_out,
           hg_w_in, hg_lower_bounds, hg_norm_g, hg_w_out):
    f32 = np.float32
    A = lambda a: np.ascontiguousarray(np.asarray(a, dtype=f32))
    x, c, ctx, c_ctx = A(x), A(c), A(ctx), A(c_ctx)
    import ml_dtypes
    bf = ml_dtypes.bfloat16
    MI, MB, MKI, MKB = na_mask_tables(A(na_rpb))
    SC = np.float32(96 ** -0.5)
    lbw = A(hg_lower_bounds).reshape(4, 8, 128)
    dec = A(ret_decay)
    shared = {
        "ADA_W": A(ada_w), "ADA_B": A(ada_b), "LN_G": A(ln_g), "LN_B": A(ln_b), "W13": A(ffn_w13), "W2": A(ffn_w2),
        "RET_IN": A(ret_w_in), "RET_OUT": A(ret_w_out),
        "NA_QKV": A(na_w_qkv), "NA_OUT": A(na_w_out), "MKI": MKI, "MKB": MKB,
        "MLA_DOWN": A(mla_w_down), "MLA_UQ": A(mla_w_uq), "MLA_UKV": A(mla_w_ukv), "MLA_OUT": A(mla_w_out),
        "GN": _rep(np.concatenate([A(mla_q_norm), A(mla_kv_norm)])),
        "HG_IN": A(hg_w_in), "HG_OUT": A(hg_w_out), "NG": _rep(A(hg_norm_g)),
        "IDN": np.eye(128, dtype=f32).astype(bf), "IDF": np.eye(128, dtype=f32), "TRI": np.stack([np.triu(np.ones((128, 128), f32)), np.tril(np.ones((128, 128), f32))]),
    }
    halfd = []
    for p in range(2):
        c256, s256 = _rope_tables(256, p)
        c32, s32 = _rope_tables(32, p)
        halfd.append({
            "COS256": c256, "SIN256": s256, "COS32": c32, "SIN32": s32, "COS32S": c32 * SC, "SIN32S": s32 * SC,
            "SEL": _rep(np.array([1.0 - p, float(p)], f32)),
            "MI": np.ascontiguousarray(MI[p * 8:(p + 1) * 8]), "MB": np.ascontiguousarray(MB[p * 8:(p + 1) * 8]),
            "DEC": _rep(np.array([dec[d, 2 * p + i] for d in range(2) for i in range(2)], f32)),
            "LBR": np.ascontiguousarray(lbw[:, 4 * p:4 * p + 4, :].transpose(2, 1, 0)),
        })
    in_maps = []
    for k in range(NCORES):
        b, p = k // 2, k % 2
        cond = np.stack([c[b], c_ctx], 0)
        d = dict(shared)
        d.update(halfd[p])
        d["XIN"] = np.ascontiguousarray(np.concatenate([ctx[b, p * 128:(p + 1) * 128], x[b, p * 4096:(p + 1) * 4096]], 0))
        d["condT"] = np.ascontiguousarray(cond.T.reshape(8, 128, 2).transpose(1, 0, 2))
        in_maps.append(d)
    res = run(build_fused(), in_maps)
    return np.ascontiguousarray(np.stack([np.concatenate([res[2 * b]["OUT"][128:], res[2 * b + 1]["OUT"][128:]], 0)
                                          for b in range(4)]).astype(np.float32))
```

```python
import numpy as np
from contextlib import ExitStack
import concourse.bass as bass
import concourse.mybir as mybir
from concourse.bass_utils import run_bass_kernel_spmd

F32 = mybir.dt.float32
BF16 = mybir.dt.bfloat16
AF = mybir.ActivationFunctionType
ALU = mybir.AluOpType
AX = mybir.AxisListType

NCORES = 8


class Buf:
    __slots__ = ("name", "t", "last_w", "readers")

    def __init__(self, name, t):
        self.name = name
        self.t = t
        self.last_w = None
        self.readers = []

    def __getitem__(self, idx):
        return self.t[idx]


class Op:
    __slots__ = ("eng", "fn", "deps", "signal", "token", "is_dma", "prewait", "is_cc")

    def __init__(self, eng, fn, is_dma):
        self.eng = eng
        self.fn = fn
        self.deps = []
        self.signal = is_dma
        self.token = None
        self.is_dma = is_dma
        self.prewait = None
        self.is_cc = False


ENGS = ("pe", "act", "dve", "pool", "sp")
NPOOL = {"sp": 48, "pool": 24, "act": 16}
CC_OUTSTANDING = 1


class Prog:
    def __init__(self):
        self.nc = nc = bass.Bass("TRN2", target_bir_lowering=False)
        self.gs = gs = ExitStack()
        self.esem = {e: gs.enter_context(nc.semaphore(f"s_{e}")) for e in ENGS if e != "sp"}
        self.dsem = {q: [gs.enter_context(nc.semaphore(f"d_{q}{i}")) for i in range(NPOOL[q])] for q in NPOOL}
        self.bsem = gs.enter_context(nc.semaphore("bar"))
        self.csem = gs.enter_context(nc.semaphore("cc"))
        self.ccnt = 0
        self.cnt = {e: 0 for e in ENGS}
        self.dcnt = {q: 0 for q in NPOOL}
        self.nbar = 0
        self.bt = {e: gs.enter_context(nc.sbuf_tensor(f"bt_{e}", [128, 8], F32)) for e in ("act", "dve", "pool")}
        self.btpe = gs.enter_context(nc.sbuf_tensor("bt_pe", [128, 8], BF16))
        self.bps = gs.enter_context(nc.psum_tensor("bps", [128, 8], F32))
        self.bdram = nc.dram_tensor("bar_d", [2, 16], F32, kind="Internal").ap()
        self.ss = ExitStack()
        self.ops = []
        self.n = 0

    def din(self, name, shape, dt):
        return self.nc.dram_tensor(name, list(shape), dt, kind="ExternalInput").ap()

    def dout(self, name, shape, dt):
        return self.nc.dram_tensor(name, list(shape), dt, kind="ExternalOutput").ap()

    def dscr(self, name, shape, dt):
        return self.nc.dram_tensor(name, list(shape), dt, kind="Internal").ap()

    def sb(self, name, shape, dt):
        self.n += 1
        t = self.ss.enter_context(self.nc.sbuf_tensor(f"{name}_{self.n}", list(shape), dt))
        return Buf(name, t)

    def ps(self, name, shape, dt=F32):
        self.n += 1
        t = self.ss.enter_context(self.nc.psum_tensor(f"{name}_{self.n}", list(shape), dt))
        return Buf(name, t)

    def op(self, eng, fn, reads=(), writes=(), is_dma=False):
        o = Op(eng, fn, is_dma)
        deps = []
        for b in reads:
            if b.last_w is not None:
                deps.append(b.last_w)
        for b in writes:
            w = b.last_w
            if w is not None and (is_dma or w.is_dma or w.eng != eng):
                deps.append(w)
            for r in b.readers:
                if is_dma or r.is_dma or r.eng != eng:
                    deps.append(r)
        for b in writes:
            b.last_w = o
            b.readers = []
        for b in reads:
            b.readers.append(o)
        seen = set()
        for d in deps:
            if id(d) not in seen and d is not o:
                seen.add(id(d))
                o.deps.append(d)
                d.signal = True
        self.ops.append(o)
        return o

    def dma(self, eng, out, in_, reads=(), writes=()):
        return self.op(eng, lambda e: e.dma_start(out=out, in_=in_), reads, writes, is_dma=True)

    def collective(self, kind, groups, in_ap, out_ap):
        o = self.op("pool", lambda e: e.collective_compute(kind, ALU.bypass, replica_groups=groups, ins=[in_ap], outs=[out_ap]),
                    (), (), is_dma=True)
        o.is_cc = True
        return o

    def dma_t(self, eng, out, in_, reads=(), writes=()):
        return self.op(eng, lambda e: e.dma_start_transpose(out=out, in_=in_), reads, writes, is_dma=True)

    def mm(self, out, lhsT, rhs, start, stop, reads, writes):
        return self.op("pe", lambda e: e.matmul(out, lhsT, rhs, start=start, stop=stop), reads, writes)

    def transpose(self, out, in_, ident, reads, writes):
        return self.op("pe", lambda e: e.transpose(out, in_, ident), reads, writes)

    def act(self, out, in_, func, reads, writes, bias=None, scale=None, accum_out=None, eng="act"):
        kw = {}
        if bias is not None:
            kw["bias"] = bias
        if scale is not None:
            kw["scale"] = scale
        if accum_out is not None:
            kw["accum_out"] = accum_out
        return self.op(eng, lambda e: e.activation(out, in_, func, **kw), reads, writes)

    def tt(self, out, in0, in1, op, reads, writes, eng="dve"):
        return self.op(eng, lambda e: e.tensor_tensor(out, in0, in1, op), reads, writes)

    def ts(self, out, in0, s1, s2, op0, op1, reads, writes, eng="dve", accum_out=None):
        if op1 is None:
            return self.op(eng, lambda e: e.tensor_scalar(out, in0, s1, None, op0), reads, writes)
        if accum_out is not None:
            return self.op(eng, lambda e: e.tensor_scalar(out, in0, s1, s2, op0, op1, accum_out=accum_out), reads, writes)
        return self.op(eng, lambda e: e.tensor_scalar(out, in0, s1, s2, op0, op1), reads, writes)

    def stt(self, out, in0, scalar, in1, op0, op1, reads, writes, eng="dve"):
        return self.op(eng, lambda e: e.scalar_tensor_tensor(out, in0, scalar, in1, op0, op1), reads, writes)

    def copy(self, out, in_, reads, writes, eng="dve"):
        return self.op(eng, lambda e: e.tensor_copy(out, in_), reads, writes)

    def memset(self, out, val, writes, eng="dve"):
        return self.op(eng, lambda e: e.memset(out, val), (), writes)

    def end_stage(self):
        nc = self.nc
        esem, dsem, cnt, dcnt = self.esem, self.dsem, self.cnt, self.dcnt
        for o in self.ops:
            if o.is_cc:
                self.ccnt += 1
                o.token = (self.csem, self.ccnt)
                if self.ccnt > CC_OUTSTANDING:
                    o.prewait = (self.csem, self.ccnt - CC_OUTSTANDING)
            elif o.is_dma:
                q = o.eng
                n = dcnt[q]
                dcnt[q] += 1
                slot = n % NPOOL[q]
                use = n // NPOOL[q]
                o.token = (dsem[q][slot], 16 * (use + 1))
                if use > 0:
                    o.prewait = (dsem[q][slot], 16 * use)
            elif o.signal:
                cnt[o.eng] += 1
                o.token = (esem[o.eng], cnt[o.eng])
        per = {e: [o for o in self.ops if o.eng == e] for e in ENGS}
        self.nbar += 1
        nbar = self.nbar
        drain = {}
        for q in NPOOL:
            drain[q] = []
            for slot in range(min(dcnt[q], NPOOL[q])):
                uses = (dcnt[q] - 1 - slot) // NPOOL[q] + 1
                drain[q].append((dsem[q][slot], 16 * uses))
        bsem = self.bsem

        def emit(eng_name, e):
            known = {}

            def wait(tok):
                s, v = tok
                if known.get(id(s), 0) >= v:
                    return
                known[id(s)] = v
                e.wait_ge(s, v)

            for o in per[eng_name]:
                if o.prewait is not None:
                    wait(o.prewait)
                for d in o.deps:
                    wait(d.token)
                inst = o.fn(e)
                if o.is_cc:
                    inst.then_inc(o.token[0], 1)
                elif o.is_dma:
                    inst.then_inc(o.token[0], 16)
                elif o.signal:
                    inst.then_inc(o.token[0], 1)
            for tok in drain.get(eng_name, ()):
                wait(tok)
            if eng_name == "pool" and self.ccnt:
                wait((self.csem, self.ccnt))
            if eng_name == "sp":
                e.dma_start(out=self.bdram[1:2, :], in_=self.bdram[0:1, :]).then_inc(bsem, 16)
            elif eng_name == "pe":
                e.matmul(self.bps[0:8, 0:8], self.btpe[:, 0:8], self.btpe[:, 0:8], start=True, stop=True).then_inc(bsem, 1)
            elif eng_name == "act":
                e.activation(self.bt["act"][:, 0:1], self.bt["act"][:, 1:2], AF.Copy).then_inc(bsem, 1)
            else:
                e.memset(self.bt[eng_name][:, 0:1], 0.0).then_inc(bsem, 1)
            e.wait_ge(bsem, 20 * nbar)

        with nc.Block() as block:
            @block.sync
            def _(e):
                emit("sp", e)

            @block.tensor
            def _(e):
                emit("pe", e)

            @block.scalar
            def _(e):
                emit("act", e)

            @block.vector
            def _(e):
                emit("dve", e)

            @block.gpsimd
            def _(e):
                emit("pool", e)
        self.ss.close()
        self.ss = ExitStack()
        self.ops = []

    def finish(self):
        if self.ops:
            self.end_stage()
        self.gs.close()
        return self.nc


def run(prog, in_maps):
    nc = prog.finish()
    res = run_bass_kernel_spmd(nc, in_maps, core_ids=list(range(NCORES)))
    return res.results


RMS_EPS = 1e-6
LN_EPS = 1e-5
ALPHA = 8 ** 0.25
T_ = 4224
TG_ = 8448
NCT_ = 1


def grow(c):
    if c < 2:
        return c * T_
    i = c - 2
    return (128 + i * 128) if i < 32 else (T_ + 128 + (i - 32) * 128)


def tok_groups(T):
    gs = []
    t = 0
    while t < T:
        g = min(512, T - t)
        gs.append((t, g))
        t += g
    return gs


def bc_load(P, tile, row_ap, n):
    P.dma("sp", tile[:, 0:n], row_ap.partition_broadcast(128), (), (tile,))


class Lin:
    def __init__(self, P, X, W, T, K, N, blocks=None, x_fm=False):
        self.P = P
        self.x_fm = x_fm
        self.finish_ep = None
        self.T, self.K, self.N = T, K, N
        self.KC = K // 128
        assert K % 128 == 0 and T % 128 == 0
        self.X, self.W = X, W
        self.blocks = blocks or [(n0, min(512, N - n0)) for n0 in range(0, N, 512)]
        self.pre_group = None
        self.blk_ep = None
        self.row_ep = None
        Wv = W.rearrange("(k p) n -> p k n", p=128)
        self.wt = []
        for k in range(self.KC):
            w = P.sb("w", [128, N], BF16)
            P.dma("pool", w[:], Wv[:, k, :], (), (w,))
            self.wt.append(w)

    def run(self):
        P = self.P
        KC = self.KC
        xb = [P.sb("xg", [128, KC, 512], BF16) for _ in range(2)]
        psb = [P.ps("ps", [128, 512]) for _ in range(4)]
        groups = tok_groups(self.T)

        def load(g):
            t0, gs = groups[g]
            if self.x_fm:
                P.dma("sp", xb[g % 2][:, :, 0:gs], self.X.rearrange("(k p) t -> p k t", p=128)[:, :, t0:t0 + gs], (), (xb[g % 2],))
            else:
                for k in range(KC):
                    P.dma_t("sp", xb[g % 2][:, k, 0:gs], self.X[t0:t0 + gs, k * 128:(k + 1) * 128], (), (xb[g % 2],))
            if self.pre_group:
                self.pre_group(g, t0, gs)

        load(0)
        ctr = 0
        for g, (t0, gs) in enumerate(groups):
            if g + 1 < len(groups):
                load(g + 1)
            xg = xb[g % 2]
            for j in range(gs // 128):
                tt = t0 // 128 + j
                for nb, (n0, ns) in enumerate(self.blocks):
                    ps = psb[ctr % 4]
                    ctr += 1
                    for k in range(KC):
                        P.mm(ps[:, 0:ns], xg[:, k, j * 128:(j + 1) * 128], self.wt[k][:, n0:n0 + ns],
                             k == 0, k == KC - 1, (xg, self.wt[k]), (ps,))
                    self.blk_ep(tt, j, nb, n0, ns, ps)
                if self.row_ep:
                    self.row_ep(tt, j, t0 + j * 128)
        if self.finish_ep:
            self.finish_ep()


def evac(P, i, out, in_, reads, writes, func=None, scale=None):
    if func is not None:
        return P.act(out, in_, func, reads, writes, scale=scale)
    if i % 2 == 0:
        return P.act(out, in_, AF.Copy, reads, writes)
    return P.copy(out, in_, reads, writes)


def bcast_mid(ap, n):
    return ap.unsqueeze(1).broadcast_to([ap.shape[0], n, ap.shape[1]])


def rope_loads(P, cs, sn, COS, SIN, g, t0, gs):
    nj = gs // 128
    P.dma("sp", cs[g % 2][:, 0:nj, :], COS[t0:t0 + gs, :].rearrange("(j p) f -> p j f", p=128), (), (cs[g % 2],))
    P.dma("sp", sn[g % 2][:, 0:nj, :], SIN[t0:t0 + gs, :].rearrange("(j p) f -> p j f", p=128), (), (sn[g % 2],))


def stage_plain(P, X, W, Y, T, K, N, silu_blocks=()):
    L = Lin(P, X, W, T, K, N)
    ob = [P.sb("ob", [128, N], BF16) for _ in range(2)]

    def blk_ep(tt, j, nb, n0, ns, ps):
        o = ob[tt % 2]
        evac(P, nb, o[:, n0:n0 + ns], ps[:, 0:ns], (ps,), (o,), func=AF.Silu if nb in silu_blocks else None)

    def row_ep(tt, j, t0):
        o = ob[tt % 2]
        P.dma("pool", Y[t0:t0 + 128, :], o[:], (o,), ())

    L.blk_ep, L.row_ep = blk_ep, row_ep
    L.run()
    P.end_stage()


def stage_ret_a(P, X, W, Y, COS, SIN, T):
    L = Lin(P, X, W, T, 1024, 6144)
    ob = [P.sb("ob", [128, 6144], BF16) for _ in range(2)]
    rb = [P.sb("rb", [128, 2048], F32) for _ in range(2)]
    cs = [P.sb("cs", [128, 4, 128], F32) for _ in range(2)]
    sn = [P.sb("sn", [128, 4, 128], F32) for _ in range(2)]
    t1 = P.sb("t1", [128, 8, 128], F32)
    t2 = P.sb("t2", [128, 8, 128], F32)
    t3 = P.sb("t3", [128, 8, 128], F32)
    t4 = P.sb("t4", [128, 8, 128], F32)

    def blk_ep(tt, j, nb, n0, ns, ps):
        if nb < 4:
            r = rb[tt % 2]
            evac(P, nb, r[:, n0:n0 + ns], ps[:, 0:ns], (ps,), (r,))
        else:
            o = ob[tt % 2]
            evac(P, nb, o[:, n0:n0 + ns], ps[:, 0:ns], (ps,), (o,), func=AF.Silu if nb >= 8 else None)

    def row_ep(tt, j, t0):
        g = (t0 // 512)
        r = rb[tt % 2]
        o = ob[tt % 2]
        rv = r[:, :].rearrange("p (h two d) -> p h two d", h=8, two=2)
        ov = o[:, 0:2048].rearrange("p (h two d) -> p h two d", h=8, two=2)
        c = bcast_mid(cs[g % 2][:, j, :], 8)
        s = bcast_mid(sn[g % 2][:, j, :], 8)
        x1, x2 = rv[:, :, 0, :], rv[:, :, 1, :]
        P.tt(t1[:], x1, c, ALU.mult, (r, cs[g % 2]), (t1,))
        P.tt(t2[:], x2, s, ALU.mult, (r, sn[g % 2]), (t2,))
        P.tt(ov[:, :, 0, :], t1[:], t2[:], ALU.subtract, (t1, t2), (o,))
        P.tt(t3[:], x1, s, ALU.mult, (r, sn[g % 2]), (t3,), eng="pool")
        P.tt(t4[:], x2, c, ALU.mult, (r, cs[g % 2]), (t4,), eng="pool")
        P.tt(ov[:, :, 1, :], t3[:], t4[:], ALU.add, (t3, t4), (o,), eng="pool")
        P.dma("pool", Y[t0:t0 + 128, :], o[:], (o,), ())

    L.pre_group = lambda g, t0, gs: rope_loads(P, cs, sn, COS, SIN, g, t0, gs)
    L.blk_ep, L.row_ep = blk_ep, row_ep
    L.run()
    P.end_stage()


def stage_mla_a1(P, X, W, Y, YKR, COS, SIN, GN, T):
    L = Lin(P, X, W, T, 1024, 800)
    gn = P.sb("gn", [128, 768], F32)
    P.dma("sp", gn[:], GN, (), (gn,))
    ob = [P.sb("ob", [128, 800], BF16) for _ in range(2)]
    rb = [P.sb("rb", [128, 800], F32) for _ in range(2)]
    cs = [P.sb("cs", [128, 4, 16], F32) for _ in range(2)]
    sn = [P.sb("sn", [128, 4, 16], F32) for _ in range(2)]
    junk = P.sb("junk", [128, 512], F32)
    epsb = P.sb("epsb", [128, 1], F32)
    P.memset(epsb[:], RMS_EPS, (epsb,))
    st = [P.sb("st", [128, 4], F32) for _ in range(2)]
    tm = [P.sb("tm", [128, 4, 16], F32) for _ in range(2)]
    okr = [P.sb("okr", [128, 128], BF16) for _ in range(2)]
    for o_ in okr:
        P.memset(o_[:], 0.0, (o_,), eng="pool")

    def blk_ep(tt, j, nb, n0, ns, ps):
        r = rb[tt % 2]
        evac(P, nb + 1, r[:, n0:n0 + ns], ps[:, 0:ns], (ps,), (r,))

    def row_ep(tt, j, t0):
        g = t0 // 512
        r, o, s, t = rb[tt % 2], ob[tt % 2], st[tt % 2], tm[tt % 2]
        for idx, (c0, cn) in enumerate(((0, 512), (512, 256))):
            P.act(junk[:, 0:cn], r[:, c0:c0 + cn], AF.Square, (r,), (junk, s), accum_out=s[:, idx:idx + 1])
            P.act(s[:, 2 + idx:3 + idx], s[:, idx:idx + 1], AF.Sqrt, (s, epsb), (s,), bias=epsb[:, 0:1], scale=1.0 / cn)
            P.op("dve", lambda e, idx=idx: e.reciprocal(s[:, 2 + idx:3 + idx], s[:, 2 + idx:3 + idx]), (s,), (s,))
            P.stt(o[:, c0:c0 + cn], r[:, c0:c0 + cn], s[:, 2 + idx:3 + idx], gn[:, c0:c0 + cn], ALU.mult, ALU.mult,
                  (r, s, gn), (o,))
        c, sn_ = cs[g % 2][:, j, :], sn[g % 2][:, j, :]
        x1, x2 = r[:, 768:784], r[:, 784:800]
        P.tt(t[:, 0, :], x1, c, ALU.mult, (r, cs[g % 2]), (t,))
        P.tt(t[:, 1, :], x2, sn_, ALU.mult, (r, sn[g % 2]), (t,))
        P.tt(o[:, 768:784], t[:, 0, :], t[:, 1, :], ALU.subtract, (t,), (o,))
        P.tt(t[:, 2, :], x1, sn_, ALU.mult, (r, sn[g % 2]), (t,))
        P.tt(t[:, 3, :], x2, c, ALU.mult, (r, cs[g % 2]), (t,))
        P.tt(o[:, 784:800], t[:, 2, :], t[:, 3, :], ALU.add, (t,), (o,))
        P.dma("pool", Y[t0:t0 + 128, :], o[:], (o,), ())
        P.copy(okr[tt % 2][:, 64:96], o[:, 768:800], (o,), (okr[tt % 2],), eng="pool")
        P.dma("pool", YKR[t0:t0 + 128, :], okr[tt % 2][:], (okr[tt % 2],), ())

    L.pre_group = lambda g, t0, gs: rope_loads(P, cs, sn, COS, SIN, g, t0, gs)
    L.blk_ep, L.row_ep = blk_ep, row_ep
    L.run()
    P.end_stage()


def stage_mla_a2q(P, X, W, Y, COS, SIN, T):
    SC = 96 ** -0.5
    L = Lin(P, X, W, T, 512, 1536)
    ob = [P.sb("ob", [128, 2048], BF16) for _ in range(2)]
    for o_ in ob:
        P.memset(o_[:], 0.0, (o_,), eng="pool")
    rb = [P.sb("rb", [128, 1536], F32) for _ in range(2)]
    cs = [P.sb("cs", [128, 4, 16], F32) for _ in range(2)]
    sn = [P.sb("sn", [128, 4, 16], F32) for _ in range(2)]
    tm = [P.sb("tm", [128, 4, 16, 16], F32) for _ in range(2)]

    def blk_ep(tt, j, nb, n0, ns, ps):
        r = rb[tt % 2]
        evac(P, nb, r[:, n0:n0 + ns], ps[:, 0:ns], (ps,), (r,))

    def row_ep(tt, j, t0):
        g = t0 // 512
        r, o, t = rb[tt % 2], ob[tt % 2], tm[tt % 2]
        rv = r[:, :].rearrange("p (h d) -> p h d", h=16)
        ov = o[:, :].rearrange("p (h d) -> p h d", h=16)
        P.act(ov[:, :, 0:64], rv[:, :, 0:64], AF.Copy, (r,), (o,), scale=SC)
        c = bcast_mid(cs[g % 2][:, j, :], 16)
        s = bcast_mid(sn[g % 2][:, j, :], 16)
        x1, x2 = rv[:, :, 64:80], rv[:, :, 80:96]
        P.tt(t[:, 0], x1, c, ALU.mult, (r, cs[g % 2]), (t,))
        P.tt(t[:, 1], x2, s, ALU.mult, (r, sn[g % 2]), (t,))
        P.tt(ov[:, :, 64:80], t[:, 0], t[:, 1], ALU.subtract, (t,), (o,))
        P.tt(t[:, 2], x1, s, ALU.mult, (r, sn[g % 2]), (t,))
        P.tt(t[:, 3], x2, c, ALU.mult, (r, cs[g % 2]), (t,))
        P.tt(ov[:, :, 80:96], t[:, 2], t[:, 3], ALU.add, (t,), (o,))
        P.dma("pool", Y[t0:t0 + 128, :], o[:], (o,), ())

    L.pre_group = lambda g, t0, gs: rope_loads(P, cs, sn, COS, SIN, g, t0, gs)
    L.blk_ep, L.row_ep = blk_ep, row_ep
    L.run()
    P.end_stage()


def stage_swiglu(P, X, W, YT, T):
    F = 2816
    FC = F // 128
    Wv = W.rearrange("(k p) n -> p k n", p=128)
    wt = []
    for k in range(8):
        w = P.sb("w", [128, 2 * F], BF16)
        P.dma("pool", w[:], Wv[:, k, :], (), (w,))
        wt.append(w)
    xb = [P.sb("xg", [128, 8, 512], BF16) for _ in range(2)]
    psg = [P.ps("psg", [128, 512]) for _ in range(2)]
    psu = [P.ps("psu", [128, 512]) for _ in range(2)]
    sg = [P.sb("sg", [128, 512], F32) for _ in range(2)]
    ob = [P.sb("ob", [128, 512], BF16) for _ in range(3)]
    groups = tok_groups(T)

    def load(g):
        t0, gs = groups[g]
        for k in range(8):
            P.dma_t("sp", xb[g % 2][:, k, 0:gs], X[t0:t0 + gs, k * 128:(k + 1) * 128], (), (xb[g % 2],))

    load(0)
    n = 0
    for g, (t0, gs) in enumerate(groups):
        if g + 1 < len(groups):
            load(g + 1)
        xg = xb[g % 2]
        for fc in range(FC):
            pg, pu, s_, o = psg[n % 2], psu[n % 2], sg[n % 2], ob[n % 3]
            for k in range(8):
                P.mm(pg[:, 0:gs], wt[k][:, fc * 128:(fc + 1) * 128], xg[:, k, 0:gs], k == 0, k == 7, (wt[k], xg), (pg,))
            for k in range(8):
                P.mm(pu[:, 0:gs], wt[k][:, F + fc * 128:F + (fc + 1) * 128], xg[:, k, 0:gs], k == 0, k == 7, (wt[k], xg), (pu,))
            P.act(s_[:, 0:gs], pg[:, 0:gs], AF.Silu, (pg,), (s_,))
            P.tt(o[:, 0:gs], pu[:, 0:gs], s_[:, 0:gs], ALU.mult, (pu, s_), (o,))
            P.dma("pool", YT[fc * 128:(fc + 1) * 128, t0:t0 + gs], o[:, 0:gs], (o,), ())
            n += 1
    P.end_stage()


def stage_resid_ln(P, X, W, H, HO, AO, vrows, T, K, x_fm=False):
    L = Lin(P, X, W, T, K, 1024, x_fm=x_fm)
    want_a = AO is not None
    vec = [P.sb("vec", [128, 1024], F32) for _ in range(8 if want_a else 4)]
    for i in range(len(vec)):
        bc_load(P, vec[i], vrows[i], 1024)
    if want_a:
        for i in (4, 6):
            P.ts(vec[i][:], vec[i][:], 1.0, None, ALU.add, None, (vec[i],), (vec[i],))
    hb = [P.sb("hb", [128, 4, 1024], F32) for _ in range(2)]
    zb = [P.sb("zb", [128, 1024], F32) for _ in range(3)]
    ho = [P.sb("ho", [128, 1024], F32) for _ in range(3)]
    ao = [P.sb("ao", [128, 1024], BF16) for _ in range(2)]
    at = [P.sb("at", [128, 1024], F32) for _ in range(2)]
    st = [P.sb("st", [128, 2, 6], F32) for _ in range(3)]
    mv = [P.sb("mv", [128, 4], F32) for _ in range(3)]
    epsb = P.sb("epsb", [128, 1], F32)
    P.memset(epsb[:], LN_EPS, (epsb,))

    def pre_group(g, t0, gs):
        nj = gs // 128
        P.dma("sp", hb[g % 2][:, 0:nj, :], H[t0:t0 + gs, :].rearrange("(j p) f -> p j f", p=128), (), (hb[g % 2],))

    def blk_ep(tt, j, nb, n0, ns, ps):
        z = zb[tt % 3]
        gate = vec[1] if tt < NCT_ else vec[0]
        P.tt(z[:, n0:n0 + ns], ps[:, 0:ns], gate[:, n0:n0 + ns], ALU.mult, (ps, gate), (z,))

    def phase_b(tt, t0):
        z, o, m = zb[tt % 3], ho[tt % 3], mv[tt % 3]
        P.op("dve", lambda e: e.reciprocal(m[:, 2:3], m[:, 2:3]), (m,), (m,))
        P.ts(m[:, 3:4], m[:, 0:1], -1.0, m[:, 2:3], ALU.mult, ALU.mult, (m,), (m,))
        P.act(z[:], z[:], AF.Identity, (z, m), (z,), bias=m[:, 3:4], scale=m[:, 2:3])
        P.tt(z[:], z[:], vec[2][:], ALU.mult, (z, vec[2]), (z,), eng="pool")
        P.tt(o[:], z[:], vec[3][:], ALU.add, (z, vec[3]), (o,), eng="pool")
        P.dma("pool", HO[t0:t0 + 128, :], o[:], (o,), ())

    def phase_c(tt, t0):
        if not want_a:
            return
        o, a = ho[tt % 3], ao[tt % 2]
        sc, sh = (vec[4], vec[5]) if tt >= NCT_ else (vec[6], vec[7])
        a_t = at[tt % 2]
        P.tt(a_t[:], o[:], sc[:], ALU.mult, (o, sc), (a_t,))
        P.tt(a[:], a_t[:], sh[:], ALU.add, (a_t, sh), (a,))
        P.dma("pool", AO[t0:t0 + 128, :], a[:], (a,), ())

    pend = []

    def row_ep(tt, j, t0):
        g = t0 // 512
        z, h, s, m = zb[tt % 3], hb[g % 2], st[tt % 3], mv[tt % 3]
        P.stt(z[:], h[:, j, :], ALPHA, z[:], ALU.mult, ALU.add, (h, z), (z,))
        for c in range(2):
            P.op("dve", lambda e, c=c: e.bn_stats(s[:, c, :], z[:, c * 512:(c + 1) * 512]), (z,), (s,))
        P.op("dve", lambda e: e.bn_aggr(m[:, 0:2], s[:, :, :].rearrange("p a b -> p (a b)")), (s,), (m,))
        P.act(m[:, 2:3], m[:, 1:2], AF.Sqrt, (m, epsb), (m,), bias=epsb[:, 0:1], scale=1.0)
        pend.append((tt, t0))
        if len(pend) >= 2:
            phase_b(*pend[-2])
        if len(pend) >= 3:
            phase_c(*pend[-3])

    def finish_ep():
        n = len(pend)
        phase_b(*pend[-1])
        if n >= 2:
            phase_c(*pend[-2])
        phase_c(*pend[-1])

    L.pre_group, L.blk_ep, L.row_ep, L.finish_ep = pre_group, blk_ep, row_ep, finish_ep
    L.run()
    P.end_stage()


def stage_mod0(P, X, vrows, AO, T):
    vec = [P.sb("vec", [128, 1024], F32) for _ in range(4)]
    for i in range(4):
        bc_load(P, vec[i], vrows[i], 1024)
    for i in (0, 2):
        P.ts(vec[i][:], vec[i][:], 1.0, None, ALU.add, None, (vec[i],), (vec[i],))
    xb = [P.sb("xb", [128, 1024], F32) for _ in range(3)]
    at = [P.sb("at", [128, 1024], F32) for _ in range(2)]
    ao = [P.sb("ao", [128, 1024], BF16) for _ in range(2)]
    for tt in range(T // 128):
        x, a_t, a = xb[tt % 3], at[tt % 2], ao[tt % 2]
        sc, sh = (vec[0], vec[1]) if tt >= NCT_ else (vec[2], vec[3])
        P.dma("sp", x[:], X[tt * 128:(tt + 1) * 128, :], (), (x,))
        eng = "dve" if tt % 2 == 0 else "pool"
        P.tt(a_t[:], x[:], sc[:], ALU.mult, (x, sc), (a_t,), eng=eng)
        P.tt(a[:], a_t[:], sh[:], ALU.add, (a_t, sh), (a,), eng=eng)
        P.dma("pool", AO[tt * 128:(tt + 1) * 128, :], a[:], (a,), ())
    P.end_stage()


def stage_k0(P, condT, ADA_W, ADA_B, MOD):
    ct = P.sb("ct", [128, 8, 2], F32)
    cs = P.sb("cs", [128, 8, 2], F32)
    ones = P.sb("ones", [1, 2], F32)
    P.dma("sp", ct[:], condT, (), (ct,))
    P.memset(ones[:], 1.0, (ones,))
    P.act(cs[:], ct[:], AF.Silu, (ct,), (cs,))
    wb = [P.sb("w", [128, 8, 512], F32) for _ in range(3)]
    bb = [P.sb("b", [1, 512], F32) for _ in range(3)]
    ob = [P.sb("o", [2, 512], F32) for _ in range(3)]
    pss = [P.ps("ps", [2, 512]) for _ in range(2)]
    n = 0
    for i in range(4):
        Wv = ADA_W[i].rearrange("(k p) n -> p k n", p=128)
        for nb in range(12):
            w, b, o, ps = wb[n % 3], bb[n % 3], ob[n % 3], pss[n % 2]
            sl = slice(nb * 512, (nb + 1) * 512)
            P.dma("sp", w[:], Wv[:, :, sl], (), (w,))
            P.dma("sp", b[:], ADA_B[i:i + 1, sl], (), (b,))
            for k in range(8):
                P.mm(ps[:], cs[:, k, :], w[:, k, :], k == 0, False, (cs, w), (ps,))
            P.mm(ps[:], ones[:], b[:], False, True, (ones, b), (ps,))
            P.copy(o[:], ps[:], (ps,), (o,))
            P.dma("pool", MOD[i, :, sl], o[:], (o,), ())
            n += 1
    P.end_stage()


RCH = 1408


def stage_mla_b(P, YQ, YKV, YKR, IDF, O, NHEADS=16, DQK=96, DV=64):
    NK = TG_
    NKT = NK // 128
    NQ = T_
    qb = [P.sb("q", [128, NQ], BF16) for _ in range(2)]
    kb = [P.sb("k", [128, NK], BF16) for _ in range(2)]
    krt = P.sb("krt", [128, NK], BF16)
    vb = [P.sb("v", [128, NKT, DV + 1], BF16) for _ in range(2)]
    idf = P.sb("idf", [128, 128], F32)
    P.dma("sp", idf[:], IDF, (), (idf,))
    pss = [P.ps("pss", [128, 1024]) for _ in range(2)]
    pso = [P.ps("pso", [128, 512]) for _ in range(2)]
    pst = P.ps("pst", [128, 4, DV + 1])
    pt = [P.sb("pt", [128, 1024], BF16) for _ in range(2)]
    osb = [P.sb("osb", [DV + 1, 512], F32) for _ in range(2)]
    rc = [P.sb("rc", [128, 4, 1], F32) for _ in range(2)]
    ob = [P.sb("ob", [128, 4, DV], BF16) for _ in range(2)]
    for r0 in range(0, NK, RCH):
        P.dma_t("sp", krt[:, r0:r0 + RCH], YKR[r0:r0 + RCH, :], (), (krt,))

    def load(h):
        q, k, v = qb[h % 2], kb[h % 2], vb[h % 2]
        for r0 in range(0, NQ, RCH):
            P.dma_t("sp", q[:, r0:r0 + RCH], YQ[r0:r0 + RCH, h * 128:(h + 1) * 128], (), (q,))
        for r0 in range(0, NK, RCH):
            P.dma_t("sp", k[:, r0:r0 + RCH], YKV[r0:r0 + RCH, h * 128:(h + 1) * 128], (), (k,))
        P.copy(k[64:96, :], krt[64:96, :], (krt, k), (k,), eng="pool")
        P.dma("sp", v[:, :, 0:DV], YKV[:, h * 128 + 64:(h + 1) * 128].rearrange("(t p) d -> p t d", p=128), (), (v,))
        P.memset(v[:, :, DV:DV + 1], 1.0, (v,), eng="pool")

    ctx_tiles = [0, T_ // 128]
    all_tiles = list(range(NKT))
    nlat = (NQ - 128) // 512
    qgroups = [([(0, 128)], ctx_tiles)] + [([(128 + (2 * g + b2) * 512, 512) for b2 in range(2)], all_tiles) for g in range(nlat // 2)]
    load(0)
    ctr = 0
    bi = 0
    for h in range(NHEADS):
        if h + 1 < NHEADS:
            load(h + 1)
        q, k, v = qb[h % 2], kb[h % 2], vb[h % 2]
        for (blocks, ktl) in qgroups:
            nb = len(blocks)
            qn = blocks[0][1]
            nkt = len(ktl)

            def S(i, c):
                kt = ktl[i]
                for b2, (q0, _) in enumerate(blocks):
                    P.mm(pss[c % 2][:, b2 * 512:b2 * 512 + qn], k[0:DQK, kt * 128:(kt + 1) * 128], q[0:DQK, q0:q0 + qn], True, True,
                         (k, q), (pss[c % 2],))

            S(0, ctr)
            for i in range(nkt):
                kt = ktl[i]
                c = ctr + i
                if i + 1 < nkt:
                    S(i + 1, c + 1)
                sv = pss[c % 2][:, :].rearrange("p (a b) -> p a b", a=2)[:, 0:nb, 0:qn]
                pv = pt[c % 2][:, :].rearrange("p (a b) -> p a b", a=2)[:, 0:nb, 0:qn]
                P.act(pv, sv, AF.Exp, (pss[c % 2],), (pt[c % 2],))
                for b2 in range(nb):
                    P.mm(pso[b2][0:DV + 1, 0:qn], v[:, kt, :], pt[c % 2][:, b2 * 512:b2 * 512 + qn], i == 0, i == nkt - 1,
                         (pt[c % 2], v), (pso[b2],))
            ctr += nkt
            nqi = qn // 128
            for b2, (q0, _) in enumerate(blocks):
                po = pso[b2]
                r, o, os_ = rc[bi % 2], ob[bi % 2], osb[bi % 2]
                P.copy(os_[:, 0:qn], po[0:DV + 1, 0:qn], (po,), (os_,))
                for qi in range(nqi):
                    P.transpose(pst[:, qi, :], os_[:, qi * 128:(qi + 1) * 128], idf[0:DV + 1, 0:DV + 1], (os_, idf), (pst,))
                P.op("dve", lambda e, r=r, nqi=nqi: e.reciprocal(r[:, 0:nqi, :], pst[:, 0:nqi, DV:DV + 1]), (pst,), (r,))
                P.tt(o[:, 0:nqi, :], pst[:, 0:nqi, 0:DV], r[:, 0:nqi, :].broadcast_to([128, nqi, DV]), ALU.mult, (pst, r), (o,))
                P.dma("sp", O[q0:q0 + qn, h * DV:(h + 1) * DV].rearrange("(j p) d -> p j d", p=128), o[:, 0:nqi, :], (o,), ())
                bi += 1
    P.end_stage()


def stage_na_b(P, Y, MI, MB, MKI, MKB, O, NHEADS=8, NLT=64, DH=64):
    NTOK = TG_
    NT = NTOK // 128
    HW = NHEADS * DH
    SC = DH ** -0.5
    gt = lambda c: grow(c) // 128
    qb = [P.sb("q", [128, NTOK], BF16) for _ in range(2)]
    kb = [P.sb("k", [128, NTOK], BF16) for _ in range(2)]
    vb = [P.sb("v", [128, NT, DH + 1], BF16) for _ in range(2)]
    mi = [P.sb("mi", [128, 5, 128], F32) for _ in range(2)]
    mb = [P.sb("mb", [128, 7, 128], F32) for _ in range(2)]
    mki = P.sb("mki", [128, 5, 128], F32)
    mkb = P.sb("mkb", [128, 7, 128], F32)
    P.dma("sp", mki[:], MKI, (), (mki,))
    P.dma("sp", mkb[:], MKB, (), (mkb,))
    psa = [P.ps("psa", [128, 4, 128]) for _ in range(2)]
    psb = [P.ps("psb", [128, 4, 128]) for _ in range(2)]
    pso = [P.ps("pso", [128, 512]) for _ in range(2)]
    sa = [P.sb("sa", [128, 5, 128], F32) for _ in range(2)]
    pt = [P.sb("pt", [128, 8, 128], BF16) for _ in range(3)]
    rc = [P.sb("rc", [128, 1], F32) for _ in range(2)]
    ob = [P.sb("ob", [128, DH], BF16) for _ in range(2)]

    def load(h):
        v = vb[h % 2]
        if h % 2 == 0:
            hp = h // 2
            q, k = qb[hp % 2], kb[hp % 2]
            for r0 in range(0, NTOK, RCH):
                P.dma_t("sp", q[:, r0:r0 + RCH], Y[r0:r0 + RCH, hp * 128:(hp + 1) * 128], (), (q,))
                P.dma_t("sp", k[:, r0:r0 + RCH], Y[r0:r0 + RCH, HW + hp * 128:HW + (hp + 1) * 128], (), (k,))
        P.dma("sp", v[:, :, 0:DH], Y[:, 2 * HW + h * DH:2 * HW + (h + 1) * DH].rearrange("(t p) d -> p t d", p=128), (), (v,))
        P.memset(v[:, :, DH:DH + 1], 1.0, (v,), eng="pool")
        P.dma("sp", mi[h % 2][:], MI[h], (), (mi[h % 2],))
        P.dma("sp", mb[h % 2][:], MB[h], (), (mb[h % 2],))
        P.tt(mi[h % 2][:], mi[h % 2][:], mki[:], ALU.add, (mi[h % 2], mki), (mi[h % 2],), eng="pool")
        P.tt(mb[h % 2][:], mb[h % 2][:], mkb[:], ALU.add, (mb[h % 2], mkb), (mb[h % 2],), eng="pool")

    load(0)
    if NHEADS > 1:
        load(1)
    ctx = [gt(0), gt(1)]
    units = []
    for h in range(NHEADS):
        for fq in range(NLT + 2):
            units.append((h, fq))
    pend = []

    def prep(h, fq, u):
        p0 = (h % 2) * DH
        q, k, v = qb[(h // 2) % 2], kb[(h // 2) % 2], vb[h % 2]
        gq = gt(fq)
        if fq >= 2:
            qt = fq - 2
            if 2 <= qt <= NLT - 3:
                lk = list(range(qt - 2, qt + 3))
                mask = mi[h % 2]
                m0 = 0
            else:
                k0 = 0 if qt < 2 else NLT - 4
                lk = list(range(k0, k0 + 4))
                mask = mb[h % 2]
                m0 = (k0 - qt) + 3
            kts = [gt(x + 2) for x in lk]
        else:
            kts, mask, m0 = [], None, 0
        pa, pb, s, p_ = psa[u % 2], psb[u % 2], sa[u % 2], pt[u % 3]
        qs = q[p0:p0 + DH, gq * 128:(gq + 1) * 128]
        nl = len(kts)
        for i, kt in enumerate(kts[:4]):
            P.mm(pa[:, i, :], k[p0:p0 + DH, kt * 128:(kt + 1) * 128], qs, True, True, (k, q), (pa,))
        rest = kts[4:] + ctx
        for i, kt in enumerate(rest):
            P.mm(pb[:, i, :], k[p0:p0 + DH, kt * 128:(kt + 1) * 128], qs, True, True, (k, q), (pb,))
        n4 = min(nl, 4)
        if nl:
            P.stt(s[:, 0:n4, :], pa[:, 0:n4, :], SC, mask[:, m0:m0 + n4, :], ALU.mult, ALU.add, (pa, mask), (s,))
            if nl > 4:
                P.stt(s[:, 4:5, :], pb[:, 0:1, :], SC, mask[:, m0 + 4:m0 + 5, :], ALU.mult, ALU.add, (pb, mask), (s,))
            P.act(p_[:, 0:nl, :], s[:, 0:nl, :], AF.Exp, (s,), (p_,))
        nr = len(rest) - (nl - n4)
        P.act(p_[:, nl:nl + nr, :], pb[:, nl - n4:nl - n4 + nr, :], AF.Exp, (pb,), (p_,), scale=SC)
        return (h, gq, kts + ctx, u)

    def fin(h, gq, allk, u):
        v = vb[h % 2]
        p_, r, o, po = pt[u % 3], rc[u % 2], ob[u % 2], pso[u % 2]
        for i, kt in enumerate(allk):
            P.mm(po[:, 0:DH + 1], p_[:, i, :], v[:, kt, :], i == 0, i == len(allk) - 1, (p_, v), (po,))
        P.op("dve", lambda e, r=r, po=po: e.reciprocal(r[:], po[:, DH:DH + 1]), (po,), (r,))
        P.ts(o[:], po[:, 0:DH], r[:, 0:1], None, ALU.mult, None, (po, r), (o,))
        P.dma("pool", O[gq * 128:(gq + 1) * 128, h * DH:(h + 1) * DH], o[:], (o,), ())

    infos = [None] * len(units)
    for u0 in range(min(2, len(units))):
        infos[u0] = prep(units[u0][0], units[u0][1], u0)
    for u, (h, fq) in enumerate(units):
        if u + 2 < len(units):
            infos[u + 2] = prep(units[u + 2][0], units[u + 2][1], u + 2)
        fin(*infos[u])
        if (u + 1 == len(units) or units[u + 1][0] != h) and h + 2 < NHEADS:
            load(h + 2)
    P.end_stage()


def na_mask_tables(rpb):
    krl = (np.arange(128) // 64)[:, None, None]
    kc = (np.arange(128) % 64)[:, None, None]
    qrl = (np.arange(128) // 64)[None, None, :]
    qc = (np.arange(128) % 64)[None, None, :]
    ws = np.clip(qc - 8, 0, 48)
    col_ok = (kc >= ws) & (kc < ws + 16)
    ci = np.clip(kc - qc + 15, 0, 30)

    def tab(ds, row_window):
        d = np.asarray(ds)[None, :, None]
        rel = 2 * d + krl - qrl
        ok = col_ok & np.ones_like(rel, bool)
        if row_window:
            ok = ok & (rel >= -4) & (rel <= 3)
        ri = np.clip(rel + 7, 0, 14)
        cib = np.broadcast_to(ci, rel.shape)
        g = rpb[:, ri, cib]
        mk = np.where(ok, 0.0, -1e30).astype(np.float32)
        return np.ascontiguousarray(g.astype(np.float32)), np.ascontiguousarray(mk)

    MI, MKI = tab([-2, -1, 0, 1, 2], True)
    MB, MKB = tab([-3, -2, -1, 0, 1, 2, 3], False)
    return MI, MB, MKI, MKB


def stage_gla(P, kind, Y, U, OF, PAR, NGsrc, IDN, TRI):
    ret = kind == "ret"
    T = TG_
    NI = 2 if ret else 4
    DKC = 2 if ret else 1
    DK = 128 * DKC
    DV = 512 if ret else 128
    CQ = 1.0 if ret else 128 ** -0.5
    CK = 1.0 / 16 if ret else 1.0
    C = 128
    NCH = T // C
    scr = [[Buf("scr", None) for _ in range(NCH)] for _ in range(NI)]

    idn = P.sb("idn", [128, 128], BF16)
    tri = [P.sb("tri", [128, 128], F32) for _ in range(2)]
    P.dma("sp", idn[:], IDN, (), (idn,))
    for d in range(2):
        P.dma("sp", tri[d][:], TRI[d], (), (tri[d],))
    ones = P.sb("ones", [128, C], F32)
    P.memset(ones[:], 1.0, (ones,))
    epsb = P.sb("epsb", [128, 1], F32)
    P.memset(epsb[:], RMS_EPS, (epsb,))
    oneb = P.sb("oneb", [128, 1], F32)
    P.memset(oneb[:], 1.0, (oneb,))
    if ret:
        dec = P.sb("dec", [128, 4], F32)
        P.dma("sp", dec[:], PAR, (), (dec,))
        P.act(dec[:], dec[:], AF.Exp, (dec,), (dec,))
        P.ts(dec[:], dec[:], -1.0, None, ALU.mult, None, (dec,), (dec,))
    else:
        lbr = P.sb("lbr", [128, NI, 4], F32)
        lb = P.sb("lb", [128, NI, 4], F32)
        ng = P.sb("ng", [128, DV], F32)
        P.dma("sp", lbr[:], PAR, (), (lbr,))
        P.dma("sp", ng[:], NGsrc, (), (ng,))
        P.act(lbr[:], lbr[:], AF.Exp, (lbr,), (lbr,))
        for it in range(NI):
            P.op("dve", lambda e, it=it: e.reduce_sum(lb[:, it, 2:3], lbr[:, it, :], axis=AX.X), (lbr,), (lb,))
            P.op("dve", lambda e, it=it: e.reciprocal(lb[:, it, 3:4], lb[:, it, 2:3]), (lb,), (lb,))
            P.tt(lb[:, it, 1:2], lbr[:, it, 0:1], lb[:, it, 3:4], ALU.mult, (lbr, lb), (lb,))
            P.ts(lb[:, it, 0:1], lb[:, it, 1:2], -1.0, 1.0, ALU.mult, ALU.add, (lb,), (lb,))

    qg = [P.sb("qg", [128, DKC, 512], BF16) for _ in range(2)]
    kg = [P.sb("kg", [128, DKC, 512], BF16) for _ in range(2)]
    vg = [P.sb("vg", [128, 4, DV], BF16) for _ in range(2)]
    sgg = [P.sb("sgg", [128, 4, DV], BF16) for _ in range(2)]
    ofg = [P.sb("ofg", [128, 4, DV], F32) for _ in range(2)]
    S = [P.sb("S", [128, DV], F32) for _ in range(DKC)]
    Sb = [[P.sb("Sb", [128, DV], BF16) for _ in range(DKC)] for _ in range(3)]
    lgT = P.sb("lgT", [128, C], F32)
    csT = P.sb("csT", [128, C], F32)
    bT = P.sb("bT", [128, C], F32)
    NDT = 1 if ret else 2
    eb = [P.sb("eb", [128, C], F32) for _ in range(NDT)]
    enb = [P.sb("enb", [128, C], F32) for _ in range(NDT)]
    ekk = [P.sb("ekk", [128, C], F32) for _ in range(NDT)]
    ebl = [P.sb("ebl", [128, 1], F32) for _ in range(NDT)]
    GW = 512
    qdG = [P.sb("qdG", [128, DKC, GW], BF16) for _ in range(2)]
    kdG = [P.sb("kdG", [128, DKC, GW], BF16) for _ in range(2)]
    kkTG = [P.sb("kkTG", [128, DKC, GW], BF16) for _ in range(2)]
    if ret:
        eb4 = P.sb("eb4", [128, GW], F32)
        enb4 = P.sb("enb4", [128, GW], F32)
        ekk4 = P.sb("ekk4", [128, GW], F32)
    else:
        tG = [P.sb("tG", [128, GW], F32) for _ in range(2)]
        kfG = [P.sb("kfG", [128, GW], F32) for _ in range(2)]
        lgG = [P.sb("lgG", [128, GW], F32) for _ in range(2)]
        csG = [P.sb("csG", [128, GW], F32) for _ in range(2)]
        bG = [P.sb("bG", [128, GW], F32) for _ in range(2)]
        dG = [P.sb("dG", [128, GW], F32) for _ in range(2)]
        ebG = [P.sb("ebG", [128, GW], F32) for _ in range(2)]
        enbG = [P.sb("enbG", [128, GW], F32) for _ in range(2)]
        ekkG = [P.sb("ekkG", [128, GW], F32) for _ in range(2)]
        eblG = [P.sb("eblG", [128, 4], F32) for _ in range(2)]
        maskG = P.sb("maskG", [128, GW], F32)
        P.memset(maskG[:], 1.0, (maskG,))
        for c4 in range(4):
            P.memset(maskG[:, c4 * 128:c4 * 128 + 1], 0.0, (maskG,))
    kk = [P.sb("kk", [128, DK], BF16) for _ in range(3)]
    att = [P.sb("att", [128, C], BF16) for _ in range(3)]
    osb = [P.sb("osb", [128, DV], F32) for _ in range(2)]
    junk = P.sb("junk", [128, DV], F32)
    st = [P.sb("st", [128, 2], F32) for _ in range(2)]
    ub = [P.sb("ub", [128, DV], BF16) for _ in range(2)]
    ps_t = P.ps("ps_t", [128, C], BF16)
    ps_a = [P.ps("ps_a", [128, 512]) for _ in range(2)]
    ps_o = [P.ps("ps_o", [128, 512]) for _ in range(2)]
    ps_d = [P.ps("ps_d", [128, 512]) for _ in range(DKC)]

    def decay_tiles(lg_ap, lg_reads, dirn, i):
        P.op("dve", lambda e: e.tensor_tensor_scan(csT[:], ones[:], lg_ap, 0.0, ALU.mult, ALU.add), (ones,) + lg_reads, (csT,))
        tot = csT[:, C - 1:C]
        if dirn == 0:
            b, breads = csT, (csT,)
        else:
            P.ts(bT[:], csT[:], tot, None, ALU.subtract, None, (csT,), (bT,))
            P.stt(bT[:], bT[:], -1.0, lg_ap, ALU.mult, ALU.add, (bT,) + lg_reads, (bT,))
            b, breads = bT, (bT, csT)
        P.act(eb[i][:], b[:], AF.Exp, breads, (eb[i],))
        P.act(enb[i][:], b[:], AF.Exp, breads, (enb[i],), scale=-1.0)
        P.act(ekk[i][:], b[:], AF.Exp, breads, (ekk[i],), scale=-1.0, bias=tot)
        P.act(ebl[i][:], tot, AF.Exp, (csT,), (ebl[i],))

    groups = [[0, 1]] + [list(range(2 + 4 * g, 6 + 4 * g)) for g in range(16)]

    def runs(chs):
        out = []
        for c in chs:
            if out and grow(c) == out[-1][0] + out[-1][1] * 128:
                out[-1][1] += 1
            else:
                out.append([grow(c), 1])
        return out

    cc = 0
    for it in range(NI):
        if ret:
            qcol, kcols, vcol, sgcol = it * 256, [512 + it * 256] * 2, 1024 + it * 512, 2048 + it * 512
        else:
            qcol, kcols, vcol, sgcol = it * 128, [512 + it * 128, 1024 + it * 128], 1536 + it * 128, 2048 + it * 128
        for dirn in range(2):
            for c in range(DKC):
                P.memset(S[c][:], 0.0, (S[c],))
                P.memset(Sb[2][c][:], 0.0, (Sb[2][c],), eng="pool")
            if ret:
                P.copy(lgT[:], dec[:, dirn * 2 + it:dirn * 2 + it + 1].broadcast_to([128, C]), (dec,), (lgT,))
                decay_tiles(lgT[:], (lgT,), dirn, 0)
                for src, dst in ((eb[0], eb4), (enb[0], enb4), (ekk[0], ekk4)):
                    P.copy(dst[:, :].rearrange("p (c t) -> p c t", t=C), src[:].unsqueeze(1).broadcast_to([128, 4, C]), (src,), (dst,))
            if dirn == 0:
                gorder = list(range(len(groups)))
            else:
                gorder = [0] + list(range(len(groups) - 1, 0, -1))
            kcol = kcols[dirn]

            def load(gi, n):
                j0 = 0
                for (r0, nch) in runs(groups[gi]):
                    gs = nch * 128
                    for c in range(DKC):
                        P.dma_t("sp", qg[n % 2][:, c, j0 * 128:j0 * 128 + gs], Y[r0:r0 + gs, qcol + c * 128:qcol + (c + 1) * 128], (), (qg[n % 2],))
                        P.dma_t("sp", kg[n % 2][:, c, j0 * 128:j0 * 128 + gs], Y[r0:r0 + gs, kcol + c * 128:kcol + (c + 1) * 128], (), (kg[n % 2],))
                    P.dma("sp", vg[n % 2][:, j0:j0 + nch, :], Y[r0:r0 + gs, vcol:vcol + DV].rearrange("(j p) d -> p j d", p=128), (), (vg[n % 2],))
                    if dirn == 1:
                        P.dma("sp", sgg[n % 2][:, j0:j0 + nch, :], Y[r0:r0 + gs, sgcol:sgcol + DV].rearrange("(j p) d -> p j d", p=128), (), (sgg[n % 2],))
                    j0 += nch
                if dirn == 1:
                    for j, ch in enumerate(groups[gi]):
                        P.dma("sp", ofg[n % 2][:, j, :], OF[grow(ch):grow(ch) + 128, it * DV:(it + 1) * DV], (scr[it][ch],), (ofg[n % 2],))

            seq = []
            for n, gi in enumerate(gorder):
                nj = len(groups[gi])
                for j in (range(nj) if dirn == 0 else range(nj - 1, -1, -1)):
                    seq.append((n, gi, j, groups[gi][j], len(seq)))

            def gprep(n, gi):
                nj = len(groups[gi])
                W = nj * C
                q_, k_ = qg[n % 2], kg[n % 2]
                g2 = n % 2
                if ret:
                    e3 = lambda t: bcast_mid(t[:, 0:W], DKC)
                    P.stt(qdG[g2][:, :, 0:W], q_[:, :, 0:W], CQ, e3(eb4), ALU.mult, ALU.mult, (q_, eb4), (qdG[g2],))
                    P.stt(kdG[g2][:, :, 0:W], k_[:, :, 0:W], CK, e3(enb4), ALU.mult, ALU.mult, (k_, enb4), (kdG[g2],))
                    P.stt(kkTG[g2][:, :, 0:W], k_[:, :, 0:W], CK, e3(ekk4), ALU.mult, ALU.mult, (k_, ekk4), (kkTG[g2],))
                    return
                t_, kf, lg, cs, b_, d_ = tG[g2], kfG[g2], lgG[g2], csG[g2], bG[g2], dG[g2]
                e_b, e_nb, e_kk, e_bl = ebG[g2], enbG[g2], ekkG[g2], eblG[g2]
                P.act(t_[:, 0:W], k_[:, 0, 0:W], AF.Exp, (k_,), (t_,), scale=-1.0)
                P.act(t_[:, 0:W], t_[:, 0:W], AF.Ln, (t_, oneb), (t_,), bias=oneb[:, 0:1], scale=1.0)
                P.act(t_[:, 0:W], t_[:, 0:W], AF.Exp, (t_,), (t_,), scale=-1.0)
                P.ts(t_[:, 0:W], t_[:, 0:W], lb[:, it, 1:2], lb[:, it, 0:1], ALU.mult, ALU.add, (t_, lb), (t_,))
                P.ts(kf[:, 0:W], t_[:, 0:W], -1.0, 1.0, ALU.mult, ALU.add, (t_,), (kf,))
                P.act(lg[:, 0:W], t_[:, 0:W], AF.Ln, (t_,), (lg,))
                P.op("dve", lambda e: e.tensor_tensor_scan(cs[:, 0:W], maskG[:, 0:W], lg[:, 0:W], 0.0, ALU.mult, ALU.add), (maskG, lg), (cs,))
                v3 = lambda t: t[:, 0:W].rearrange("p (c t) -> p c t", t=C)
                tot = v3(cs)[:, :, C - 1:C]
                totb = tot.broadcast_to([128, nj, C])
                if dirn == 0:
                    bsrc, breads = cs, (cs,)
                else:
                    P.tt(v3(b_), v3(cs), totb, ALU.subtract, (cs,), (b_,))
                    P.stt(b_[:, 0:W], b_[:, 0:W], -1.0, lg[:, 0:W], ALU.mult, ALU.add, (b_, lg), (b_,))
                    bsrc, breads = b_, (b_, cs)
                P.act(e_b[:, 0:W], bsrc[:, 0:W], AF.Exp, breads, (e_b,))
                P.act(e_nb[:, 0:W], bsrc[:, 0:W], AF.Exp, breads, (e_nb,), scale=-1.0)
                P.tt(v3(d_), totb, v3(bsrc), ALU.subtract, breads + (cs,), (d_,))
                P.act(e_kk[:, 0:W], d_[:, 0:W], AF.Exp, (d_,), (e_kk,))
                P.act(e_bl[:, 0:nj].unsqueeze(2), tot, AF.Exp, (cs,), (e_bl,))
                P.stt(qdG[g2][:, 0, 0:W], q_[:, 0, 0:W], CQ, e_b[:, 0:W], ALU.mult, ALU.mult, (q_, e_b), (qdG[g2],))
                P.stt(kdG[g2][:, 0, 0:W], kf[:, 0:W], CK, e_nb[:, 0:W], ALU.mult, ALU.mult, (kf, e_nb), (kdG[g2],))
                P.stt(kkTG[g2][:, 0, 0:W], kf[:, 0:W], CK, e_kk[:, 0:W], ALU.mult, ALU.mult, (kf, e_kk), (kkTG[g2],))

            def prep(n, gi, j, ch, i):
                nonlocal cc
                if i == 0 or seq[i - 1][0] != n:
                    gprep(n, gi)
                v_ = vg[n % 2]
                x = cc % 3
                cc += 1
                g2 = n % 2
                sl = slice(j * 128, (j + 1) * 128)
                qd_, kd_, kkT_ = qdG[g2], kdG[g2], kkTG[g2]
                kk_, att_ = kk[x], att[x]
                ebl_ap = ebl[0][:, 0:1] if ret else eblG[g2][:, j:j + 1]
                ebl_buf = ebl[0] if ret else eblG[g2]
                pa = ps_a[x % 2]
                for c in range(DKC):
                    P.mm(pa[:, 0:C], kd_[:, c, sl], qd_[:, c, sl], c == 0, c == DKC - 1, (kd_, qd_), (pa,))
                P.tt(att_[:], pa[:, 0:C], tri[dirn][:], ALU.mult, (pa, tri[dirn]), (att_,))
                for c in range(DKC):
                    P.transpose(ps_t[:], kkT_[:, c, sl], idn[:], (kkT_, idn), (ps_t,))
                    P.act(kk_[:, c * 128:(c + 1) * 128], ps_t[:], AF.Copy, (ps_t,), (kk_,))
                return (n, j, ch, x, i, ebl_ap, ebl_buf)

            def prepB(n, j, ch, x, i, ebl_ap, ebl_buf):
                v_ = vg[n % 2]
                kk_ = kk[x]
                for c in range(DKC):
                    P.mm(ps_d[c][:, 0:DV], kk_[:, c * 128:(c + 1) * 128], v_[:, j, :], True, True, (kk_, v_), (ps_d[c],))
                    P.stt(S[c][:], S[c][:], ebl_ap, ps_d[c][:, 0:DV], ALU.mult, ALU.add, (S[c], ebl_buf, ps_d[c]), (S[c],))
                    P.act(Sb[i % 3][c][:], S[c][:], AF.Copy, (S[c],), (Sb[i % 3][c],))
                return (n, j, ch, x, i)

            def state(n, j, ch, x, i):
                v_, sg_, of_ = vg[n % 2], sgg[n % 2], ofg[n % 2]
                qd_, att_ = qdG[n % 2], att[x]
                sl = slice(j * 128, (j + 1) * 128)
                Sprev = Sb[(i - 1) % 3]
                x = i % 2
                po = ps_o[x]
                P.mm(po[:, 0:DV], att_[:], v_[:, j, :], True, False, (att_, v_), (po,))
                for c in range(DKC):
                    P.mm(po[:, 0:DV], qd_[:, c, sl], Sprev[c][:], False, c == DKC - 1, (qd_, Sprev[c]), (po,))
                o_ = osb[x]
                if dirn == 0:
                    P.act(o_[:], po[:, 0:DV], AF.Copy, (po,), (o_,))
                    P.dma("pool", OF[grow(ch):grow(ch) + 128, it * DV:(it + 1) * DV], o_[:], (o_,), (scr[it][ch],))
                else:
                    s_, u_ = st[x], ub[x]
                    P.tt(o_[:], po[:, 0:DV], of_[:, j, :], ALU.add, (po, of_), (o_,))
                    P.act(junk[:], o_[:], AF.Square, (o_,), (junk, s_), accum_out=s_[:, 0:1])
                    P.act(s_[:, 1:2], s_[:, 0:1], AF.Ln, (s_, epsb), (s_,), bias=epsb[:, 0:1], scale=1.0 / DV)
                    P.act(s_[:, 1:2], s_[:, 1:2], AF.Exp, (s_,), (s_,), scale=-0.5)
                    if ret:
                        P.stt(u_[:], o_[:], s_[:, 1:2], sg_[:, j, :], ALU.mult, ALU.mult, (o_, s_, sg_), (u_,))
                    else:
                        P.stt(o_[:], o_[:], s_[:, 1:2], ng[:], ALU.mult, ALU.mult, (o_, s_, ng), (o_,))
                        P.tt(u_[:], o_[:], sg_[:, j, :], ALU.mult, (o_, sg_), (u_,), eng="pool")
                    P.dma("pool", U[grow(ch):grow(ch) + 128, it * DV:(it + 1) * DV], u_[:], (u_,), ())

            load(gorder[0], 0)
            if len(gorder) > 1:
                load(gorder[1], 1)
            infos = [None] * len(seq)
            infos[0] = prep(*seq[0])
            if len(seq) > 1:
                infos[1] = prep(*seq[1])
            prepB(*infos[0])
            for i in range(len(seq)):
                if i + 2 < len(seq):
                    infos[i + 2] = prep(*seq[i + 2])
                if i + 1 < len(seq):
                    prepB(*infos[i + 1])
                state(*infos[i][0:5])
                n_cur = seq[i][0]
                if (i + 1 == len(seq) or seq[i + 1][0] != n_cur) and n_cur + 2 < len(gorder):
                    load(gorder[n_cur + 2], n_cur + 2)
    P.end_stage()


PAIRS = [[0, 1], [2, 3], [4, 5], [6, 7]]


def stage_gather(P, SRC, GC, nrows, rc):
    for j in range(nrows // rc):
        P.collective("AllGather", PAIRS, SRC[j * rc:(j + 1) * rc, :], GC[j * 2 * rc:(j + 1) * 2 * rc, :])
    P.end_stage()


def gc_row(rc, r, t):
    return (t // rc) * 2 * rc + r * rc + (t % rc)


def stage_blend(P, OUT, nrows, segs, SEL):
    sel = P.sb("sel", [128, 2], F32)
    P.dma("sp", sel[:], SEL, (), (sel,))
    NB, LA = 4, 3
    ab = [[P.sb("ba", [128, w], BF16) for _ in range(NB)] for (_, w, _, _) in segs]
    bb = [[P.sb("bb", [128, w], BF16) for _ in range(NB)] for (_, w, _, _) in segs]
    ob = [[P.sb("bo", [128, w], BF16) for _ in range(NB)] for (_, w, _, _) in segs]
    jobs = [(t, si) for t in range(nrows // 128) for si in range(len(segs))]

    def load(n):
        t, si = jobs[n]
        c0, w, fa, fb = segs[si]
        a, b = ab[si][t % NB], bb[si][t % NB]
        P.dma("sp", a[:], fa(t * 128), (), (a,))
        P.dma("act", b[:], fb(t * 128), (), (b,))

    for n in range(min(LA, len(jobs))):
        load(n)
    for n, (t, si) in enumerate(jobs):
        if n + LA < len(jobs):
            load(n + LA)
        c0, w, fa, fb = segs[si]
        a, b, o = ab[si][t % NB], bb[si][t % NB], ob[si][t % NB]
        P.ts(a[:], a[:], sel[:, 0:1], None, ALU.mult, None, (a, sel), (a,))
        P.stt(o[:], b[:], sel[:, 1:2], a[:], ALU.mult, ALU.add, (b, sel, a), (o,))
        P.dma("pool", OUT[t * 128:(t + 1) * 128, c0:c0 + w], o[:], (o,), ())
    P.end_stage()


def head_select(P, GC, rc, M, segcols, SEL):
    segs = []
    c = 0
    for (g0, wt) in segcols:
        w = wt // 2

        def fa(r0, g0=g0, w=w, off=0):
            g = gc_row(rc, r0 // T_, r0 % T_)
            return GC[g:g + 128, g0 + off:g0 + off + w]

        segs.append((c, w, fa, (lambda r0, fa=fa, w=w: fa(r0, off=w))))
        c += w
    stage_blend(P, M, TG_, segs, SEL)


def token_select(P, GC, rc, L, w, SEL):
    segs = []
    for r in range(2):
        segs.append((r * w, w, (lambda r0, r=r: GC[gc_row(rc, r, r0):gc_row(rc, r, r0) + 128, :]),
                     (lambda r0, r=r: GC[gc_row(rc, r, T_ + r0):gc_row(rc, r, T_ + r0) + 128, :])))
    stage_blend(P, L, T_, segs, SEL)


def stage_reorder(P, GC, rc, G, nrows):
    for j in range(nrows // rc):
        for r in range(2):
            P.dma("sp", G[r * nrows + j * rc:r * nrows + (j + 1) * rc, :], GC[gc_row(rc, r, j * rc):gc_row(rc, r, j * rc) + rc, :], (), ())
    P.end_stage()


def build_fused():
    P = Prog()
    T, TG = T_, TG_
    f = lambda name, shape, dt=F32: P.din(name, shape, dt)
    scr = P.dscr

    XIN = f("XIN", [T, 1024])
    condT = f("condT", [128, 8, 2])
    SEL = f("SEL", [128, 2])
    ADA_W, ADA_B = f("ADA_W", [4, 1024, 6144]), f("ADA_B", [4, 6144])
    LN_G, LN_B = f("LN_G", [4, 2, 1024]), f("LN_B", [4, 2, 1024])
    W13, W2 = f("W13", [4, 1024, 5632]), f("W2", [4, 2816, 1024])
    RET_IN, RET_OUT, DEC = f("RET_IN", [1024, 6144]), f("RET_OUT", [2048, 1024]), f("DEC", [128, 4])
    NA_QKV, NA_OUT = f("NA_QKV", [1024, 3072]), f("NA_OUT", [1024, 1024])
    MI, MB, MKI, MKB = f("MI", [8, 128, 5, 128]), f("MB", [8, 128, 7, 128]), f("MKI", [128, 5, 128]), f("MKB", [128, 7, 128])
    MLA_DOWN, MLA_UQ, MLA_UKV, MLA_OUT = f("MLA_DOWN", [1024, 800]), f("MLA_UQ", [512, 1536]), f("MLA_UKV", [256, 2048]), f("MLA_OUT", [1024, 1024])
    GN = f("GN", [128, 768])
    HG_IN, HG_OUT, LBR, NG = f("HG_IN", [1024, 5120]), f("HG_OUT", [1024, 1024]), f("LBR", [128, 4, 4]), f("NG", [128, 128])
    COS256, SIN256 = f("COS256", [T, 128]), f("SIN256", [T, 128])
    COS32, SIN32 = f("COS32", [T, 16]), f("SIN32", [T, 16])
    COS32S, SIN32S = f("COS32S", [T, 16]), f("SIN32S", [T, 16])
    IDN, TRI, IDF = f("IDN", [128, 128], BF16), f("TRI", [2, 128, 128]), f("IDF", [128, 128])
    OUT = P.dout("OUT", [T, 1024], F32)

    MOD = scr("MOD", [4, 2, 6144], F32)
    A0 = scr("A0", [T, 1024], BF16)
    A2 = scr("A2", [T, 1024], BF16)
    FF = scr("FF", [2816, T], BF16)
    HA = scr("HA", [T, 1024], F32)
    HB = scr("HB", [T, 1024], F32)

    def m(i, r, s):
        return MOD[i, r, s * 1024:(s + 1) * 1024]

    stage_k0(P, condT, ADA_W, ADA_B, MOD)
    stage_mod0(P, XIN, [m(0, 0, 1), m(0, 0, 0), m(0, 1, 1), m(0, 1, 0)], A0, T)

    def post(i, Usrc, K, WOUT, Hin, last):
        v1 = [m(i, 0, 2), m(i, 1, 2), LN_G[i, 0], LN_B[i, 0], m(i, 0, 4), m(i, 0, 3), m(i, 1, 4), m(i, 1, 3)]
        stage_resid_ln(P, Usrc, WOUT, Hin, HB, A2, v1, T, K)
        stage_swiglu(P, A2, W13[i], FF, T)
        if last:
            v2 = [m(i, 0, 5), m(i, 1, 5), LN_G[i, 1], LN_B[i, 1]]
            stage_resid_ln(P, FF, W2[i], HB, OUT, None, v2, T, 2816, x_fm=True)
        else:
            v2 = [m(i, 0, 5), m(i, 1, 5), LN_G[i, 1], LN_B[i, 1], m(i + 1, 0, 1), m(i + 1, 0, 0), m(i + 1, 1, 1), m(i + 1, 1, 0)]
            stage_resid_ln(P, FF, W2[i], HB, HA, A0, v2, T, 2816, x_fm=True)

    Y0L, Y0G, Y0M = scr("Y0L", [T, 6144], BF16), scr("Y0G", [TG, 6144], BF16), scr("Y0M", [TG, 3072], BF16)
    U0M, OF0, U0G, U0L = scr("U0M", [TG, 1024], BF16), scr("OF0", [TG, 1024], F32), scr("U0G", [2 * TG, 1024], BF16), scr("U0L", [T, 2048], BF16)
    stage_ret_a(P, A0, RET_IN, Y0L, COS256, SIN256, T)
    stage_gather(P, Y0L, Y0G, T, 128)
    head_select(P, Y0G, 128, Y0M, [(0, 1024), (1024, 1024), (2048, 2048), (4096, 2048)], SEL)
    stage_gla(P, "ret", Y0M, U0M, OF0, DEC, None, IDN, TRI)
    stage_gather(P, U0M, U0G, TG, 768)
    token_select(P, U0G, 768, U0L, 1024, SEL)
    post(0, U0L, 2048, RET_OUT, XIN, False)
    Y1L, Y1G, Y1M = scr("Y1L", [T, 3072], BF16), scr("Y1G", [TG, 3072], BF16), scr("Y1M", [TG, 1536], BF16)
    O1M, O1G, O1L = scr("O1M", [TG, 512], BF16), scr("O1G", [2 * TG, 512], BF16), scr("O1L", [T, 1024], BF16)
    stage_plain(P, A0, NA_QKV, Y1L, T, 1024, 3072)
    stage_gather(P, Y1L, Y1G, T, 128)
    head_select(P, Y1G, 128, Y1M, [(0, 1024), (1024, 1024), (2048, 1024)], SEL)
    stage_na_b(P, Y1M, MI, MB, MKI, MKB, O1M)
    stage_gather(P, O1M, O1G, TG, 1408)
    token_select(P, O1G, 1408, O1L, 512, SEL)
    post(1, O1L, 1024, NA_OUT, HA, False)
    Y2A, YKRL, YKRG = scr("Y2A", [T, 800], BF16), scr("YKRL", [T, 128], BF16), scr("YKRG", [TG, 128], BF16)
    Y2Q, Y2KVL, Y2KVG, O2L = scr("Y2Q", [T, 2048], BF16), scr("Y2KVL", [T, 2048], BF16), scr("Y2KVG", [TG, 2048], BF16), scr("O2L", [T, 1024], BF16)
    stage_mla_a1(P, A0, MLA_DOWN, Y2A, YKRL, COS32, SIN32, GN, T)
    stage_mla_a2q(P, Y2A[:, 0:512], MLA_UQ, Y2Q, COS32S, SIN32S, T)
    stage_plain(P, Y2A[:, 512:768], MLA_UKV, Y2KVL, T, 256, 2048)
    Y2KVC = scr("Y2KVC", [TG, 2048], BF16)
    stage_gather(P, Y2KVL, Y2KVC, T, 384)
    stage_reorder(P, Y2KVC, 384, Y2KVG, T)
    stage_gather(P, YKRL, YKRG, T, T)
    stage_mla_b(P, Y2Q, Y2KVG, YKRG, IDF, O2L)
    post(2, O2L, 1024, MLA_OUT, HA, False)
    Y3L, Y3G, Y3M = scr("Y3L", [T, 5120], BF16), scr("Y3G", [TG, 5120], BF16), scr("Y3M", [TG, 2560], BF16)
    U3M, OF3, U3G, U3L = scr("U3M", [TG, 512], BF16), scr("OF3", [TG, 512], F32), scr("U3G", [2 * TG, 512], BF16), scr("U3L", [T, 1024], BF16)
    stage_plain(P, A0, HG_IN, Y3L, T, 1024, 5120, silu_blocks=(0, 1, 8, 9))
    stage_gather(P, Y3L, Y3G, T, 128)
    head_select(P, Y3G, 128, Y3M, [(j * 1024, 1024) for j in range(5)], SEL)
    stage_gla(P, "hg", Y3M, U3M, OF3, LBR, NG, IDN, TRI)
    stage_gather(P, U3M, U3G, TG, 1408)
    token_select(P, U3G, 1408, U3L, 512, SEL)
    post(3, U3L, 1024, HG_OUT, HA, True)
    return P


def _rep(v):
    v = np.asarray(v)
    return np.ascontiguousarray(np.broadcast_to(v[None], (128,) + v.shape))


def _rope_tables(rot_dim, half, L=8192):
    t = np.arange(L)
    rows = (t // 64).astype(np.float32)
    cols = (t % 64).astype(np.float32)
    nf = rot_dim // 4
    inv = (np.float32(10000.0) ** (-np.arange(nf, dtype=np.float32) / np.float32(nf))).astype(np.float32)
    ang = np.concatenate([rows[:, None] * inv, cols[:, None] * inv], -1).astype(np.float32)
    nc_ = rot_dim // 2
    sl = slice(half * (L // 2), (half + 1) * (L // 2))
    cos = np.concatenate([np.ones((128, nc_), np.float32), np.cos(ang).astype(np.float32)[sl]], 0)
    sin = np.concatenate([np.zeros((128, nc_), np.float32), np.sin(ang).astype(np.float32)[sl]], 0)
    return np.ascontiguousarray(cos), np.ascontiguousarray(sin)


def kernel(x, c, ctx, c_ctx, ada_w, ada_b, ln_g, ln_b, ffn_w13, ffn_w2,
           ret_w_in, ret_decay, ret_w_out, na_w_qkv, na_rpb, na_w_out,
           mla_w_down, mla_q_norm, mla_kv_norm, mla_w_uq, mla_w_ukv, mla_w_out,
           hg_w_in, hg_lower_bounds, hg_norm_g, hg_w_out):
    f32 = np.float32
    A = lambda a: np.ascontiguousarray(np.asarray(a, dtype=f32))
    x, c, ctx, c_ctx = A(x), A(c), A(ctx), A(c_ctx)
    import ml_dtypes
    bf = ml_dtypes.bfloat16
    MI, MB, MKI, MKB = na_mask_tables(A(na_rpb))
    SC = np.float32(96 ** -0.5)
    lbw = A(hg_lower_bounds).reshape(4, 8, 128)
    dec = A(ret_decay)
    shared = {
        "ADA_W": A(ada_w), "ADA_B": A(ada_b), "LN_G": A(ln_g), "LN_B": A(ln_b), "W13": A(ffn_w13), "W2": A(ffn_w2),
        "RET_IN": A(ret_w_in), "RET_OUT": A(ret_w_out),
        "NA_QKV": A(na_w_qkv), "NA_OUT": A(na_w_out), "MKI": MKI, "MKB": MKB,
        "MLA_DOWN": A(mla_w_down), "MLA_UQ": A(mla_w_uq), "MLA_UKV": A(mla_w_ukv), "MLA_OUT": A(mla_w_out),
        "GN": _rep(np.concatenate([A(mla_q_norm), A(mla_kv_norm)])),
        "HG_IN": A(hg_w_in), "HG_OUT": A(hg_w_out), "NG": _rep(A(hg_norm_g)),
        "IDN": np.eye(128, dtype=f32).astype(bf), "IDF": np.eye(128, dtype=f32), "TRI": np.stack([np.triu(np.ones((128, 128), f32)), np.tril(np.ones((128, 128), f32))]),
    }
    halfd = []
    for p in range(2):
        c256, s256 = _rope_tables(256, p)
        c32, s32 = _rope_tables(32, p)
        halfd.append({
            "COS256": c256, "SIN256": s256, "COS32": c32, "SIN32": s32, "COS32S": c32 * SC, "SIN32S": s32 * SC,
            "SEL": _rep(np.array([1.0 - p, float(p)], f32)),
            "MI": np.ascontiguousarray(MI[p * 8:(p + 1) * 8]), "MB": np.ascontiguousarray(MB[p * 8:(p + 1) * 8]),
            "DEC": _rep(np.array([dec[d, 2 * p + i] for d in range(2) for i in range(2)], f32)),
            "LBR": np.ascontiguousarray(lbw[:, 4 * p:4 * p + 4, :].transpose(2, 1, 0)),
        })
    in_maps = []
    for k in range(NCORES):
        b, p = k // 2, k % 2
        cond = np.stack([c[b], c_ctx], 0)
        d = dict(shared)
        d.update(halfd[p])
        d["XIN"] = np.ascontiguousarray(np.concatenate([ctx[b, p * 128:(p + 1) * 128], x[b, p * 4096:(p + 1) * 4096]], 0))
        d["condT"] = np.ascontiguousarray(cond.T.reshape(8, 128, 2).transpose(1, 0, 2))
        in_maps.append(d)
    res = run(build_fused(), in_maps)
    return np.ascontiguousarray(np.stack([np.concatenate([res[2 * b]["OUT"][128:], res[2 * b + 1]["OUT"][128:]], 0)
                                          for b in range(4)]).astype(np.float32))
```
